# Optimizing a Trainium2 kernel written in Bass

```python
import jax
import jax.numpy as jnp
from jax import lax
import numpy as np

D_MODEL = 2048
BATCH = 16
SEQ = 2048
DEPTH = 2
DEC_BATCH = 4
DEC_SEQ = 8192
PAST_LEN = 128

HEAD_DIM = 128
MIX_HEADS = D_MODEL // HEAD_DIM
MEM_HEADS = 4
TOK_HEADS = MIX_HEADS - MEM_HEADS
TOK_WIDTH = TOK_HEADS * HEAD_DIM
MEM_WIDTH = MEM_HEADS * HEAD_DIM
MIX_WIDTH = TOK_WIDTH + MEM_WIDTH
N_MEM = 256
D_FF = 4 * D_MODEL
N_MIXERS = 2
N_RET = (DEPTH + 1) // 2
N_NA = DEPTH // 2
RET_CHUNK = 128
RET_DECAY_BASE = 5.0
ROPE_BASE = 10000.0
GRID_W = 64
NA_KH_MAX = 8
NA_KW = 16
NA_COL_BLOCK = 16
NA_COL_SPAN = 32
N_COL_BLOCKS = GRID_W // NA_COL_BLOCK
NORM_EPS = 1e-6

kernel_name = 'hybrid_retention_natten_encoder'


def rms_norm(x, gain):
    xf = x.astype(jnp.float32)
    xf = xf * lax.rsqrt(jnp.mean(xf * xf, axis=-1, keepdims=True) + NORM_EPS)
    return (xf * gain.astype(jnp.float32)).astype(x.dtype)


def rotary(x):
    l, dh = x.shape[1], x.shape[-1]
    half = dh // 2
    inv_freq = ROPE_BASE ** (-jnp.arange(half, dtype=jnp.float32) / half)
    ang = jnp.arange(l, dtype=jnp.float32)[:, None] * inv_freq[None, :]
    cos = jnp.cos(ang)[None, :, None, :]
    sin = jnp.sin(ang)[None, :, None, :]
    xf = x.astype(jnp.float32)
    x1, x2 = xf[..., :half], xf[..., half:]
    return jnp.concatenate([x1 * cos - x2 * sin, x1 * sin + x2 * cos], axis=-1)


def retention_one_direction(q, k, v, log_gamma, strict):
    b, h, n, c, dh = q.shape
    idx = jnp.arange(c, dtype=jnp.float32)
    diff = idx[:, None] - idx[None, :]
    mask = (diff > 0) if strict else (diff >= 0)
    decay_intra = jnp.where(mask[None], jnp.exp(jnp.where(mask, diff, 0.0)[None] * log_gamma[:, None, None]), 0.0)
    scores = jnp.einsum('bhnid,bhnjd->bhnij', q, k) * decay_intra[None, :, None]
    y_intra = jnp.einsum('bhnij,bhnjd->bhnid', scores, v)
    k_decay = jnp.exp((c - 1 - idx)[None, :] * log_gamma[:, None])
    kv = jnp.einsum('bhnjd,bhnje->nbhde', k * k_decay[None, :, None, :, None], v)
    chunk_decay = jnp.exp(c * log_gamma)[None, :, None, None]

    def step(state, kv_n):
        return chunk_decay * state + kv_n, state

    _, states = lax.scan(step, jnp.zeros((b, h, dh, dh), jnp.float32), kv)
    q_decay = jnp.exp((idx + 1.0)[None, :] * log_gamma[:, None])
    y_cross = jnp.einsum('bhnid,nbhde->bhnie', q * q_decay[None, :, None, :, None], states)
    return y_intra + y_cross


def retention_mixer(q, k, v, gate, decay_exp):
    b, l, _ = q.shape
    n = l // RET_CHUNK
    log_gamma = jnp.log1p(-jnp.exp2(-decay_exp.astype(jnp.float32)))
    heads = lambda t: t.reshape(b, l, TOK_HEADS, HEAD_DIM)
    qr = rotary(heads(q))
    kr = rotary(heads(k)) * (HEAD_DIM ** -0.5)
    vf = heads(v).astype(jnp.float32)
    chunks = lambda t: t.reshape(b, n, RET_CHUNK, TOK_HEADS, HEAD_DIM).transpose(0, 3, 1, 2, 4)
    qc, kc, vc = chunks(qr), chunks(kr), chunks(vf)
    flip = lambda t: t[:, :, ::-1, ::-1]
    y = (retention_one_direction(qc, kc, vc, log_gamma[0], False)
         + flip(retention_one_direction(flip(qc), flip(kc), flip(vc), log_gamma[1], True)))
    y = y.transpose(0, 2, 3, 1, 4).reshape(b, l, TOK_HEADS, HEAD_DIM)
    y = y * lax.rsqrt(jnp.mean(y * y, axis=-1, keepdims=True) + NORM_EPS)
    return jax.nn.silu(gate) * y.reshape(b, l, TOK_WIDTH).astype(gate.dtype)


def na_column_tables():
    cb = np.arange(N_COL_BLOCKS)
    q_col = cb[:, None] * NA_COL_BLOCK + np.arange(NA_COL_BLOCK)[None, :]
    win_start = np.clip(q_col - NA_KW // 2, 0, GRID_W - NA_KW)
    blk_start = np.clip(cb * NA_COL_BLOCK - NA_KW // 2, 0, GRID_W - NA_COL_SPAN)
    key_col = blk_start[:, None] + np.arange(NA_COL_SPAN)[None, :]
    kcol = key_col[:, None, :]
    valid = (kcol >= win_start[:, :, None]) & (kcol < win_start[:, :, None] + NA_KW)
    dc_idx = np.clip(kcol - q_col[:, :, None] + NA_KW - 1, 0, 2 * NA_KW - 2)
    return key_col, valid, dc_idx


def neighbourhood_mixer(q, k, v, rpb):
    b, l, _ = q.shape
    rows = l // GRID_W
    kh = min(NA_KH_MAX, rows)
    key_col, valid, dc_idx = na_column_tables()
    qg = q.reshape(b, rows, N_COL_BLOCKS, NA_COL_BLOCK, TOK_HEADS, HEAD_DIM)
    kg = k.reshape(b, rows, GRID_W, TOK_HEADS, HEAD_DIM)[:, :, key_col]
    vg = v.reshape(b, rows, GRID_W, TOK_HEADS, HEAD_DIM)[:, :, key_col]
    rpb_f = rpb.astype(jnp.float32)
    scale = HEAD_DIM ** -0.5

    def row_block(r):
        rs = jnp.clip(r - kh // 2, 0, rows - kh)
        k_win = lax.dynamic_slice_in_dim(kg, rs, kh, axis=1)
        v_win = lax.dynamic_slice_in_dim(vg, rs, kh, axis=1)
        q_row = lax.dynamic_index_in_dim(qg, r, axis=1, keepdims=False)
        s = jnp.einsum('bcqhd,brckhd->bhcqrk', q_row, k_win).astype(jnp.float32) * scale
        dr_idx = rs + jnp.arange(kh) - r + (NA_KH_MAX - 1)
        bias = rpb_f[:, dr_idx][:, :, dc_idx].transpose(0, 2, 3, 1, 4)
        s = jnp.where(valid[:, :, None, :], s + bias[None], -jnp.inf)
        p = jax.nn.softmax(s, axis=(-2, -1)).astype(v_win.dtype)
        return jnp.einsum('bhcqrk,brckhd->bcqhd', p, v_win)

    out = lax.map(row_block, jnp.arange(rows))
    return out.transpose(1, 0, 2, 3, 4, 5).reshape(b, l, TOK_WIDTH)


def memory_attention(q, mem_k, mem_v):
    s = jnp.einsum('blhd,bmhd->bhlm', q, mem_k).astype(jnp.float32) * (HEAD_DIM ** -0.5)
    p = jax.nn.softmax(s, axis=-1).astype(mem_v.dtype)
    return jnp.einsum('bhlm,bmhd->blhd', p, mem_v)


def run_trunk(x, mem, norm_gain, mem_norm_gain, w_mem_kv, w_out, w_mlp_in, w_mlp_out,
              w_in_ret, ret_decay, w_in_na, na_rpb):
    b, l, _ = x.shape
    for i in range(DEPTH):
        g = norm_gain[i]
        h = rms_norm(x, g[0])
        m = rms_norm(mem, mem_norm_gain[i])
        mem_k, mem_v = jnp.split(m @ w_mem_kv[i], 2, axis=-1)
        mem_k = mem_k.reshape(b, N_MEM, MEM_HEADS, HEAD_DIM)
        mem_v = mem_v.reshape(b, N_MEM, MEM_HEADS, HEAD_DIM)
        j = i // N_MIXERS
        if i % N_MIXERS == 0:
            q, k, v, gate, q_mem = jnp.split(h @ w_in_ret[j], [TOK_WIDTH, 2 * TOK_WIDTH, 3 * TOK_WIDTH, 4 * TOK_WIDTH], axis=-1)
            tok = retention_mixer(q, k, v, gate, ret_decay[j])
        else:
            q, k, v, q_mem = jnp.split(h @ w_in_na[j], [TOK_WIDTH, 2 * TOK_WIDTH, 3 * TOK_WIDTH], axis=-1)
            tok = neighbourhood_mixer(q, k, v, na_rpb[j])
        mem_out = memory_attention(q_mem.reshape(b, l, MEM_HEADS, HEAD_DIM), mem_k, mem_v).reshape(b, l, MEM_WIDTH)
        mixed = jnp.concatenate([tok, mem_out], axis=-1) @ w_out[i]
        x = x + rms_norm(mixed, g[1])
        u = jnp.square(jax.nn.relu(rms_norm(x, g[2]) @ w_mlp_in[i]))
        x = x + rms_norm(u @ w_mlp_out[i], g[3])
    return x


def setup_inputs(seed: int = 0) -> dict:
    key = jax.random.key(seed)
    ks = jax.random.split(key, 14)
    normal = lambda k, shape: jax.random.normal(k, shape, jnp.float32)
    lin = lambda k, shape, fan_in: normal(k, shape) * (fan_in ** -0.5)
    decay_init = RET_DECAY_BASE + jnp.arange(TOK_HEADS, dtype=jnp.float32)[None, None, :]
    return {
        'x_prompt': normal(ks[0], (BATCH, SEQ, D_MODEL)),
        'x_sample': normal(ks[1], (DEC_BATCH, DEC_SEQ, D_MODEL)),
        'mem_prompt': normal(ks[2], (BATCH, N_MEM, D_MODEL)),
        'mem_sample': normal(ks[3], (DEC_BATCH, N_MEM, D_MODEL)),
        'norm_gain': 1.0 + 0.02 * normal(ks[4], (DEPTH, 4, D_MODEL)),
        'mem_norm_gain': 1.0 + 0.02 * normal(ks[5], (DEPTH, D_MODEL)),
        'w_mem_kv': lin(ks[6], (DEPTH, D_MODEL, 2 * MEM_WIDTH), D_MODEL),
        'w_out': lin(ks[7], (DEPTH, MIX_WIDTH, D_MODEL), MIX_WIDTH),
        'w_mlp_in': lin(ks[8], (DEPTH, D_MODEL, D_FF), D_MODEL),
        'w_mlp_out': lin(ks[9], (DEPTH, D_FF, D_MODEL), D_FF),
        'w_in_ret': lin(ks[10], (N_RET, D_MODEL, 4 * TOK_WIDTH + MEM_WIDTH), D_MODEL),
        'ret_decay': decay_init + 0.1 * normal(ks[11], (N_RET, 2, TOK_HEADS)),
        'w_in_na': lin(ks[12], (N_NA, D_MODEL, 3 * TOK_WIDTH + MEM_WIDTH), D_MODEL),
        'na_rpb': 0.05 * normal(ks[13], (N_NA, TOK_HEADS, 2 * NA_KH_MAX - 1, 2 * NA_KW - 1)),
    }


def reference(x_prompt, x_sample, mem_prompt, mem_sample, norm_gain, mem_norm_gain, w_mem_kv, w_out,
              w_mlp_in, w_mlp_out, w_in_ret, ret_decay, w_in_na, na_rpb):
    y_prompt = run_trunk(x_prompt, mem_prompt, norm_gain, mem_norm_gain, w_mem_kv, w_out, w_mlp_in, w_mlp_out,
                         w_in_ret, ret_decay, w_in_na, na_rpb)
    y_sample = run_trunk(x_sample, mem_sample, norm_gain, mem_norm_gain, w_mem_kv, w_out, w_mlp_in, w_mlp_out,
                         w_in_ret, ret_decay, w_in_na, na_rpb)
    return (y_prompt, y_sample)
```

```python
import os
import numpy as np
from contextlib import ExitStack
import concourse.bass as bass
import concourse.mybir as mybir
from concourse.bass_utils import run_bass_kernel_spmd

F32 = mybir.dt.float32
BF16 = mybir.dt.bfloat16
AF = mybir.ActivationFunctionType
ALU = mybir.AluOpType

D = 2048
NTOK = 8192
NSUB = 64
NTILE = 16
DFF = 8192
EPS = 1e-6
SCALE = 128.0 ** -0.5
W_RET = 6656
W_NA = 5120


class TR:
    def __init__(self, nc, es):
        self.nc = nc
        self.es = es
        self.eng = {"pe": nc.tensor, "act": nc.scalar, "dve": nc.vector, "pool": nc.gpsimd, "sp": nc.sync}
        self.sems = {}
        self.cnt = {}
        self.epoch = 0
        self.esem = {}
        self.waited = {e: {} for e in self.eng}
        self.lastw = {}
        self.reads = {}
        self.flip = 0
        self._new_engine_sems()

    def _sem(self, key):
        if key not in self.sems:
            self.sems[key] = self.es.enter_context(self.nc.semaphore("s_%s" % str(key).replace(" ", "")))
            self.cnt[key] = 0
        return self.sems[key]

    def _new_engine_sems(self):
        for e in ("pe", "act", "dve", "pool"):
            k = ("E", e, self.epoch)
            self._sem(k)
            self.esem[e] = k

    def _wait(self, e, tok):
        k, v = tok
        if k == self.esem.get(e) and e == "pe":
            return
        if self.waited[e].get(k, 0) >= v:
            return
        self.eng[e].wait_ge(self.sems[k], v)
        self.waited[e][k] = v

    def _deps(self, e, r, w):
        toks = []
        for x in r:
            t = self.lastw.get(x)
            if t is not None:
                toks.append(t)
        for x in w:
            t = self.lastw.get(x)
            if t is not None:
                toks.append(t)
            toks.extend(self.reads.get(x, ()))
        for t in toks:
            self._wait(e, t)

    def _commit(self, tok, r, w):
        for x in r:
            if isinstance(x, tuple) and x and x[0] == "const":
                continue
            self.reads.setdefault(x, []).append(tok)
        for x in w:
            self.lastw[x] = tok
            self.reads[x] = []

    def op(self, e, fn, r=(), w=()):
        self._deps(e, r, w)
        ins = fn(self.eng[e])
        k = self.esem[e]
        ins.then_inc(self.sems[k], 1)
        self.cnt[k] += 1
        tok = (k, self.cnt[k])
        self._commit(tok, r, w)
        return tok

    def dma(self, out, in_, r=(), w=(), sem="d", q="sp", **kw):
        self._deps(q, r, w)
        k = ("D", sem)
        s = self._sem(k)
        ins = self.eng[q].dma_start(out=out, in_=in_, **kw)
        ins.then_inc(s, 16)
        self.cnt[k] += 16
        tok = (k, self.cnt[k])
        self._commit(tok, r, w)
        return tok

    def barrier(self, new_sems=False):
        for e in self.eng:
            for k, s in self.sems.items():
                if self.cnt[k] > 0 and self.waited[e].get(k, 0) < self.cnt[k]:
                    if k == self.esem.get(e) and e == "pe":
                        continue
                    self.eng[e].wait_ge(s, self.cnt[k])
                    self.waited[e][k] = self.cnt[k]
        self.lastw = {}
        self.reads = {}
        if new_sems:
            self.epoch += 1
            self._new_engine_sems()

    def alt(self):
        self.flip ^= 1
        return "act" if self.flip else "dve"


def _row_window(core_is_sample, s, r_local):
    if core_is_sample:
        R = 32 * s + r_local
        return int(np.clip(R - 4, 0, 120))
    return 32 * s + int(np.clip(r_local - 4, 0, 24))


def _boundary_plan():
    plan = {}
    for s in range(4):
        for t in (0, 1, 14, 15):
            js = []
            for j in range(-3, 4):
                T2 = 16 * s + t + j
                if T2 < 0 or T2 > 63:
                    continue
                ok = False
                for samp in (False, True):
                    for b in (0, 1):
                        rs = _row_window(samp, s, 2 * t + b)
                        for a in (0, 1):
                            kr = 2 * T2 + a
                            if rs <= kr < rs + 8:
                                ok = True
                if ok:
                    js.append(j)
            plan[(s, t)] = js
    return plan


_BPLAN = _boundary_plan()
_MASKIDX = {}
for (_s, _t), _js in sorted(_BPLAN.items()):
    for _j in _js:
        for _b in (0, 1):
            _MASKIDX[(_s, _t, _j, _b)] = len(_MASKIDX)
NMASK = len(_MASKIDX)


def _mask_table(core_is_sample):
    m = np.zeros((128, NMASK), np.float32)
    a = np.arange(128) // 64
    for (s, t, j, b), idx in _MASKIDX.items():
        rs = _row_window(core_is_sample, s, 2 * t + b)
        kr = 2 * (16 * s + t + j) + a
        m[:, idx] = ((kr >= rs) & (kr < rs + 8)).astype(np.float32)
    return m


def _na_bias_tables(rpb):
    NEG = np.float32(-30000.0)
    kp = np.arange(128)
    a = (kp // 64)[:, None]
    kc = (kp % 64)[:, None]
    qp = np.arange(128)
    b = (qp // 64)[None, :]
    qc = (qp % 64)[None, :]
    ws = np.clip(qc - 8, 0, 48)
    colvalid = (kc >= ws) & (kc < ws + 16)
    dc = np.clip(kc - qc + 15, 0, 30)
    out = np.empty((12, 128, 12, 128), np.float32)
    tabs = [(j, True) for j in range(-2, 3)] + [(j, False) for j in range(-3, 4)]
    for ti, (j, interior) in enumerate(tabs):
        dr = 2 * j + a - b
        if interior:
            rowvalid = (dr >= -4) & (dr <= 3)
        else:
            rowvalid = (dr >= -7) & (dr <= 7)
        valid = colvalid & rowvalid
        dri = np.clip(dr + 7, 0, 14)
        for h in range(12):
            g = rpb[h][dri, dc]
            out[h, :, ti, :] = np.where(valid, g, NEG)
    return out


def _const_tables(core_is_sample):
    c = {}
    t = np.arange(NTOK)
    pos = (t if core_is_sample else (t % 2048)).astype(np.float32)
    half = 64
    inv_freq = (np.float32(10000.0) ** (-np.arange(half, dtype=np.float32) / np.float32(half))).astype(np.float32)
    ang = (pos[None, :] * inv_freq[:, None]).astype(np.float32)
    cos = np.cos(ang).astype(np.float32)
    sin = np.sin(ang).astype(np.float32)
    c["rotc"] = np.ascontiguousarray(np.concatenate([cos, cos], 0))
    c["rots"] = np.ascontiguousarray(np.concatenate([-sin, sin], 0))
    j = np.arange(128, dtype=np.float32)[:, None]
    i = np.arange(128, dtype=np.float32)[None, :]
    misc = np.zeros((128, 8, 128), np.float32)
    misc[:, 0, :] = np.maximum(i - j, 0.0)
    misc[:, 1, :] = np.maximum(j - i, 0.0)
    misc[:, 2, :] = (i >= j).astype(np.float32) * np.float32(SCALE)
    misc[:, 3, :] = (j > i).astype(np.float32) * np.float32(SCALE)
    misc[:, 4, :] = i + 1.0
    misc[:, 5, :] = 128.0 - i
    misc[:, 6, :] = np.eye(128, dtype=np.float32)
    perm = np.zeros((128, 128), np.float32)
    for m in range(128):
        perm[(m + 64) % 128, m] = 1.0
    misc[:, 7, :] = perm
    c["misc"] = misc
    cols = np.zeros((128, 8), np.float32)
    cols[:, 0] = 127.0 - np.arange(128)
    cols[:, 1] = np.arange(128)
    cols[:, 2] = 1.0 if core_is_sample else 0.0
    cols[:, 3] = EPS
    c["cols"] = cols
    c["maskc"] = _mask_table(core_is_sample)
    return c


def build_program(debug=False):
    nc = bass.Bass("TRN2", target_bir_lowering=False)

    def din(name, shape, dt=F32):
        return nc.dram_tensor(name, list(shape), dt, kind="ExternalInput").ap()

    dump = set(os.environ.get("MK_DUMP", "").split(","))

    def dscr(name, shape, dt):
        if debug and name in dump:
            return nc.dram_tensor(name, list(shape), dt, kind="ExternalOutput").ap()
        return nc.dram_tensor(name, list(shape), dt).ap()

    x_in = din("x", [NTOK, D])
    mem_in = din("mem", [4, 256, D])
    ng_in = din("norm_gain", [2, 4, D])
    mg_in = din("mem_norm_gain", [2, D])
    w_memkv = din("w_mem_kv", [2, D, 1024])
    w_out = din("w_out", [2, D, D])
    w_mi = din("w_mlp_in", [2, D, DFF])
    w_mo = din("w_mlp_out", [2, DFF, D])
    w_ret = din("w_in_ret", [D, W_RET])
    w_na = din("w_in_na", [D, W_NA])
    decay_in = din("ret_decay", [1, 24])
    btab_in = din("btab", [12, 128, 12, 128])
    rotc_in = din("rotc", [128, NTOK])
    rots_in = din("rots", [128, NTOK])
    misc_in = din("misc", [128, 8, 128])
    cols_in = din("cols", [128, 8])
    maskc_in = din("maskc", [128, NMASK])
    y_out = nc.dram_tensor("y", [NTOK, D], F32, kind="ExternalOutput").ap()

    wb_memkv = dscr("wb_memkv", [2, D, 1024], BF16)
    wb_out = dscr("wb_out", [2, D, D], BF16)
    wb_mi = dscr("wb_mi", [2, D, DFF], BF16)
    wb_mo = dscr("wb_mo", [2, DFF, D], BF16)
    wb_ret = dscr("wb_ret", [D, W_RET], BF16)
    wb_na = dscr("wb_na", [D, W_NA], BF16)
    QT = [dscr("qt%d" % l, [28, 128, NTOK], BF16) for l in range(2)]
    VG = dscr("vg", [NTOK, 3072], BF16)
    CAT = dscr("cat", [16, 128, NTOK], BF16)
    X1 = dscr("x1", [NTOK, D], F32)

    es = ExitStack()
    with es:
        tr = TR(nc, es)

        sbn = [0]

        def sb(name, shape, dt, stack=None):
            sbn[0] += 1
            return (stack or es).enter_context(nc.sbuf_tensor("sb%d_%s" % (sbn[0], name), list(shape), dt))

        banks = [es.enter_context(nc.psum_tensor("ps%d" % i, [128, 512], F32)) for i in range(8)]
        bank_rr = [0]

        def next_bank():
            b = bank_rr[0]
            bank_rr[0] = (b + 1) % 8
            return b

        def next_bank4():
            b = bank_rr[0]
            if b % 4 != 0:
                b = (b + 3) // 4 * 4 % 8
            bank_rr[0] = (b + 4) % 8
            return [b, b + 1, b + 2, b + 3]

        def bank_bf(b):
            return banks[b][:].bitcast(BF16)

        cols = sb("cols", [128, 8], F32)
        ident = sb("ident", [128, 128], BF16)
        perm = sb("perm", [128, 128], BF16)
        with ExitStack() as ps0:
            miscf0 = sb("miscf0", [128, 2, 128], F32, ps0)
            tr.dma(miscf0[:], misc_in[:, 6:8, :], w=["miscf0"], sem="c0")
            tr.dma(cols[:], cols_in, w=["cols"], sem="c1")
            tr.op("dve", lambda e: e.tensor_copy(out=ident[:], in_=miscf0[:, 0, :]), r=["miscf0"], w=["ident"])
            tr.op("dve", lambda e: e.tensor_copy(out=perm[:], in_=miscf0[:, 1, :]), r=["miscf0"], w=["perm"])
            tr.barrier()
        eps_col = cols[:, 3:4]
        carry_col = cols[:, 2:3]

        def cast(dst, src, key):
            n = 1
            for d_ in src.shape:
                n *= d_
            fl = "a b -> (a b)"
            s_ = src.rearrange(fl).rearrange("(p a n) -> p a n", p=128, n=2048)
            d_ = dst.rearrange(fl).rearrange("(p a n) -> p a n", p=128, n=2048)
            tr.dma(d_, s_, w=[("const", key)], sem="cast_" + key, q="pool")

        def emit_casts():
            cast(wb_memkv[0], w_memkv[0], "memkv0")
            cast(wb_out[0], w_out[0], "out0")
            cast(wb_mi[0], w_mi[0], "mi0")
            cast(wb_mo[0], w_mo[0], "mo0")
            cast(wb_na, w_na, "na")
            cast(wb_memkv[1], w_memkv[1], "memkv1")
            cast(wb_out[1], w_out[1], "out1")
            cast(wb_mi[1], w_mi[1], "mi1")
            cast(wb_mo[1], w_mo[1], "mo1")

        NSLOT = 4
        wslots = []
        wrr = [0]

        def load_wblock(wb, key, row0, col0):
            s = wrr[0]
            wrr[0] = (s + 1) % NSLOT
            src = wb[row0:row0 + 1024, col0:col0 + 512].rearrange("(kc p) n -> p kc n", p=128)
            ck = ("const", "ret", col0 // 512, row0 // 1024) if key == "ret" else ("const", key)
            tr.dma(wslots[s][:], src, r=[ck], w=[("w", s)], sem="w%d" % s)
            return s

        class WStream:
            def __init__(self, items, ahead=NSLOT - 1):
                self.items = items
                self.pos = 0
                self.slots = []
                self.ahead = ahead

            def fill(self):
                while self.pos < len(self.items) and len(self.slots) < self.ahead:
                    self.slots.append(load_wblock(*self.items[self.pos]))
                    self.pos += 1

            def get(self):
                self.fill()
                s = self.slots.pop(0)
                return s

        def unit_items(wb, key, row0, nkc, col0):
            return [(wb, key, row0 + hb * 1024, col0) for hb in range(nkc // 8)]

        def unit_fm(ws, nkc, actT, evac):
            bs = next_bank4()
            nhb = nkc // 8
            slots = []
            for hb in range(nhb):
                s = ws.get()
                slots.append(s)

                def f(e, hb=hb, s=s):
                    ins = None
                    for c in range(4):
                        for k8 in range(8):
                            kc = hb * 8 + k8
                            ins = e.matmul(banks[bs[c]][:], lhsT=wslots[s][:, k8, c * 128:(c + 1) * 128],
                                           rhs=actT[:, kc, :], start=(kc == 0), stop=(kc == nkc - 1))
                    return ins
                tr.op("pe", f, r=[("w", s), "actT"], w=[("bank", b) for b in bs])
                ws.fill()
            for c in range(4):
                evac(c, bs[c])

        def unit_tm(ws, nkc, actT, evac, kc_off=0, akey="actT"):
            bs = next_bank4()
            nhb = nkc // 8
            for hb in range(nhb):
                s = ws.get()

                def f(e, hb=hb, s=s):
                    ins = None
                    for k8 in range(8):
                        kc = hb * 8 + k8
                        for su in range(4):
                            ins = e.matmul(banks[bs[su]][:], lhsT=actT[:, kc_off + kc, su * 128:(su + 1) * 128],
                                           rhs=wslots[s][:, k8, :], start=(kc == 0), stop=(kc == nkc - 1))
                    return ins
                tr.op("pe", f, r=[("w", s)] + (list(akey) if isinstance(akey, (list, tuple)) and akey and isinstance(akey[0], tuple) else [akey]), w=[("bank", b) for b in bs])
                ws.fill()
            for su in range(4):
                evac(su, bs[su])

        def evac_copy(dst_ap, b, wkeys):
            e = tr.alt()
            if e == "act":
                tr.op("act", lambda en: en.activation(out=dst_ap, in_=banks[b][:], func=AF.Copy),
                      r=[("bank", b)], w=wkeys)
            else:
                tr.op("dve", lambda en: en.tensor_copy(out=dst_ap, in_=banks[b][:]), r=[("bank", b)], w=wkeys)

        def rstd_from_ss(ss_ap, rs_ap, n, keys_r, keys_w, inv_n):
            tr.op("act", lambda e: e.activation(out=rs_ap, in_=ss_ap, func=AF.Sqrt, scale=inv_n, bias=eps_col[0:n] if n < 128 else eps_col),
                  r=keys_r + ["cols"], w=keys_w)
            tr.op("dve", lambda e: e.reciprocal(out=rs_ap, in_=rs_ap), r=keys_w, w=keys_w)

        def transposes_to_actT(h_ap, hkey, actT, su):
            for g in range(4):
                b = next_bank()
                tb = bank_bf(b)

                def f(e, g=g, tb=tb):
                    ins = None
                    for q4 in range(4):
                        kc = g * 4 + q4
                        ins = e.transpose(out=tb[:, q4 * 128:(q4 + 1) * 128], in_=h_ap[:, kc * 128:(kc + 1) * 128],
                                          identity=ident[:])
                    return ins
                tr.op("pe", f, r=[hkey, "ident"], w=[("bank", b)])
                src = tb[:, 0:512].rearrange("p (a t) -> p a t", a=4)
                dst = actT[:, g * 4:(g + 1) * 4, su * 128:(su + 1) * 128]
                e_ = tr.alt()
                if e_ == "act":
                    tr.op("act", lambda en, src=src, dst=dst: en.activation(out=dst, in_=src, func=AF.Copy),
                          r=[("bank", b)], w=["actT"])
                else:
                    tr.op("dve", lambda en, src=src, dst=dst: en.tensor_copy(out=dst, in_=src),
                          r=[("bank", b)], w=["actT"])

        memh = {}

        def alloc_mem(stack):
            memh["K"] = sb("memKT", [128, 4, 4, 256], BF16, stack)
            memh["V"] = sb("memV", [128, 4, 2, 4, 129], BF16, stack)
            tr.op("pool", lambda e: e.memset(memh["V"][:], 1.0), w=["memV"])

        def phase_mem(layer):
            with ExitStack() as ps:
                wkv = sb("wkv", [128, 16, 1024], BF16, ps)
                mt = sb("mt", [128, 2, D], F32, ps)
                mh = sb("mh", [128, D], BF16, ps)
                mT = sb("mT", [128, 16, 256], BF16, ps)
                gm = sb("gm", [128, D], F32, ps)
                junk = sb("mjunk", [128, D], BF16, ps)
                ssm = sb("ssm", [128, 2], F32, ps)
                key = "memkv%d" % layer
                tr.dma(wkv[:], wb_memkv[layer].rearrange("(kc p) n -> p kc n", p=128), r=[("const", key)], w=["wkv"], sem="mk")
                tr.dma(gm[:], mg_in[layer:layer + 1, :].broadcast_to([128, D]), w=["gm"], sem="mk3")
                for seg in range(4):
                    tr.dma(mt[:], mem_in[seg].rearrange("(c p) d -> p c d", p=128), w=["mt"], sem="mk2")
                    for c in range(2):
                        tr.op("act", lambda e, c=c: e.activation(out=junk[:], in_=mt[:, c, :], func=AF.Square,
                                                                 accum_out=ssm[:, c:c + 1]), r=["mt"], w=["mjunk", ("ssm", c)])
                    rstd_from_ss(ssm[:], ssm[:], 128, [("ssm", 0), ("ssm", 1)], [("ssm", 0), ("ssm", 1)], 1.0 / D)
                    for c in range(2):
                        tr.op("dve", lambda e, c=c: e.scalar_tensor_tensor(out=mh[:], in0=mt[:, c, :], scalar=ssm[:, c:c + 1],
                                                                          in1=gm[:], op0=ALU.mult, op1=ALU.mult),
                              r=["mt", ("ssm", 0), ("ssm", 1), "gm"], w=["mh"])
                        for g in range(4):
                            b = next_bank()
                            tb = bank_bf(b)

                            def f(e, g=g, tb=tb):
                                ins = None
                                for q4 in range(4):
                                    kc = g * 4 + q4
                                    ins = e.transpose(out=tb[:, q4 * 128:(q4 + 1) * 128], in_=mh[:, kc * 128:(kc + 1) * 128],
                                                      identity=ident[:])
                                return ins
                            tr.op("pe", f, r=["mh", "ident"], w=[("bank", b)])
                            tr.op("act", lambda en, tb=tb, g=g, c=c: en.activation(
                                out=mT[:, g * 4:(g + 1) * 4, c * 128:(c + 1) * 128],
                                in_=tb[:, 0:512].rearrange("p (a t) -> p a t", a=4), func=AF.Copy),
                                r=[("bank", b)], w=["mT"])
                    for h in range(4):
                        b = next_bank()

                        def f(e, h=h, b=b):
                            ins = None
                            for kc in range(16):
                                ins = e.matmul(banks[b][:, 0:256], lhsT=wkv[:, kc, h * 128:(h + 1) * 128], rhs=mT[:, kc, :],
                                               start=(kc == 0), stop=(kc == 15))
                            return ins
                        tr.op("pe", f, r=["wkv", "mT"], w=[("bank", b)])
                        tr.op("act", lambda en, h=h, b=b, seg=seg: en.activation(out=memh["K"][:, seg, h, :], in_=banks[b][:, 0:256],
                                                                               func=AF.Copy), r=[("bank", b)], w=["memKT"])
                    for c in range(2):
                        b = next_bank()

                        def f(e, c=c, b=b):
                            ins = None
                            for kc in range(16):
                                ins = e.matmul(banks[b][:], lhsT=mT[:, kc, c * 128:(c + 1) * 128], rhs=wkv[:, kc, 512:1024],
                                               start=(kc == 0), stop=(kc == 15))
                            return ins
                        tr.op("pe", f, r=["wkv", "mT"], w=[("bank", b)])
                        tr.op("dve", lambda en, c=c, b=b, seg=seg: en.tensor_copy(
                            out=memh["V"][:, seg, c, :, 0:128], in_=banks[b][:].rearrange("p (h e) -> p h e", h=4)),
                            r=[("bank", b)], w=["memV"])
                tr.barrier()

        def inproj(layer, tile_i, actT, stage_views, hook=None):
            tok0 = tile_i * 512
            if layer == 0:
                wb, key, nblk = wb_ret, "ret", 13
                kinds = ["fm"] * 6 + ["tm"] * 6 + ["fm"]
            else:
                wb, key, nblk = wb_na, "na", 10
                kinds = ["fm"] * 6 + ["tm"] * 3 + ["fm"]
            items = []
            for b_ in range(nblk):
                items += unit_items(wb, key, 0, 16, b_ * 512)
            ws = WStream(items)
            ws.fill()
            fm_i = 0
            tm_i = 0
            for b_ in range(nblk):
                if hook is not None and b_ in hook:
                    hook[b_]()
                if kinds[b_] == "fm":
                    sv, skey = stage_views[fm_i % 2]
                    fm_i += 1
                    if layer == 0:
                        chunk0 = 4 * b_ if b_ < 6 else 24
                    else:
                        chunk0 = 4 * b_ if b_ < 6 else 24

                    def ev(c, bank, sv=sv, skey=skey):
                        evac_copy(sv[:, c, :], bank, [skey])
                    unit_fm(ws, 16, actT, ev)
                    dst = QT[layer][chunk0:chunk0 + 4, :, tok0:tok0 + 512].rearrange("c p t -> p c t")
                    tr.dma(dst, sv, r=[skey], sem="st_" + str(skey))
                else:
                    sv, skey = stage_views[2 + tm_i % 2]
                    tm_i += 1
                    col0 = (b_ - 6) * 512

                    def ev(su, bank, sv=sv, skey=skey):
                        evac_copy(sv[:, su, :], bank, [skey])
                    unit_tm(ws, 16, actT, ev)
                    dst = VG[tok0:tok0 + 512, col0:col0 + 512].rearrange("(s p) c -> p s c", p=128)
                    tr.dma(dst, sv, r=[skey], sem="st_" + str(skey))

        def phase_c(layer, a0=False):
            with ExitStack() as ps:
                del wslots[:]
                wslots.extend(sb("wslot%d" % i, [128, 8, 512], BF16, ps) for i in range(NSLOT))
                xt = sb("xt", [128, 4, D], F32, ps)
                xp = sb("xp", [128, 2, D], F32, ps)
                actT = sb("actT", [128, 16, 512], BF16, ps)
                hs = sb("hs", [128, 4, D], BF16, ps)
                catT = hs[:].rearrange("p s d -> p (s d)").rearrange("p (c t) -> p c t", c=16)
                HSK = [("hs", su) for su in range(4)]

                def X(ti, su):
                    if su < 2 and ti % 2 == 1:
                        return xp[:, su, :], ("xp", su)
                    return xt[:, su, :], ("xt", su)
                ga = sb("ga", [128, D], F32, ps)
                gb = ga if a0 else sb("gb", [128, D], F32, ps)
                gc = ga if a0 else sb("gc", [128, D], F32, ps)
                tmp2 = sb("tmp2", [128, 2, 512], F32, ps)
                junk = tmp2[:].rearrange("p a n -> p (a n)").bitcast(BF16)
                ss = sb("ss", [128, 16], F32, ps)
                ssp = sb("ssp", [128, 2, 4, 4], F32, ps)
                o = sb("o", [128, 4, D], F32, ps)
                stage_views = []
                for k_ in range(4):
                    v = o[:, k_, :].bitcast(BF16)[:, 0:2048].rearrange("p (a n) -> p a n", a=4)
                    stage_views.append((v, ("o", k_)))
                uT = sb("uT", [128, 32, 512] if not a0 else [128, 1, 512], BF16, ps)
                mixed = None if a0 else uT[:].rearrange("p a n -> p (a n)").bitcast(F32).rearrange("p (s d) -> p s d", s=4)
                xsrc = x_in if layer == 0 else X1

                def load_gain(buf, bkey, l_, gi):
                    tr.dma(buf[:], ng_in[l_, gi:gi + 1, :].broadcast_to([128, D]), w=[bkey], sem="g_" + bkey)

                def load_x(ti, sus=(0, 1, 2, 3)):
                    tok0 = ti * 512
                    for su in sus:
                        xa, xk = X(ti, su)
                        tr.dma(xa, xsrc[tok0 + su * 128:tok0 + (su + 1) * 128, :], w=[xk], sem="x%d" % su)

                def load_cat(ti):
                    tok0 = ti * 512
                    tr.dma(catT, CAT[:, :, tok0:tok0 + 512].rearrange("c p t -> p c t"), w=HSK, sem="cat")

                def sumsq(src_ap, rkeys, col):
                    tr.op("act", lambda e: e.activation(out=junk, in_=src_ap, func=AF.Square, accum_out=ss[:, col:col + 1]),
                          r=rkeys, w=[("tmp", 0), ("tmp", 1), ("ss", col)])

                def evac_block(dst, skey_fn, gbuf, gkey, pi):
                    def mk(cb, add_to=None):
                        def ev(su, bank):
                            d = dst[:, su, cb * 512:(cb + 1) * 512]
                            if add_to is None:
                                tr.op("act", lambda en: en.activation(out=d, in_=banks[bank][:], func=AF.Copy),
                                      r=[("bank", bank)], w=[skey_fn(su)])
                            else:
                                tr.op("dve", lambda en: en.tensor_tensor(out=d, in0=banks[bank][:], in1=d, op=ALU.add),
                                      r=[("bank", bank), skey_fn(su)], w=[skey_fn(su)])
                        return ev
                    return mk

                def finish_block(dst, skey_fn, gbuf, gkey, pi, cb):
                    for su in range(4):
                        d = dst[:, su, cb * 512:(cb + 1) * 512]
                        tpi = (cb * 4 + su) % 2
                        tr.op("act", lambda en, d=d, su=su, tpi=tpi: en.activation(out=tmp2[:, tpi, :], in_=d, func=AF.Square,
                                                                                 accum_out=ssp[:, pi, su, cb:cb + 1]),
                              r=[skey_fn(su)], w=[("tmp", tpi), ("ssp", pi, su, cb)])
                        tr.op("pool", lambda en, d=d: en.tensor_tensor(out=d, in0=d, in1=gbuf[:, cb * 512:(cb + 1) * 512], op=ALU.mult),
                              r=[skey_fn(su), gkey, ("ssp", pi, su, cb)], w=[skey_fn(su)])

                def residual_from(dst, skey_fn, pi, col0, ti):
                    pk = [("ssp", pi, su, cb) for su in range(4) for cb in range(4)]
                    kk = [("ss", col0 + su) for su in range(4)]
                    tr.op("dve", lambda e: e.tensor_tensor(out=ss[:, col0:col0 + 4], in0=ssp[:, pi, :, 0], in1=ssp[:, pi, :, 1], op=ALU.add),
                          r=pk, w=kk)
                    tr.op("dve", lambda e: e.tensor_tensor(out=ss[:, col0:col0 + 4], in0=ss[:, col0:col0 + 4], in1=ssp[:, pi, :, 2], op=ALU.add),
                          r=pk + kk, w=kk)
                    tr.op("dve", lambda e: e.tensor_tensor(out=ss[:, col0:col0 + 4], in0=ss[:, col0:col0 + 4], in1=ssp[:, pi, :, 3], op=ALU.add),
                          r=pk + kk, w=kk)
                    rstd_from_ss(ss[:, col0:col0 + 4], ss[:, col0:col0 + 4], 128, kk, kk, 1.0 / D)
                    for su in range(4):
                        xa, xk = X(ti, su)
                        tr.op("dve", lambda e, su=su, xa=xa: e.scalar_tensor_tensor(
                            out=xa, in0=dst[:, su, :], scalar=ss[:, col0 + su:col0 + su + 1], in1=xa,
                            op0=ALU.mult, op1=ALU.add), r=[skey_fn(su), xk] + kk, w=[xk])

                def norm_to_hs(gbuf, gkey, col0, ti):
                    for su in range(4):
                        xa, xk = X(ti, su)
                        sumsq(xa, [xk], col0 + su)
                        k1 = [("ss", col0 + su)]
                        rstd_from_ss(ss[:, col0 + su:col0 + su + 1], ss[:, col0 + su:col0 + su + 1], 128, k1, k1, 1.0 / D)
                        tr.op("dve", lambda e, su=su, xa=xa: e.scalar_tensor_tensor(
                            out=hs[:, su, :], in0=xa, scalar=ss[:, col0 + su:col0 + su + 1], in1=gbuf[:],
                            op0=ALU.mult, op1=ALU.mult), r=[xk, gkey] + k1, w=[("hs", su)])

                def hs_to_actT():
                    for su in range(4):
                        transposes_to_actT(hs[:, su, :], ("hs", su), actT, su)

                def outproj(ti):
                    items = []
                    for cb in range(4):
                        items += unit_items(wb_out[layer], "out%d" % layer, 0, 16, cb * 512)
                    ws = WStream(items)
                    ws.fill()
                    mk = evac_block(mixed, lambda su: ("uT", su), ga, "ga", 0)
                    for cb in range(4):
                        unit_tm(ws, 16, catT, mk(cb), akey=HSK)
                        finish_block(mixed, lambda su: ("uT", su), ga, "ga", 0, cb)

                def pre_mlp(ti):
                    residual_from(mixed, lambda su: ("uT", su), 0, 0, ti)
                    norm_to_hs(gb, "gb", 4, ti)

                if a0:
                    load_gain(ga, "ga", 0, 0)
                    load_x(0)
                    pc = ExitStack()
                    wf = [sb("wf%d" % i, [128, 8, 512], F32, pc) for i in range(2)]
                    wc = [sb("wc%d" % i, [128, 8, 512], BF16, pc) for i in range(2)]
                    pn = [0]

                    def cast_block(b_):
                        for hb in range(2):
                            pp = pn[0] % 2
                            pn[0] += 1
                            rows = slice(hb * 1024, (hb + 1) * 1024)
                            csl_ = slice(b_ * 512, (b_ + 1) * 512)
                            tr.dma(wf[pp][:], w_ret[rows, csl_].rearrange("(kc p) n -> p kc n", p=128), w=[("wf", pp)], sem="cl%d" % pp)
                            tr.op("pool", lambda e, pp=pp: e.tensor_copy(out=wc[pp][:, 0:3, :], in_=wf[pp][:, 0:3, :]),
                                  r=[("wf", pp)], w=[("wc", pp, 0)])
                            tr.op("act", lambda e, pp=pp: e.activation(out=wc[pp][:, 3:5, :], in_=wf[pp][:, 3:5, :], func=AF.Copy),
                                  r=[("wf", pp)], w=[("wc", pp, 1)])
                            tr.op("dve", lambda e, pp=pp: e.tensor_copy(out=wc[pp][:, 5:8, :], in_=wf[pp][:, 5:8, :]),
                                  r=[("wf", pp)], w=[("wc", pp, 2)])
                            tr.dma(wb_ret[rows, csl_].rearrange("(kc p) n -> p kc n", p=128), wc[pp][:],
                                   r=[("wc", pp, 0), ("wc", pp, 1), ("wc", pp, 2)], w=[("const", "ret", b_, hb)], sem="cw%d" % pp)

                    cast_block(0)
                    cast_block(1)
                    norm_to_hs(ga, "ga", 12, 0)
                    for ti in range(NTILE):
                        hs_to_actT()
                        hooks = {}
                        if ti == 0:
                            for b_ in range(11):
                                hooks[b_] = [(lambda b_=b_: cast_block(b_ + 2))]
                        if ti + 1 < NTILE:
                            hooks.setdefault(0, []).append(lambda ti=ti: load_x(ti + 1))
                            hooks.setdefault(3, []).append(lambda ti=ti: norm_to_hs(ga, "ga", 12, ti + 1))
                        hk = {k_: (lambda fs=fs: [f_() for f_ in fs]) for k_, fs in hooks.items()}
                        inproj(0, ti, actT, stage_views, hook=hk)
                        if ti == 0:
                            emit_casts()
                    pc.close()
                    tr.barrier()
                    return
                load_gain(ga, "ga", layer, 1)
                load_gain(gb, "gb", layer, 2)
                load_gain(gc, "gc", layer, 3)
                load_x(0)
                load_cat(0)
                outproj(0)
                pre_mlp(0)
                if layer == 0:
                    load_gain(gb, "gb", 1, 0)
                for ti in range(NTILE):
                    tok0 = ti * 512
                    hs_to_actT()
                    for half in range(2):
                        items = []
                        for b_ in range(8):
                            items += unit_items(wb_mi[layer], "mi%d" % layer, 0, 16, half * 4096 + b_ * 512)
                        for cb in range(4):
                            items += unit_items(wb_mo[layer], "mo%d" % layer, half * 4096, 32, cb * 512)
                        ws = WStream(items)
                        ws.fill()
                        for b_ in range(8):
                            def ev(c, bank, b_=b_):
                                tpi = (b_ * 4 + c) % 2
                                tr.op("act", lambda en: en.activation(out=tmp2[:, tpi, :], in_=banks[bank][:], func=AF.Relu),
                                      r=[("bank", bank)], w=[("tmp", tpi)])
                                tr.op("pool", lambda en: en.tensor_tensor(out=uT[:, b_ * 4 + c, :], in0=tmp2[:, tpi, :],
                                                                          in1=tmp2[:, tpi, :], op=ALU.mult),
                                      r=[("tmp", tpi)], w=[("uT", 0), ("uT", 1), ("uT", 2), ("uT", 3)])
                            unit_fm(ws, 16, actT, ev)
                        if half == 1 and ti + 1 < NTILE:
                            load_cat(ti + 1)
                            load_x(ti + 1, (0, 1))
                        mk = evac_block(o, lambda su: ("o", su), gc, "gc", 1)
                        for cb in range(4):
                            unit_tm_u(ws, uT, mk(cb, add_to=(None if half == 0 else True)))
                            if half == 1:
                                finish_block(o, lambda su: ("o", su), gc, "gc", 1, cb)
                    if ti + 1 < NTILE:
                        outproj(ti + 1)
                    residual_from(o, lambda su: ("o", su), 1, 8, ti)
                    if layer == 0:
                        for su in range(4):
                            xa, xk = X(ti, su)
                            tr.dma(X1[tok0 + su * 128:tok0 + (su + 1) * 128, :], xa, r=[xk], sem="xo%d" % su)
                        norm_to_hs(gb, "gb", 12, ti)
                        hs_to_actT()
                        hooks = {}
                        if ti + 1 < NTILE:
                            def hook1(ti=ti):
                                load_gain(gb, "gb", 0, 2)
                                load_x(ti + 1, (2, 3))

                            def hook2(ti=ti):
                                pre_mlp(ti + 1)
                            hooks = {0: hook1, 3: hook2}
                        inproj(1, ti, actT, stage_views, hook=hooks)
                        if ti + 1 < NTILE:
                            load_gain(gb, "gb", 1, 0)
                    else:
                        for su in range(4):
                            xa, xk = X(ti, su)
                            tr.dma(y_out[tok0 + su * 128:tok0 + (su + 1) * 128, :], xa, r=[xk], sem="xo%d" % su)
                        if ti + 1 < NTILE:
                            load_x(ti + 1, (2, 3))
                            pre_mlp(ti + 1)
                tr.barrier(new_sems=True)

        def unit_tm_u(ws, uT, evac):
            bs = next_bank4()
            for hb in range(4):
                s = ws.get()

                def f(e, hb=hb, s=s):
                    ins = None
                    for k8 in range(8):
                        n = hb * 8 + k8
                        for su in range(4):
                            ins = e.matmul(banks[bs[su]][:], lhsT=uT[:, n, su * 128:(su + 1) * 128], rhs=wslots[s][:, k8, :],
                                           start=(n == 0), stop=(n == 31))
                    return ins
                tr.op("pe", f, r=[("w", s), ("uT", 0), ("uT", 1), ("uT", 2), ("uT", 3)], w=[("bank", b) for b in bs])
                ws.fill()
            for su in range(4):
                evac(su, bs[su])

        def pipeline(N, stages):
            S = len(stages)
            for step in range(N + S - 1):
                for si in range(S - 1, -1, -1):
                    n = step - si
                    if 0 <= n < N:
                        stages[si](n)

        RB = 4

        def phase_memattn(layer, ps, catst4):
            qm = sb("qm", [128, RB, 512], BF16, ps)
            pT = sb("pT", [128, RB, 2, 512], BF16, ps)
            rc = sb("rc", [128, RB, 4], F32, ps)
            om = sb("om", [128, RB, 4, 128], BF16, ps)
            st = {}

            def s0(i):
                ti, h = i // 4, i % 4
                seg = ti // 4
                tok0 = ti * 512
                r = i % RB
                tr.dma(qm[:, r, :], QT[layer][24 + h, :, tok0:tok0 + 512], w=[("qm", r)], sem="qm%d" % r)
                bs = [next_bank(), next_bank()]
                st[i] = bs
                for kc in range(2):
                    tr.op("pe", lambda e, kc=kc: e.matmul(banks[bs[kc]][:], lhsT=memh["K"][:, seg, h, kc * 128:(kc + 1) * 128],
                                                          rhs=qm[:, r, :], start=True, stop=True),
                          r=["memKT", ("qm", r)], w=[("bank", bs[kc])])

            def s1(i):
                r = i % RB
                bs = st[i]
                for kc in range(2):
                    tr.op("act", lambda e, kc=kc: e.activation(out=pT[:, r, kc, :], in_=banks[bs[kc]][:], func=AF.Exp, scale=SCALE),
                          r=[("bank", bs[kc])], w=[("pT", r)])

            def s2(i):
                ti, h = i // 4, i % 4
                seg = ti // 4
                r = i % RB
                bs = st[i]
                for pr in range(2):
                    def f(e, pr=pr):
                        ins = None
                        for s2_ in range(2):
                            su = pr * 2 + s2_
                            for kc in range(2):
                                ins = e.matmul(banks[bs[pr]][:, s2_ * 129:(s2_ + 1) * 129], lhsT=pT[:, r, kc, su * 128:(su + 1) * 128],
                                               rhs=memh["V"][:, seg, kc, h, :], start=(kc == 0), stop=(kc == 1))
                        return ins
                    tr.op("pe", f, r=[("pT", r), "memV"], w=[("bank", bs[pr])])

            def s3(i):
                r = i % RB
                bs = st[i]
                for pr in range(2):
                    v = banks[bs[pr]][:, 0:258].rearrange("p (s c) -> p s c", c=129)
                    tr.op("dve", lambda e, v=v, pr=pr: e.reciprocal(out=rc[:, r, pr * 2:pr * 2 + 2].unsqueeze(2), in_=v[:, :, 128:129]),
                          r=[("bank", bs[pr])], w=[("rc", r)])
                    tr.op("dve", lambda e, v=v, pr=pr: e.tensor_tensor(out=om[:, r, pr * 2:pr * 2 + 2, :], in0=v[:, :, 0:128],
                                                                      in1=rc[:, r, pr * 2:pr * 2 + 2].unsqueeze(2).to_broadcast([128, 2, 128]),
                                                                      op=ALU.mult),
                          r=[("bank", bs[pr]), ("rc", r)], w=[("om", r)])

            def s4(i):
                ti, h = i // 4, i % 4
                tok0 = ti * 512
                r = i % RB
                bs = st.pop(i)
                ci = i % 4
                for pr in range(2):
                    tb = bank_bf(bs[pr])

                    def f(e, pr=pr, tb=tb):
                        ins = None
                        for s2_ in range(2):
                            ins = e.transpose(out=tb[:, 768 + s2_ * 128:768 + (s2_ + 1) * 128], in_=om[:, r, pr * 2 + s2_, :], identity=ident[:])
                        return ins
                    tr.op("pe", f, r=[("om", r), "ident"], w=[("bank", bs[pr])])
                    en_ = tr.alt()
                    if en_ == "act":
                        tr.op("act", lambda e, pr=pr, tb=tb: e.activation(out=catst4[:, ci, pr * 256:(pr + 1) * 256], in_=tb[:, 768:1024], func=AF.Copy),
                              r=[("bank", bs[pr])], w=[("catst", ci)])
                    else:
                        tr.op("dve", lambda e, pr=pr, tb=tb: e.tensor_copy(out=catst4[:, ci, pr * 256:(pr + 1) * 256], in_=tb[:, 768:1024]),
                              r=[("bank", bs[pr])], w=[("catst", ci)])
                tr.dma(CAT[12 + h, :, tok0:tok0 + 512], catst4[:, ci, :], r=[("catst", ci)], sem="cs%d" % ci)

            pipeline(NTILE * 4, [s0, s1, s2, s3, s4])

        def phase_ret(catst):
            with ExitStack() as ps:
                qT2 = [sb("qT%d" % i, [128, NTOK], BF16, ps) for i in range(2)]
                kT2 = [sb("kT%d" % i, [128, NTOK], BF16, ps) for i in range(2)]
                Kt = sb("Kt", [128, NSUB, 128], BF16, ps)
                V2 = [sb("V%d" % i, [128, NSUB, 128], BF16, ps) for i in range(2)]
                Vs = sb("Vs", [128, NSUB, 128], BF16, ps)
                Sb = sb("Sb", [128, NSUB, 128], BF16, ps)
                rc_ = sb("rotc", [128, 2, 512], F32, ps)
                rs_ = sb("rots", [128, 2, 512], F32, ps)
                ta = sb("ta", [128, 2, 512], F32, ps)
                tb_ = sb("tbb", [128, 2, 512], F32, ps)
                dtab = sb("dtab", [128, 24], F32, ps)
                lg = sb("lg", [128, 24], F32, ps)
                hd = sb("hd", [128, 6, 128], F32, ps)
                hc = sb("hc", [128, 8], F32, ps)
                Sf = sb("Sf", [128, 2, 128], F32, ps)
                Sbk = sb("Sbk", [128, 2, 128], F32, ps)
                Sfb = sb("Sfb", [128, RB, 128], BF16, ps)
                qfb = sb("qfb", [128, 3, 2, 512], BF16, ps)
                gt = sb("gt", [128, 2, 1024], BF16, ps)
                sg = sb("sg", [128, 2, 1024], F32, ps)
                PTm = sb("PTm", [128, RB, 128], BF16, ps)
                tok = sb("tok", [128, RB, 128], BF16, ps)
                nst = sb("nst", [128, RB], F32, ps)
                junk = sb("rjunk", [128, 128], F32, ps)
                miscf = sb("miscf", [128, 6, 128], F32, ps)
                tr.dma(miscf[:], misc_in[:, 0:6, :], w=["miscf"], sem="c0")

                tr.dma(dtab[:], decay_in.broadcast_to([128, 24]), w=["dtab"], sem="dt")
                tr.op("act", lambda e: e.activation(out=dtab[:], in_=dtab[:], func=AF.Exp, scale=-float(np.log(2.0))),
                      r=["dtab"], w=["dtab"])
                tr.op("dve", lambda e: e.tensor_scalar(out=lg[:], in0=dtab[:], scalar1=1.0 / 9.0, scalar2=None, op0=ALU.mult),
                      r=["dtab"], w=["lg"])
                for kk in range(8, 0, -1):
                    tr.op("dve", lambda e, kk=kk: e.scalar_tensor_tensor(out=lg[:], in0=lg[:], scalar=1.0 / kk, in1=dtab[:],
                                                                        op0=ALU.add, op1=ALU.mult), r=["lg", "dtab"], w=["lg"])
                tr.op("dve", lambda e: e.tensor_scalar(out=lg[:], in0=lg[:], scalar1=-1.0, scalar2=None, op0=ALU.mult),
                      r=["lg"], w=["lg"])
                def head_loads(h):
                    hp = h % 2
                    tr.dma(qT2[hp][:], QT[0][h], w=[("qT", hp, b_) for b_ in range(16)], sem="lq%d" % hp)
                    tr.dma(kT2[hp][:], QT[0][12 + h], w=[("kT", hp, b_) for b_ in range(16)], sem="lk%d" % hp)
                    for q4 in range(4):
                        tr.dma(V2[hp][:, q4 * 16:(q4 + 1) * 16, :],
                               VG[q4 * 2048:(q4 + 1) * 2048, h * 128:(h + 1) * 128].rearrange("(n p) e -> p n e", p=128),
                               w=[("V", hp)], sem="lv%d" % hp)

                def rotary_block(hh, blk):
                    hpp = hh % 2
                    pi = blk % 2
                    tsl = slice(blk * 512, (blk + 1) * 512)
                    tr.dma(rc_[:, pi, :], rotc_in[:, tsl], w=[("rc", pi)], sem="rc%d" % pi)
                    tr.dma(rs_[:, pi, :], rots_in[:, tsl], w=[("rs", pi)], sem="rs%d" % pi)
                    for qk, (nm0, T_) in enumerate((("qT", qT2[hpp]), ("kT", kT2[hpp]))):
                        ti_ = qk
                        nm = (nm0, hpp)
                        b = next_bank()
                        tr.op("pe", lambda e, b=b, T_=T_: e.matmul(banks[b][:], lhsT=perm[:], rhs=T_[:, tsl], start=True, stop=True),
                              r=[nm + (blk,), "perm"], w=[("bank", b)])
                        tr.op("dve", lambda e, T_=T_: e.tensor_tensor(out=ta[:, ti_, :], in0=T_[:, tsl], in1=rc_[:, pi, :], op=ALU.mult),
                              r=[nm + (blk,), ("rc", pi)], w=[("ta", ti_)])
                        tr.op("dve", lambda e, b=b: e.tensor_tensor(out=tb_[:, ti_, :], in0=banks[b][:], in1=rs_[:, pi, :], op=ALU.mult),
                              r=[("bank", b), ("rs", pi)], w=[("tb", ti_)])
                        tr.op("pool", lambda e, T_=T_: e.tensor_tensor(out=T_[:, tsl], in0=ta[:, ti_, :], in1=tb_[:, ti_, :], op=ALU.add),
                              r=[("ta", ti_), ("tb", ti_)], w=[nm + (blk,)])

                head_loads(0)
                for h in range(12):
                    lf = lg[:, h:h + 1]
                    lb = lg[:, 12 + h:13 + h]
                    hp = h % 2
                    qT, kT, V = qT2[hp], kT2[hp], V2[hp]
                    Vk = ("V", hp)
                    tr.op("act", lambda e: e.activation(out=hd[:, 0, :], in_=miscf[:, 0, :], func=AF.Exp, scale=lf),
                          r=["lg", "miscf"], w=[("hd", 0)])
                    tr.op("act", lambda e: e.activation(out=hd[:, 1, :], in_=miscf[:, 1, :], func=AF.Exp, scale=lb),
                          r=["lg", "miscf"], w=[("hd", 1)])
                    tr.op("dve", lambda e: e.tensor_tensor(out=hd[:, 0, :], in0=hd[:, 0, :], in1=miscf[:, 2, :], op=ALU.mult),
                          r=[("hd", 0), "miscf"], w=[("hd", 0)])
                    tr.op("dve", lambda e: e.tensor_tensor(out=hd[:, 1, :], in0=hd[:, 1, :], in1=miscf[:, 3, :], op=ALU.mult),
                          r=[("hd", 1), "miscf"], w=[("hd", 1)])
                    tr.op("dve", lambda e: e.tensor_tensor(out=hd[:, 2, :], in0=hd[:, 0, :], in1=hd[:, 1, :], op=ALU.add),
                          r=[("hd", 0), ("hd", 1)], w=[("hd", 2)])
                    tr.op("act", lambda e: e.activation(out=hd[:, 3, :], in_=miscf[:, 4, :], func=AF.Exp, scale=lf),
                          r=["lg", "miscf"], w=[("hd", 3)])
                    tr.op("act", lambda e: e.activation(out=hd[:, 4, :], in_=miscf[:, 5, :], func=AF.Exp, scale=lb),
                          r=["lg", "miscf"], w=[("hd", 4)])
                    tr.op("act", lambda e: e.activation(out=hc[:, 0:1], in_=cols[:, 0:1], func=AF.Exp, scale=lf),
                          r=["lg", "cols"], w=["hc"])
                    tr.op("act", lambda e: e.activation(out=hc[:, 1:2], in_=cols[:, 1:2], func=AF.Exp, scale=lb),
                          r=["lg", "cols", "hc"], w=["hc"])
                    tr.op("act", lambda e: e.activation(out=hc[:, 2:3], in_=lf, func=AF.Exp, scale=128.0), r=["lg", "hc"], w=["hc"])
                    tr.op("act", lambda e: e.activation(out=hc[:, 3:4], in_=lb, func=AF.Exp, scale=128.0), r=["lg", "hc"], w=["hc"])
                    tr.op("dve", lambda e: e.tensor_scalar(out=hc[:, 0:2], in0=hc[:, 0:2], scalar1=float(SCALE), scalar2=None,
                                                           op0=ALU.mult), r=["hc"], w=["hc"])
                    kdf, kdb, cdf, cdb = hc[:, 0:1], hc[:, 1:2], hc[:, 2:3], hc[:, 3:4]
                    if h == 0:
                        for blk in range(16):
                            rotary_block(0, blk)
                    for g in range(16):
                        b = next_bank()
                        tbk = bank_bf(b)

                        def f(e, g=g, tbk=tbk):
                            ins = None
                            for q4 in range(4):
                                n = g * 4 + q4
                                ins = e.transpose(out=tbk[:, q4 * 128:(q4 + 1) * 128], in_=kT[:, n * 128:(n + 1) * 128], identity=ident[:])
                            return ins
                        tr.op("pe", f, r=[("kT", hp, g), "ident"], w=[("bank", b)])
                        en_ = tr.alt()
                        src = tbk[:, 0:512].rearrange("p (a d) -> p a d", a=4)
                        if en_ == "act":
                            tr.op("act", lambda e, g=g, src=src: e.activation(out=Kt[:, g * 4:(g + 1) * 4, :], in_=src, func=AF.Copy),
                                  r=[("bank", b)], w=[("Kt", g)])
                        else:
                            tr.op("dve", lambda e, g=g, src=src: e.tensor_copy(out=Kt[:, g * 4:(g + 1) * 4, :], in_=src),
                                  r=[("bank", b)], w=[("Kt", g)])
                    if h + 1 < 12:
                        head_loads(h + 1)
                    tr.op("dve", lambda e: e.tensor_scalar(out=Vs[:].rearrange("p n e -> p (n e)"), in0=V[:].rearrange("p n e -> p (n e)"),
                                                           scalar1=kdb, scalar2=None, op0=ALU.mult), r=[Vk, "hc"], w=["Vs"])
                    tr.op("pool", lambda e: e.memset(Sbk[:, (NSUB - 1) % 2, :], 0.0), w=[("Sbk", (NSUB - 1) % 2)])
                    tr.op("pool", lambda e: e.memset(Sb[:, NSUB - 1, :], 0.0), w=[("Sb", NSUB - 1)])
                    for n in range(NSUB - 2, -1, -1):
                        b = next_bank()
                        pw, pr_ = n % 2, (n + 1) % 2
                        tr.op("pe", lambda e, b=b, n=n: e.matmul(banks[b][:, 0:128], lhsT=Kt[:, n + 1, :], rhs=Vs[:, n + 1, :], start=True, stop=True),
                              r=[("Kt", (n + 1) // 4), "Vs"], w=[("bank", b)])
                        tr.op("dve", lambda e, b=b, pw=pw, pr_=pr_: e.scalar_tensor_tensor(out=Sbk[:, pw, :], in0=Sbk[:, pr_, :], scalar=cdb, in1=banks[b][:, 0:128],
                                                                          op0=ALU.mult, op1=ALU.add), r=[("Sbk", pr_), "hc", ("bank", b)], w=[("Sbk", pw)])
                        if (n + 1) % 16 == 0:
                            tr.op("dve", lambda e, pw=pw: e.tensor_scalar(out=Sbk[:, pw, :], in0=Sbk[:, pw, :], scalar1=carry_col, scalar2=None, op0=ALU.mult),
                                  r=[("Sbk", pw), "cols"], w=[("Sbk", pw)])
                        tr.op("act", lambda e, n=n, pw=pw: e.activation(out=Sb[:, n, :], in_=Sbk[:, pw, :], func=AF.Copy), r=[("Sbk", pw)], w=[("Sb", n)])
                    tr.op("dve", lambda e: e.tensor_scalar(out=Vs[:].rearrange("p n e -> p (n e)"), in0=V[:].rearrange("p n e -> p (n e)"),
                                                           scalar1=kdf, scalar2=None, op0=ALU.mult), r=[Vk, "hc"], w=["Vs"])
                    tr.op("pool", lambda e: e.memset(Sf[:, 1, :], 0.0), w=[("Sf", 1)])
                    tr.op("pool", lambda e: e.memset(Sfb[:, 0, :], 0.0), w=[("Sfb", 0)])
                    st = {}

                    def gate_group(G, h=h):
                        gp = G % 2
                        tr.dma(gt[:, gp, :].rearrange("p (n e) -> p n e", n=8),
                               VG[G * 1024:(G + 1) * 1024, 1536 + h * 128:1536 + (h + 1) * 128].rearrange("(n p) e -> p n e", p=128),
                               w=[("gt", gp)], sem="gt%d" % gp)
                        tr.op("act", lambda e: e.activation(out=sg[:, gp, :], in_=gt[:, gp, :], func=AF.Silu), r=[("gt", gp)], w=[("sg", gp)])

                    gate_group(0)

                    def f0(n, h=h):
                        g = n // 4
                        gi = g % 3
                        if n % 8 == 4 and n // 8 + 1 < 8:
                            gate_group(n // 8 + 1)
                        if n % 4 == 2 and h + 1 < 12:
                            rotary_block(h + 1, n // 4)
                        if n % 4 == 0:
                            gsl = slice(g * 512, (g + 1) * 512)
                            tr.op("pool", lambda e: e.tensor_tensor(out=qfb[:, gi, 0, :].rearrange("p (n i) -> p n i", n=4),
                                                                    in0=qT[:, gsl].rearrange("p (n i) -> p n i", n=4),
                                                                    in1=hd[:, 3:4, :].to_broadcast([128, 4, 128]), op=ALU.mult),
                                  r=[("qT", hp, g), ("hd", 3)], w=[("qfb", gi, 0)])
                            tr.op("pool", lambda e: e.tensor_tensor(out=qfb[:, gi, 1, :].rearrange("p (n i) -> p n i", n=4),
                                                                    in0=qT[:, gsl].rearrange("p (n i) -> p n i", n=4),
                                                                    in1=hd[:, 4:5, :].to_broadcast([128, 4, 128]), op=ALU.mult),
                                  r=[("qT", hp, g), ("hd", 4)], w=[("qfb", gi, 1)])
                        csl = slice(n * 128, (n + 1) * 128)
                        bA = next_bank()
                        bY = next_bank()
                        st[n] = (bA, bY)

                        def f(e):
                            e.matmul(banks[bA][:, 0:128], lhsT=kT[:, csl], rhs=qT[:, csl], start=True, stop=True)
                            return e.matmul(banks[bA][:, 128:256], lhsT=Kt[:, n, :], rhs=Vs[:, n, :], start=True, stop=True)
                        tr.op("pe", f, r=[("kT", hp, g), ("qT", hp, g), ("Kt", g), "Vs"], w=[("bank", bA)])

                    def f1(n):
                        bA, bY = st[n]
                        r_ = n % RB
                        tr.op("dve", lambda e: e.tensor_tensor(out=PTm[:, r_, :], in0=banks[bA][:, 0:128], in1=hd[:, 2, :], op=ALU.mult),
                              r=[("bank", bA), ("hd", 2)], w=[("PTm", r_)])
                        pw, pr_ = n % 2, (n + 1) % 2
                        tr.op("dve", lambda e: e.scalar_tensor_tensor(out=Sf[:, pw, :], in0=Sf[:, pr_, :], scalar=cdf, in1=banks[bA][:, 128:256],
                                                                     op0=ALU.mult, op1=ALU.add), r=[("Sf", pr_), "hc", ("bank", bA)], w=[("Sf", pw)])
                        if (n + 1) % 16 == 0 and n + 1 < NSUB:
                            tr.op("dve", lambda e: e.tensor_scalar(out=Sf[:, pw, :], in0=Sf[:, pw, :], scalar1=carry_col, scalar2=None, op0=ALU.mult),
                                  r=[("Sf", pw), "cols"], w=[("Sf", pw)])
                        if n + 1 < NSUB:
                            rn = (n + 1) % RB
                            tr.op("pool", lambda e: e.tensor_copy(out=Sfb[:, rn, :], in_=Sf[:, pw, :]), r=[("Sf", pw)], w=[("Sfb", rn)])

                    def f2(n):
                        bA, bY = st[n]
                        r_ = n % RB
                        gi = (n // 4) % 3
                        lsl = slice((n % 4) * 128, (n % 4 + 1) * 128)

                        def fy(e):
                            e.matmul(banks[bY][:, 0:128], lhsT=PTm[:, r_, :], rhs=V[:, n, :], start=True, stop=False)
                            e.matmul(banks[bY][:, 0:128], lhsT=qfb[:, gi, 0, lsl], rhs=Sfb[:, r_, :], start=False, stop=False)
                            return e.matmul(banks[bY][:, 0:128], lhsT=qfb[:, gi, 1, lsl], rhs=Sb[:, n, :], start=False, stop=True)
                        tr.op("pe", fy, r=[("PTm", r_), Vk, ("qfb", gi, 0), ("qfb", gi, 1), ("Sfb", r_), ("Sb", n)], w=[("bank", bY)])

                    def f3(n):
                        bA, bY = st[n]
                        r_ = n % RB
                        gi = (n // 8) % 2
                        lsl = slice((n % 8) * 128, (n % 8 + 1) * 128)
                        tr.op("act", lambda e: e.activation(out=junk[:], in_=banks[bY][:, 0:128], func=AF.Square, accum_out=nst[:, r_:r_ + 1]),
                              r=[("bank", bY)], w=["rjunk", ("nst", r_)])
                        rstd_from_ss(nst[:, r_:r_ + 1], nst[:, r_:r_ + 1], 128, [("nst", r_)], [("nst", r_)], 1.0 / 128.0)
                        tr.op("dve", lambda e: e.scalar_tensor_tensor(out=tok[:, r_, :], in0=banks[bY][:, 0:128], scalar=nst[:, r_:r_ + 1],
                                                                     in1=sg[:, gi, lsl], op0=ALU.mult, op1=ALU.mult),
                              r=[("bank", bY), ("nst", r_), ("sg", gi)], w=[("tok", r_)])

                    def f4(n, h=h):
                        bA, bY = st.pop(n)
                        r_ = n % RB
                        g = n // 4
                        ci = g % 4
                        lsl = slice((n % 4) * 128, (n % 4 + 1) * 128)
                        tbt = bank_bf(bY)
                        tr.op("pe", lambda e: e.transpose(out=tbt[:, 512:640], in_=tok[:, r_, :], identity=ident[:]),
                              r=[("tok", r_), "ident"], w=[("bank", bY)])
                        tr.op("dve", lambda e: e.tensor_copy(out=catst[:, ci, lsl], in_=tbt[:, 512:640]),
                              r=[("bank", bY)], w=[("catst", ci)])
                        if n % 4 == 3:
                            tr.dma(CAT[h, :, g * 512:(g + 1) * 512], catst[:, ci, :], r=[("catst", ci)], sem="cs%d" % ci)

                    pipeline(NSUB, [f0, f1, f2, f3, f4])
                tr.barrier(new_sems=True)

        def phase_na():
            with ExitStack() as ps:
                catst = sb("catst", [128, 4, 512], BF16, ps)
                with ExitStack() as ps2:
                    phase_memattn(1, ps2, catst)
                    tr.barrier()
                qT2 = [sb("qT%d" % i, [128, NTOK], BF16, ps) for i in range(2)]
                kT2 = [sb("kT%d" % i, [128, NTOK], BF16, ps) for i in range(2)]
                Va2 = [sb("Va%d" % i, [128, NSUB, 129], BF16, ps) for i in range(2)]
                Et2 = [sb("Et%d" % i, [128, 12, 128], F32, ps) for i in range(2)]
                maskc = sb("maskc", [128, NMASK], F32, ps)
                E1 = sb("E1", [128, RB, 8, 128], F32, ps)
                PT = sb("PT", [128, RB, 8, 128], BF16, ps)
                rc = sb("rcn", [128, RB], F32, ps)
                ot = sb("ot", [128, RB, 128], BF16, ps)
                tr.dma(maskc[:], maskc_in, w=["maskc"], sem="mc")
                for i_ in range(2):
                    tr.op("pool", lambda e, i_=i_: e.memset(Va2[i_][:], 1.0), w=[("Va", i_)])

                def head_loads(h):
                    hp = h % 2
                    tr.dma(qT2[hp][:], QT[1][h], w=[("qT", hp)], sem="lq%d" % hp)
                    tr.dma(kT2[hp][:], QT[1][12 + h], w=[("kT", hp)], sem="lk%d" % hp)
                    for q4 in range(4):
                        tr.dma(Va2[hp][:, q4 * 16:(q4 + 1) * 16, 0:128],
                               VG[q4 * 2048:(q4 + 1) * 2048, h * 128:(h + 1) * 128].rearrange("(n p) e -> p n e", p=128),
                               w=[("Va", hp)], sem="lv%d" % hp)
                    tr.dma(Et2[hp][:], btab_in[h], w=[("Et", hp)], sem="bt%d" % hp)

                head_loads(0)
                for h in range(12):
                    hp = h % 2
                    qT, kT, Va, Et = qT2[hp], kT2[hp], Va2[hp], Et2[hp]
                    qk_, kk_, vk_, ek_ = ("qT", hp), ("kT", hp), ("Va", hp), ("Et", hp)
                    tr.op("act", lambda e: e.activation(out=Et[:], in_=Et[:], func=AF.Exp), r=[ek_], w=[ek_])
                    if h + 1 < 12:
                        head_loads(h + 1)
                    st = {}

                    def jlist(T):
                        s_, t_ = T // 16, T % 16
                        interior = 2 <= t_ <= 13
                        return (list(range(-2, 3)) if interior else _BPLAN[(s_, t_)]), interior

                    def g0(T):
                        js, interior = jlist(T)
                        qsl = slice(T * 128, (T + 1) * 128)
                        b1 = next_bank()
                        b2 = next_bank()
                        st[T] = (b1, b2)
                        g1_, g2_ = js[0:4], js[4:8]

                        def f(e):
                            ins = None
                            for gi_, j in enumerate(g1_):
                                T2 = T + j
                                ins = e.matmul(banks[b1][:, gi_ * 128:(gi_ + 1) * 128], lhsT=kT[:, T2 * 128:(T2 + 1) * 128],
                                               rhs=qT[:, qsl], start=True, stop=True)
                            return ins
                        tr.op("pe", f, r=[kk_, qk_], w=[("bank", b1)])

                        def f2_(e):
                            ins = None
                            for gi_, j in enumerate(g2_):
                                T2 = T + j
                                ins = e.matmul(banks[b2][:, gi_ * 128:(gi_ + 1) * 128], lhsT=kT[:, T2 * 128:(T2 + 1) * 128],
                                               rhs=qT[:, qsl], start=True, stop=True)
                            return ins
                        if g2_:
                            tr.op("pe", f2_, r=[kk_, qk_], w=[("bank", b2)])

                    def g1(T):
                        js, interior = jlist(T)
                        b1, b2 = st[T]
                        r_ = T % RB
                        n1 = len(js[0:4])
                        n2 = len(js[4:8])
                        tr.op("act", lambda e: e.activation(out=E1[:, r_, 0:n1, :], in_=banks[b1][:, 0:n1 * 128].rearrange("p (a q) -> p a q", a=n1),
                                                            func=AF.Exp, scale=SCALE), r=[("bank", b1)], w=[("E1", r_)])
                        if n2:
                            tr.op("act", lambda e: e.activation(out=E1[:, r_, 4:4 + n2, :], in_=banks[b2][:, 0:n2 * 128].rearrange("p (a q) -> p a q", a=n2),
                                                                func=AF.Exp, scale=SCALE), r=[("bank", b2)], w=[("E1", r_)])

                    def g2(T):
                        js, interior = jlist(T)
                        s_, t_ = T // 16, T % 16
                        r_ = T % RB
                        if interior:
                            en_ = "pool" if T % 3 == 0 else "dve"
                            tr.op(en_, lambda e: e.tensor_tensor(out=PT[:, r_, 0:5, :], in0=E1[:, r_, 0:5, :], in1=Et[:, 0:5, :], op=ALU.mult),
                                  r=[("E1", r_), ek_], w=[("PT", r_)])
                        else:
                            for ji, j in enumerate(js):
                                for b_ in (0, 1):
                                    mi = _MASKIDX[(s_, t_, j, b_)]
                                    hs_ = slice(b_ * 64, (b_ + 1) * 64)
                                    tr.op("dve", lambda e, ji=ji, j=j, mi=mi, hs_=hs_: e.scalar_tensor_tensor(
                                        out=PT[:, r_, ji, hs_], in0=E1[:, r_, ji, hs_], scalar=maskc[:, mi:mi + 1],
                                        in1=Et[:, 5 + j + 3, hs_], op0=ALU.mult, op1=ALU.mult),
                                        r=[("E1", r_), ek_, "maskc"], w=[("PT", r_)])

                    def g3(T):
                        js, interior = jlist(T)
                        b1, b2 = st[T]
                        r_ = T % RB

                        def fo(e):
                            ins = None
                            for ji, j in enumerate(js):
                                ins = e.matmul(banks[b2][:, 256:385], lhsT=PT[:, r_, ji, :], rhs=Va[:, T + j, :],
                                               start=(ji == 0), stop=(ji == len(js) - 1))
                            return ins
                        tr.op("pe", fo, r=[("PT", r_), vk_], w=[("bank", b2)])

                    def g4(T):
                        b1, b2 = st[T]
                        r_ = T % RB
                        tr.op("dve", lambda e: e.reciprocal(out=rc[:, r_:r_ + 1], in_=banks[b2][:, 384:385]),
                              r=[("bank", b2)], w=[("rcn", r_)])
                        tr.op("act", lambda e: e.activation(out=ot[:, r_, :], in_=banks[b2][:, 256:384], func=AF.Copy, scale=rc[:, r_:r_ + 1]),
                              r=[("bank", b2), ("rcn", r_)], w=[("ot", r_)])

                    def g5(T, h=h):
                        b1, b2 = st.pop(T)
                        r_ = T % RB
                        tbt = bank_bf(b2)
                        tr.op("pe", lambda e: e.transpose(out=tbt[:, 800:928], in_=ot[:, r_, :], identity=ident[:]),
                              r=[("ot", r_), "ident"], w=[("bank", b2)])
                        g = T // 4
                        ci = g % 4
                        lsl = slice((T % 4) * 128, (T % 4 + 1) * 128)
                        tr.op("dve", lambda e: e.tensor_copy(out=catst[:, ci, lsl], in_=tbt[:, 800:928]),
                              r=[("bank", b2)], w=[("catst", ci)])
                        if T % 4 == 3:
                            tr.dma(CAT[h, :, g * 512:(g + 1) * 512], catst[:, ci, :], r=[("catst", ci)], sem="cs%d" % ci)

                    pipeline(NSUB, [g0, g1, g2, g3, g4, g5])
                tr.barrier(new_sems=True)

        phases = os.environ.get("MK_PHASES", "all")
        phase_c(0, a0=True)
        if phases != "a0":
            with ExitStack() as outer:
                catst0 = sb("catst", [128, 4, 512], BF16, outer)
                with ExitStack() as lb:
                    alloc_mem(lb)
                    phase_mem(0)
                    with ExitStack() as ps2:
                        phase_memattn(0, ps2, catst0)
                        tr.barrier()
                phase_ret(catst0)
            if phases != "b0":
                phase_c(0)
                if phases != "c0":
                    with ExitStack() as lb:
                        alloc_mem(lb)
                        phase_mem(1)
                        phase_na()
                    if phases != "b1":
                        phase_c(1)
        tr.barrier()
    return nc


_NC_CACHE = {}


def _prep_inputs(x_prompt, x_sample, mem_prompt, mem_sample, norm_gain, mem_norm_gain, w_mem_kv, w_out,
                 w_mlp_in, w_mlp_out, w_in_ret, ret_decay, w_in_na, na_rpb):
    f = lambda a: np.ascontiguousarray(np.asarray(a, dtype=np.float32))
    shared = {
        "norm_gain": f(norm_gain), "mem_norm_gain": f(mem_norm_gain), "w_mem_kv": f(w_mem_kv), "w_out": f(w_out),
        "w_mlp_in": f(w_mlp_in), "w_mlp_out": f(w_mlp_out), "w_in_ret": f(w_in_ret)[0], "w_in_na": f(w_in_na)[0],
        "ret_decay": f(ret_decay).reshape(1, 24), "btab": _na_bias_tables(f(na_rpb)[0]),
    }
    consts = {False: _const_tables(False), True: _const_tables(True)}
    xp = f(x_prompt)
    xs = f(x_sample)
    mp = f(mem_prompt)
    ms = f(mem_sample)
    in_maps = []
    for c in range(8):
        samp = c >= 4
        m = dict(shared)
        if not samp:
            m["x"] = xp[4 * c:4 * c + 4].reshape(NTOK, D)
            m["mem"] = mp[4 * c:4 * c + 4]
        else:
            m["x"] = xs[c - 4]
            m["mem"] = np.ascontiguousarray(np.broadcast_to(ms[c - 4][None], (4, 256, D)))
        m.update(consts[samp])
        in_maps.append(m)
    return in_maps


def kernel(x_prompt, x_sample, mem_prompt, mem_sample, norm_gain, mem_norm_gain, w_mem_kv, w_out,
           w_mlp_in, w_mlp_out, w_in_ret, ret_decay, w_in_na, na_rpb):
    in_maps = _prep_inputs(x_prompt, x_sample, mem_prompt, mem_sample, norm_gain, mem_norm_gain, w_mem_kv, w_out,
                           w_mlp_in, w_mlp_out, w_in_ret, ret_decay, w_in_na, na_rpb)
    if "nc" not in _NC_CACHE:
        _NC_CACHE["nc"] = build_program()
    res = run_bass_kernel_spmd(_NC_CACHE["nc"], in_maps, core_ids=list(range(8)))
    ys = [np.asarray(r["y"], dtype=np.float32) for r in res.results]
    y_prompt = np.concatenate([y.reshape(4, 2048, D) for y in ys[:4]], axis=0)
    y_sample = np.stack([y.reshape(8192, D) for y in ys[4:]], axis=0)
    return (y_prompt, y_sample)
```

```python
import os
import numpy as np
from contextlib import ExitStack
import concourse.bass as bass
import concourse.mybir as mybir
from concourse.bass_utils import run_bass_kernel_spmd

F32 = mybir.dt.float32
BF16 = mybir.dt.bfloat16
AF = mybir.ActivationFunctionType
ALU = mybir.AluOpType

D = 2048
NTOK = 8192
NSUB = 64
NTILE = 16
DFF = 8192
EPS = 1e-6
SCALE = 128.0 ** -0.5
W_RET = 6656
W_NA = 5120


class TR:
    def __init__(self, nc, es):
        self.nc = nc
        self.es = es
        self.eng = {"pe": nc.tensor, "act": nc.scalar, "dve": nc.vector, "pool": nc.gpsimd, "sp": nc.sync}
        self.sems = {}
        self.cnt = {}
        self.epoch = 0
        self.esem = {}
        self.waited = {e: {} for e in self.eng}
        self.lastw = {}
        self.reads = {}
        self.flip = 0
        self._new_engine_sems()

    def _sem(self, key):
        if key not in self.sems:
            self.sems[key] = self.es.enter_context(self.nc.semaphore("s_%s" % str(key).replace(" ", "")))
            self.cnt[key] = 0
        return self.sems[key]

    def _new_engine_sems(self):
        for e in ("pe", "act", "dve", "pool"):
            k = ("E", e, self.epoch)
            self._sem(k)
            self.esem[e] = k

    def _wait(self, e, tok):
        k, v = tok
        if k == self.esem.get(e) and e == "pe":
            return
        if self.waited[e].get(k, 0) >= v:
            return
        self.eng[e].wait_ge(self.sems[k], v)
        self.waited[e][k] = v

    def _deps(self, e, r, w):
        toks = []
        for x in r:
            t = self.lastw.get(x)
            if t is not None:
                toks.append(t)
        for x in w:
            t = self.lastw.get(x)
            if t is not None:
                toks.append(t)
            toks.extend(self.reads.get(x, ()))
        for t in toks:
            self._wait(e, t)

    def _commit(self, tok, r, w):
        for x in r:
            if isinstance(x, tuple) and x and x[0] == "const":
                continue
            self.reads.setdefault(x, []).append(tok)
        for x in w:
            self.lastw[x] = tok
            self.reads[x] = []

    def op(self, e, fn, r=(), w=()):
        self._deps(e, r, w)
        ins = fn(self.eng[e])
        k = self.esem[e]
        ins.then_inc(self.sems[k], 1)
        self.cnt[k] += 1
        tok = (k, self.cnt[k])
        self._commit(tok, r, w)
        return tok

    def dma(self, out, in_, r=(), w=(), sem="d", q="sp", **kw):
        self._deps(q, r, w)
        k = ("D", sem)
        s = self._sem(k)
        ins = self.eng[q].dma_start(out=out, in_=in_, **kw)
        ins.then_inc(s, 16)
        self.cnt[k] += 16
        tok = (k, self.cnt[k])
        self._commit(tok, r, w)
        return tok

    def barrier(self, new_sems=False):
        for e in self.eng:
            for k, s in self.sems.items():
                if self.cnt[k] > 0 and self.waited[e].get(k, 0) < self.cnt[k]:
                    if k == self.esem.get(e) and e == "pe":
                        continue
                    self.eng[e].wait_ge(s, self.cnt[k])
                    self.waited[e][k] = self.cnt[k]
        self.lastw = {}
        self.reads = {}
        if new_sems:
            self.epoch += 1
            self._new_engine_sems()

    def alt(self):
        self.flip ^= 1
        return "act" if self.flip else "dve"


def _row_window(core_is_sample, s, r_local):
    if core_is_sample:
        R = 32 * s + r_local
        return int(np.clip(R - 4, 0, 120))
    return 32 * s + int(np.clip(r_local - 4, 0, 24))


def _boundary_plan():
    plan = {}
    for s in range(4):
        for t in (0, 1, 14, 15):
            js = []
            for j in range(-3, 4):
                T2 = 16 * s + t + j
                if T2 < 0 or T2 > 63:
                    continue
                ok = False
                for samp in (False, True):
                    for b in (0, 1):
                        rs = _row_window(samp, s, 2 * t + b)
                        for a in (0, 1):
                            kr = 2 * T2 + a
                            if rs <= kr < rs + 8:
                                ok = True
                if ok:
                    js.append(j)
            plan[(s, t)] = js
    return plan


_BPLAN = _boundary_plan()
_MASKIDX = {}
for (_s, _t), _js in sorted(_BPLAN.items()):
    for _j in _js:
        for _b in (0, 1):
            _MASKIDX[(_s, _t, _j, _b)] = len(_MASKIDX)
NMASK = len(_MASKIDX)


def _mask_table(core_is_sample):
    m = np.zeros((128, NMASK), np.float32)
    a = np.arange(128) // 64
    for (s, t, j, b), idx in _MASKIDX.items():
        rs = _row_window(core_is_sample, s, 2 * t + b)
        kr = 2 * (16 * s + t + j) + a
        m[:, idx] = ((kr >= rs) & (kr < rs + 8)).astype(np.float32)
    return m


def _na_bias_tables(rpb):
    NEG = np.float32(-30000.0)
    kp = np.arange(128)
    a = (kp // 64)[:, None]
    kc = (kp % 64)[:, None]
    qp = np.arange(128)
    b = (qp // 64)[None, :]
    qc = (qp % 64)[None, :]
    ws = np.clip(qc - 8, 0, 48)
    colvalid = (kc >= ws) & (kc < ws + 16)
    dc = np.clip(kc - qc + 15, 0, 30)
    out = np.empty((12, 128, 12, 128), np.float32)
    tabs = [(j, True) for j in range(-2, 3)] + [(j, False) for j in range(-3, 4)]
    for ti, (j, interior) in enumerate(tabs):
        dr = 2 * j + a - b
        if interior:
            rowvalid = (dr >= -4) & (dr <= 3)
        else:
            rowvalid = (dr >= -7) & (dr <= 7)
        valid = colvalid & rowvalid
        dri = np.clip(dr + 7, 0, 14)
        for h in range(12):
            g = rpb[h][dri, dc]
            out[h, :, ti, :] = np.where(valid, g, NEG)
    return out


def _const_tables(core_is_sample):
    c = {}
    t = np.arange(NTOK)
    pos = (t if core_is_sample else (t % 2048)).astype(np.float32)
    half = 64
    inv_freq = (np.float32(10000.0) ** (-np.arange(half, dtype=np.float32) / np.float32(half))).astype(np.float32)
    ang = (pos[None, :] * inv_freq[:, None]).astype(np.float32)
    cos = np.cos(ang).astype(np.float32)
    sin = np.sin(ang).astype(np.float32)
    c["rotc"] = np.ascontiguousarray(np.concatenate([cos, cos], 0))
    c["rots"] = np.ascontiguousarray(np.concatenate([-sin, sin], 0))
    j = np.arange(128, dtype=np.float32)[:, None]
    i = np.arange(128, dtype=np.float32)[None, :]
    misc = np.zeros((128, 8, 128), np.float32)
    misc[:, 0, :] = np.maximum(i - j, 0.0)
    misc[:, 1, :] = np.maximum(j - i, 0.0)
    misc[:, 2, :] = (i >= j).astype(np.float32) * np.float32(SCALE)
    misc[:, 3, :] = (j > i).astype(np.float32) * np.float32(SCALE)
    misc[:, 4, :] = i + 1.0
    misc[:, 5, :] = 128.0 - i
    misc[:, 6, :] = np.eye(128, dtype=np.float32)
    perm = np.zeros((128, 128), np.float32)
    for m in range(128):
        perm[(m + 64) % 128, m] = 1.0
    misc[:, 7, :] = perm
    c["misc"] = misc
    cols = np.zeros((128, 8), np.float32)
    cols[:, 0] = 127.0 - np.arange(128)
    cols[:, 1] = np.arange(128)
    cols[:, 2] = 1.0 if core_is_sample else 0.0
    cols[:, 3] = EPS
    c["cols"] = cols
    c["maskc"] = _mask_table(core_is_sample)
    return c


def build_program(debug=False):
    nc = bass.Bass("TRN2", target_bir_lowering=False)

    def din(name, shape, dt=F32):
        return nc.dram_tensor(name, list(shape), dt, kind="ExternalInput").ap()

    dump = set(os.environ.get("MK_DUMP", "").split(","))

    def dscr(name, shape, dt):
        if debug and name in dump:
            return nc.dram_tensor(name, list(shape), dt, kind="ExternalOutput").ap()
        return nc.dram_tensor(name, list(shape), dt).ap()

    x_in = din("x", [NTOK, D])
    mem_in = din("mem", [4, 256, D])
    ng_in = din("norm_gain", [2, 4, D])
    mg_in = din("mem_norm_gain", [2, D])
    w_memkv = din("w_mem_kv", [2, D, 1024])
    w_out = din("w_out", [2, D, D])
    w_mi = din("w_mlp_in", [2, D, DFF])
    w_mo = din("w_mlp_out", [2, DFF, D])
    w_ret = din("w_in_ret", [D, W_RET])
    w_na = din("w_in_na", [D, W_NA])
    decay_in = din("ret_decay", [1, 24])
    btab_in = din("btab", [12, 128, 12, 128])
    rotc_in = din("rotc", [128, NTOK])
    rots_in = din("rots", [128, NTOK])
    misc_in = din("misc", [128, 8, 128])
    cols_in = din("cols", [128, 8])
    maskc_in = din("maskc", [128, NMASK])
    y_out = nc.dram_tensor("y", [NTOK, D], F32, kind="ExternalOutput").ap()

    wb_memkv = dscr("wb_memkv", [2, D, 1024], BF16)
    wb_out = dscr("wb_out", [2, D, D], BF16)
    wb_mi = dscr("wb_mi", [2, D, DFF], BF16)
    wb_mo = dscr("wb_mo", [2, DFF, D], BF16)
    wb_ret = dscr("wb_ret", [D, W_RET], BF16)
    wb_na = dscr("wb_na", [D, W_NA], BF16)
    QT = [dscr("qt%d" % l, [28, 128, NTOK], BF16) for l in range(2)]
    VG = dscr("vg", [NTOK, 3072], BF16)
    CAT = dscr("cat", [16, 128, NTOK], BF16)
    X1 = dscr("x1", [NTOK, D], F32)

    es = ExitStack()
    with es:
        tr = TR(nc, es)

        sbn = [0]

        def sb(name, shape, dt, stack=None):
            sbn[0] += 1
            return (stack or es).enter_context(nc.sbuf_tensor("sb%d_%s" % (sbn[0], name), list(shape), dt))

        banks = [es.enter_context(nc.psum_tensor("ps%d" % i, [128, 512], F32)) for i in range(8)]
        bank_rr = [0]

        def next_bank():
            b = bank_rr[0]
            bank_rr[0] = (b + 1) % 8
            return b

        def next_bank4():
            b = bank_rr[0]
            if b % 4 != 0:
                b = (b + 3) // 4 * 4 % 8
            bank_rr[0] = (b + 4) % 8
            return [b, b + 1, b + 2, b + 3]

        def bank_bf(b):
            return banks[b][:].bitcast(BF16)

        cols = sb("cols", [128, 8], F32)
        ident = sb("ident", [128, 128], BF16)
        perm = sb("perm", [128, 128], BF16)
        with ExitStack() as ps0:
            miscf0 = sb("miscf0", [128, 2, 128], F32, ps0)
            tr.dma(miscf0[:], misc_in[:, 6:8, :], w=["miscf0"], sem="c0")
            tr.dma(cols[:], cols_in, w=["cols"], sem="c1")
            tr.op("dve", lambda e: e.tensor_copy(out=ident[:], in_=miscf0[:, 0, :]), r=["miscf0"], w=["ident"])
            tr.op("dve", lambda e: e.tensor_copy(out=perm[:], in_=miscf0[:, 1, :]), r=["miscf0"], w=["perm"])
            tr.barrier()
        eps_col = cols[:, 3:4]
        carry_col = cols[:, 2:3]

        def cast(dst, src, key):
            n = 1
            for d_ in src.shape:
                n *= d_
            fl = "a b -> (a b)"
            s_ = src.rearrange(fl).rearrange("(p a n) -> p a n", p=128, n=2048)
            d_ = dst.rearrange(fl).rearrange("(p a n) -> p a n", p=128, n=2048)
            tr.dma(d_, s_, w=[("const", key)], sem="cast_" + key, q="pool")

        def emit_casts():
            cast(wb_memkv[0], w_memkv[0], "memkv0")
            cast(wb_out[0], w_out[0], "out0")
            cast(wb_mi[0], w_mi[0], "mi0")
            cast(wb_mo[0], w_mo[0], "mo0")
            cast(wb_na, w_na, "na")
            cast(wb_memkv[1], w_memkv[1], "memkv1")
            cast(wb_out[1], w_out[1], "out1")
            cast(wb_mi[1], w_mi[1], "mi1")
            cast(wb_mo[1], w_mo[1], "mo1")

        NSLOT = 4
        wslots = []
        wrr = [0]

        def load_wblock(wb, key, row0, col0):
            s = wrr[0]
            wrr[0] = (s + 1) % NSLOT
            src = wb[row0:row0 + 1024, col0:col0 + 512].rearrange("(kc p) n -> p kc n", p=128)
            ck = ("const", "ret", col0 // 512, row0 // 1024) if key == "ret" else ("const", key)
            tr.dma(wslots[s][:], src, r=[ck], w=[("w", s)], sem="w%d" % s)
            return s

        class WStream:
            def __init__(self, items, ahead=NSLOT - 1):
                self.items = items
                self.pos = 0
                self.slots = []
                self.ahead = ahead

            def extend(self, items):
                self.items = self.items + list(items)

            def fill(self):
                while self.pos < len(self.items) and len(self.slots) < self.ahead:
                    self.slots.append(load_wblock(*self.items[self.pos]))
                    self.pos += 1

            def get(self):
                self.fill()
                s = self.slots.pop(0)
                return s

        def unit_items(wb, key, row0, nkc, col0):
            return [(wb, key, row0 + hb * 1024, col0) for hb in range(nkc // 8)]

        def unit_fm(ws, nkc, actT, evac):
            bs = next_bank4()
            nhb = nkc // 8
            slots = []
            for hb in range(nhb):
                s = ws.get()
                slots.append(s)

                def f(e, hb=hb, s=s):
                    ins = None
                    for c in range(4):
                        for k8 in range(8):
                            kc = hb * 8 + k8
                            ins = e.matmul(banks[bs[c]][:], lhsT=wslots[s][:, k8, c * 128:(c + 1) * 128],
                                           rhs=actT[:, kc, :], start=(kc == 0), stop=(kc == nkc - 1))
                    return ins
                tr.op("pe", f, r=[("w", s), "actT"], w=[("bank", b) for b in bs])
                ws.fill()
            for c in range(4):
                evac(c, bs[c])

        def unit_tm(ws, nkc, actT, evac, kc_off=0, akey="actT"):
            bs = next_bank4()
            nhb = nkc // 8
            for hb in range(nhb):
                s = ws.get()

                def f(e, hb=hb, s=s):
                    ins = None
                    for k8 in range(8):
                        kc = hb * 8 + k8
                        for su in range(4):
                            ins = e.matmul(banks[bs[su]][:], lhsT=actT[:, kc_off + kc, su * 128:(su + 1) * 128],
                                           rhs=wslots[s][:, k8, :], start=(kc == 0), stop=(kc == nkc - 1))
                    return ins
                tr.op("pe", f, r=[("w", s)] + (list(akey) if isinstance(akey, (list, tuple)) and akey and isinstance(akey[0], tuple) else [akey]), w=[("bank", b) for b in bs])
                ws.fill()
            for su in range(4):
                evac(su, bs[su])

        def evac_copy(dst_ap, b, wkeys):
            e = tr.alt()
            if e == "act":
                tr.op("act", lambda en: en.activation(out=dst_ap, in_=banks[b][:], func=AF.Copy),
                      r=[("bank", b)], w=wkeys)
            else:
                tr.op("dve", lambda en: en.tensor_copy(out=dst_ap, in_=banks[b][:]), r=[("bank", b)], w=wkeys)

        def rstd_from_ss(ss_ap, rs_ap, n, keys_r, keys_w, inv_n):
            tr.op("act", lambda e: e.activation(out=rs_ap, in_=ss_ap, func=AF.Sqrt, scale=inv_n, bias=eps_col[0:n] if n < 128 else eps_col),
                  r=keys_r + ["cols"], w=keys_w)
            tr.op("dve", lambda e: e.reciprocal(out=rs_ap, in_=rs_ap), r=keys_w, w=keys_w)

        def transposes_to_actT(h_ap, hkey, actT, su):
            for g in range(4):
                b = next_bank()
                tb = bank_bf(b)

                def f(e, g=g, tb=tb):
                    ins = None
                    for q4 in range(4):
                        kc = g * 4 + q4
                        ins = e.transpose(out=tb[:, q4 * 128:(q4 + 1) * 128], in_=h_ap[:, kc * 128:(kc + 1) * 128],
                                          identity=ident[:])
                    return ins
                tr.op("pe", f, r=[hkey, "ident"], w=[("bank", b)])
                src = tb[:, 0:512].rearrange("p (a t) -> p a t", a=4)
                dst = actT[:, g * 4:(g + 1) * 4, su * 128:(su + 1) * 128]
                e_ = tr.alt()
                if e_ == "act":
                    tr.op("act", lambda en, src=src, dst=dst: en.activation(out=dst, in_=src, func=AF.Copy),
                          r=[("bank", b)], w=["actT"])
                else:
                    tr.op("dve", lambda en, src=src, dst=dst: en.tensor_copy(out=dst, in_=src),
                          r=[("bank", b)], w=["actT"])

        memh = {}

        def alloc_mem(stack):
            memh["K"] = sb("memKT", [128, 4, 4, 256], BF16, stack)
            memh["V"] = sb("memV", [128, 4, 2, 4, 129], BF16, stack)
            tr.op("pool", lambda e: e.memset(memh["V"][:], 1.0), w=["memV"])

        def phase_mem(layer):
            with ExitStack() as ps:
                wkv = sb("wkv", [128, 16, 1024], BF16, ps)
                mt = sb("mt", [128, 2, D], F32, ps)
                mh = sb("mh", [128, D], BF16, ps)
                mT = sb("mT", [128, 16, 256], BF16, ps)
                gm = sb("gm", [128, D], F32, ps)
                junk = sb("mjunk", [128, D], BF16, ps)
                ssm = sb("ssm", [128, 2], F32, ps)
                key = "memkv%d" % layer
                tr.dma(wkv[:], wb_memkv[layer].rearrange("(kc p) n -> p kc n", p=128), r=[("const", key)], w=["wkv"], sem="mk")
                tr.dma(gm[:], mg_in[layer:layer + 1, :].broadcast_to([128, D]), w=["gm"], sem="mk3")
                for seg in range(4):
                    tr.dma(mt[:], mem_in[seg].rearrange("(c p) d -> p c d", p=128), w=["mt"], sem="mk2")
                    for c in range(2):
                        tr.op("act", lambda e, c=c: e.activation(out=junk[:], in_=mt[:, c, :], func=AF.Square,
                                                                 accum_out=ssm[:, c:c + 1]), r=["mt"], w=["mjunk", ("ssm", c)])
                    rstd_from_ss(ssm[:], ssm[:], 128, [("ssm", 0), ("ssm", 1)], [("ssm", 0), ("ssm", 1)], 1.0 / D)
                    for c in range(2):
                        tr.op("dve", lambda e, c=c: e.scalar_tensor_tensor(out=mh[:], in0=mt[:, c, :], scalar=ssm[:, c:c + 1],
                                                                          in1=gm[:], op0=ALU.mult, op1=ALU.mult),
                              r=["mt", ("ssm", 0), ("ssm", 1), "gm"], w=["mh"])
                        for g in range(4):
                            b = next_bank()
                            tb = bank_bf(b)

                            def f(e, g=g, tb=tb):
                                ins = None
                                for q4 in range(4):
                                    kc = g * 4 + q4
                                    ins = e.transpose(out=tb[:, q4 * 128:(q4 + 1) * 128], in_=mh[:, kc * 128:(kc + 1) * 128],
                                                      identity=ident[:])
                                return ins
                            tr.op("pe", f, r=["mh", "ident"], w=[("bank", b)])
                            tr.op("act", lambda en, tb=tb, g=g, c=c: en.activation(
                                out=mT[:, g * 4:(g + 1) * 4, c * 128:(c + 1) * 128],
                                in_=tb[:, 0:512].rearrange("p (a t) -> p a t", a=4), func=AF.Copy),
                                r=[("bank", b)], w=["mT"])
                    for h in range(4):
                        b = next_bank()

                        def f(e, h=h, b=b):
                            ins = None
                            for kc in range(16):
                                ins = e.matmul(banks[b][:, 0:256], lhsT=wkv[:, kc, h * 128:(h + 1) * 128], rhs=mT[:, kc, :],
                                               start=(kc == 0), stop=(kc == 15))
                            return ins
                        tr.op("pe", f, r=["wkv", "mT"], w=[("bank", b)])
                        tr.op("act", lambda en, h=h, b=b, seg=seg: en.activation(out=memh["K"][:, seg, h, :], in_=banks[b][:, 0:256],
                                                                               func=AF.Copy), r=[("bank", b)], w=["memKT"])
                    for c in range(2):
                        b = next_bank()

                        def f(e, c=c, b=b):
                            ins = None
                            for kc in range(16):
                                ins = e.matmul(banks[b][:], lhsT=mT[:, kc, c * 128:(c + 1) * 128], rhs=wkv[:, kc, 512:1024],
                                               start=(kc == 0), stop=(kc == 15))
                            return ins
                        tr.op("pe", f, r=["wkv", "mT"], w=[("bank", b)])
                        tr.op("dve", lambda en, c=c, b=b, seg=seg: en.tensor_copy(
                            out=memh["V"][:, seg, c, :, 0:128], in_=banks[b][:].rearrange("p (h e) -> p h e", h=4)),
                            r=[("bank", b)], w=["memV"])
                tr.barrier()

        def inproj_items(layer):
            if layer == 0:
                wb, key, nblk = wb_ret, "ret", 13
            else:
                wb, key, nblk = wb_na, "na", 10
            items = []
            for b_ in range(nblk):
                items += unit_items(wb, key, 0, 16, b_ * 512)
            return items

        def inproj(layer, tile_i, actT, stage_views, hook=None, ws=None):
            tok0 = tile_i * 512
            if layer == 0:
                wb, key, nblk = wb_ret, "ret", 13
                kinds = ["fm"] * 6 + ["tm"] * 6 + ["fm"]
            else:
                wb, key, nblk = wb_na, "na", 10
                kinds = ["fm"] * 6 + ["tm"] * 3 + ["fm"]
            items = []
            for b_ in range(nblk):
                items += unit_items(wb, key, 0, 16, b_ * 512)
            if ws is None:
                ws = WStream(items)
            ws.fill()
            fm_i = 0
            tm_i = 0
            for b_ in range(nblk):
                if hook is not None and b_ in hook:
                    hook[b_]()
                if kinds[b_] == "fm":
                    sv, skey = stage_views[fm_i % 2]
                    fm_i += 1
                    if layer == 0:
                        chunk0 = 4 * b_ if b_ < 6 else 24
                    else:
                        chunk0 = 4 * b_ if b_ < 6 else 24

                    def ev(c, bank, sv=sv, skey=skey):
                        evac_copy(sv[:, c, :], bank, [skey])
                    unit_fm(ws, 16, actT, ev)
                    dst = QT[layer][chunk0:chunk0 + 4, :, tok0:tok0 + 512].rearrange("c p t -> p c t")
                    tr.dma(dst, sv, r=[skey], sem="st_" + str(skey))
                else:
                    sv, skey = stage_views[2 + tm_i % 2]
                    tm_i += 1
                    col0 = (b_ - 6) * 512

                    def ev(su, bank, sv=sv, skey=skey):
                        evac_copy(sv[:, su, :], bank, [skey])
                    unit_tm(ws, 16, actT, ev)
                    dst = VG[tok0:tok0 + 512, col0:col0 + 512].rearrange("(s p) c -> p s c", p=128)
                    tr.dma(dst, sv, r=[skey], sem="st_" + str(skey))

        def phase_c(layer, a0=False):
            with ExitStack() as ps:
                del wslots[:]
                wslots.extend(sb("wslot%d" % i, [128, 8, 512], BF16, ps) for i in range(NSLOT))
                xt = sb("xt", [128, 4, D], F32, ps)
                xp = sb("xp", [128, 2, D], F32, ps)
                actT = sb("actT", [128, 16, 512], BF16, ps)
                hs = sb("hs", [128, 4, D], BF16, ps)
                catT = hs[:].rearrange("p s d -> p (s d)").rearrange("p (c t) -> p c t", c=16)
                HSK = [("hs", su) for su in range(4)]

                def X(ti, su):
                    if su < 2 and ti % 2 == 1:
                        return xp[:, su, :], ("xp", su)
                    return xt[:, su, :], ("xt", su)
                ga = sb("ga", [128, D], F32, ps)
                gb = ga if a0 else sb("gb", [128, D], F32, ps)
                gc = ga if a0 else sb("gc", [128, D], F32, ps)
                tmp2 = sb("tmp2", [128, 2, 512], F32, ps)
                junk = tmp2[:].rearrange("p a n -> p (a n)").bitcast(BF16)
                ss = sb("ss", [128, 16], F32, ps)
                ssp = sb("ssp", [128, 2, 4, 4], F32, ps)
                o = sb("o", [128, 4, D], F32, ps)
                stage_views = []
                for k_ in range(4):
                    v = o[:, k_, :].bitcast(BF16)[:, 0:2048].rearrange("p (a n) -> p a n", a=4)
                    stage_views.append((v, ("o", k_)))
                uT = sb("uT", [128, 32, 512] if not a0 else [128, 1, 512], BF16, ps)
                mixed = None if a0 else uT[:].rearrange("p a n -> p (a n)").bitcast(F32).rearrange("p (s d) -> p s d", s=4)
                xsrc = x_in if layer == 0 else X1

                def load_gain(buf, bkey, l_, gi):
                    tr.dma(buf[:], ng_in[l_, gi:gi + 1, :].broadcast_to([128, D]), w=[bkey], sem="g_" + bkey)

                def load_x(ti, sus=(0, 1, 2, 3)):
                    tok0 = ti * 512
                    for su in sus:
                        xa, xk = X(ti, su)
                        tr.dma(xa, xsrc[tok0 + su * 128:tok0 + (su + 1) * 128, :], w=[xk], sem="x%d" % su)

                def load_cat(ti):
                    tok0 = ti * 512
                    tr.dma(catT, CAT[:, :, tok0:tok0 + 512].rearrange("c p t -> p c t"), w=HSK, sem="cat")

                def sumsq(src_ap, rkeys, col):
                    tr.op("act", lambda e: e.activation(out=junk, in_=src_ap, func=AF.Square, accum_out=ss[:, col:col + 1]),
                          r=rkeys, w=[("tmp", 0), ("tmp", 1), ("ss", col)])

                def evac_block(dst, skey_fn, gbuf, gkey, pi):
                    def mk(cb, add_to=None):
                        def ev(su, bank):
                            d = dst[:, su, cb * 512:(cb + 1) * 512]
                            if add_to is None:
                                tr.op("act", lambda en: en.activation(out=d, in_=banks[bank][:], func=AF.Copy),
                                      r=[("bank", bank)], w=[skey_fn(su)])
                            else:
                                tr.op("dve", lambda en: en.tensor_tensor(out=d, in0=banks[bank][:], in1=d, op=ALU.add),
                                      r=[("bank", bank), skey_fn(su)], w=[skey_fn(su)])
                        return ev
                    return mk

                def finish_block(dst, skey_fn, gbuf, gkey, pi, cb):
                    for su in range(4):
                        d = dst[:, su, cb * 512:(cb + 1) * 512]
                        tpi = (cb * 4 + su) % 2
                        tr.op("act", lambda en, d=d, su=su, tpi=tpi: en.activation(out=tmp2[:, tpi, :], in_=d, func=AF.Square,
                                                                                 accum_out=ssp[:, pi, su, cb:cb + 1]),
                              r=[skey_fn(su)], w=[("tmp", tpi), ("ssp", pi, su, cb)])
                        tr.op("pool", lambda en, d=d: en.tensor_tensor(out=d, in0=d, in1=gbuf[:, cb * 512:(cb + 1) * 512], op=ALU.mult),
                              r=[skey_fn(su), gkey, ("ssp", pi, su, cb)], w=[skey_fn(su)])

                def residual_from(dst, skey_fn, pi, col0, ti):
                    pk = [("ssp", pi, su, cb) for su in range(4) for cb in range(4)]
                    kk = [("ss", col0 + su) for su in range(4)]
                    tr.op("dve", lambda e: e.tensor_tensor(out=ss[:, col0:col0 + 4], in0=ssp[:, pi, :, 0], in1=ssp[:, pi, :, 1], op=ALU.add),
                          r=pk, w=kk)
                    tr.op("dve", lambda e: e.tensor_tensor(out=ss[:, col0:col0 + 4], in0=ss[:, col0:col0 + 4], in1=ssp[:, pi, :, 2], op=ALU.add),
                          r=pk + kk, w=kk)
                    tr.op("dve", lambda e: e.tensor_tensor(out=ss[:, col0:col0 + 4], in0=ss[:, col0:col0 + 4], in1=ssp[:, pi, :, 3], op=ALU.add),
                          r=pk + kk, w=kk)
                    rstd_from_ss(ss[:, col0:col0 + 4], ss[:, col0:col0 + 4], 128, kk, kk, 1.0 / D)
                    for su in range(4):
                        xa, xk = X(ti, su)
                        tr.op("dve", lambda e, su=su, xa=xa: e.scalar_tensor_tensor(
                            out=xa, in0=dst[:, su, :], scalar=ss[:, col0 + su:col0 + su + 1], in1=xa,
                            op0=ALU.mult, op1=ALU.add), r=[skey_fn(su), xk] + kk, w=[xk])

                def norm_to_hs(gbuf, gkey, col0, ti):
                    for su in range(4):
                        xa, xk = X(ti, su)
                        sumsq(xa, [xk], col0 + su)
                        k1 = [("ss", col0 + su)]
                        rstd_from_ss(ss[:, col0 + su:col0 + su + 1], ss[:, col0 + su:col0 + su + 1], 128, k1, k1, 1.0 / D)
                        tr.op("dve", lambda e, su=su, xa=xa: e.scalar_tensor_tensor(
                            out=hs[:, su, :], in0=xa, scalar=ss[:, col0 + su:col0 + su + 1], in1=gbuf[:],
                            op0=ALU.mult, op1=ALU.mult), r=[xk, gkey] + k1, w=[("hs", su)])

                def hs_to_actT():
                    for su in range(4):
                        transposes_to_actT(hs[:, su, :], ("hs", su), actT, su)

                def outproj_items():
                    items = []
                    for cb in range(4):
                        items += unit_items(wb_out[layer], "out%d" % layer, 0, 16, cb * 512)
                    return items

                def mlp_items(half):
                    items = []
                    for b_ in range(8):
                        items += unit_items(wb_mi[layer], "mi%d" % layer, 0, 16, half * 4096 + b_ * 512)
                    for cb in range(4):
                        items += unit_items(wb_mo[layer], "mo%d" % layer, half * 4096, 32, cb * 512)
                    return items

                pws = WStream([])
                if a0:
                    for ti in range(NTILE):
                        pws.extend(inproj_items(0))
                else:
                    pws.extend(outproj_items())
                    for ti in range(NTILE):
                        pws.extend(mlp_items(0))
                        pws.extend(mlp_items(1))
                        if ti + 1 < NTILE:
                            pws.extend(outproj_items())
                        if layer == 0:
                            pws.extend(inproj_items(1))

                def outproj(ti):
                    ws = pws
                    ws.fill()
                    mk = evac_block(mixed, lambda su: ("uT", su), ga, "ga", 0)
                    for cb in range(4):
                        unit_tm(ws, 16, catT, mk(cb), akey=HSK)
                        finish_block(mixed, lambda su: ("uT", su), ga, "ga", 0, cb)

                def pre_mlp(ti):
                    residual_from(mixed, lambda su: ("uT", su), 0, 0, ti)
                    norm_to_hs(gb, "gb", 4, ti)

                if a0:
                    load_gain(ga, "ga", 0, 0)
                    load_x(0)
                    with ExitStack() as pc:
                        wf = [sb("wf%d" % i, [128, 8, 512], F32, pc) for i in range(2)]
                        wc = [sb("wc%d" % i, [128, 8, 512], BF16, pc) for i in range(2)]
                        pn = 0
                        for b_ in range(13):
                            for hb in range(2):
                                pp = pn % 2
                                pn += 1
                                rows = slice(hb * 1024, (hb + 1) * 1024)
                                csl_ = slice(b_ * 512, (b_ + 1) * 512)
                                tr.dma(wf[pp][:], w_ret[rows, csl_].rearrange("(kc p) n -> p kc n", p=128), w=[("wf", pp)], sem="cl%d" % pp)
                                tr.op("pool", lambda e, pp=pp: e.tensor_copy(out=wc[pp][:, 0:3, :], in_=wf[pp][:, 0:3, :]),
                                      r=[("wf", pp)], w=[("wc", pp, 0)])
                                tr.op("act", lambda e, pp=pp: e.activation(out=wc[pp][:, 3:5, :], in_=wf[pp][:, 3:5, :], func=AF.Copy),
                                      r=[("wf", pp)], w=[("wc", pp, 1)])
                                tr.op("dve", lambda e, pp=pp: e.tensor_copy(out=wc[pp][:, 5:8, :], in_=wf[pp][:, 5:8, :]),
                                      r=[("wf", pp)], w=[("wc", pp, 2)])
                                tr.dma(wb_ret[rows, csl_].rearrange("(kc p) n -> p kc n", p=128), wc[pp][:],
                                       r=[("wc", pp, 0), ("wc", pp, 1), ("wc", pp, 2)], w=[("const", "ret", b_, hb)], sem="cw%d" % pp)
                    emit_casts()
                    norm_to_hs(ga, "ga", 12, 0)
                    for ti in range(NTILE):
                        hs_to_actT()
                        hooks = {}
                        if ti + 1 < NTILE:
                            hooks = {0: (lambda ti=ti: load_x(ti + 1)), 3: (lambda ti=ti: norm_to_hs(ga, "ga", 12, ti + 1))}
                        inproj(0, ti, actT, stage_views, hook=hooks, ws=pws)
                    tr.barrier()
                    return
                load_gain(ga, "ga", layer, 1)
                load_gain(gb, "gb", layer, 2)
                load_gain(gc, "gc", layer, 3)
                load_x(0)
                load_cat(0)
                outproj(0)
                pre_mlp(0)
                if layer == 0:
                    load_gain(gb, "gb", 1, 0)
                for ti in range(NTILE):
                    tok0 = ti * 512
                    hs_to_actT()
                    for half in range(2):
                        ws = pws
                        ws.fill()
                        for b_ in range(8):
                            def ev(c, bank, b_=b_):
                                tpi = (b_ * 4 + c) % 2
                                tr.op("act", lambda en: en.activation(out=tmp2[:, tpi, :], in_=banks[bank][:], func=AF.Relu),
                                      r=[("bank", bank)], w=[("tmp", tpi)])
                                tr.op("pool", lambda en: en.tensor_tensor(out=uT[:, b_ * 4 + c, :], in0=tmp2[:, tpi, :],
                                                                          in1=tmp2[:, tpi, :], op=ALU.mult),
                                      r=[("tmp", tpi)], w=[("uT", 0), ("uT", 1), ("uT", 2), ("uT", 3)])
                            unit_fm(ws, 16, actT, ev)
                        if half == 1 and ti + 1 < NTILE:
                            load_cat(ti + 1)
                            load_x(ti + 1, (0, 1))
                        mk = evac_block(o, lambda su: ("o", su), gc, "gc", 1)
                        for cb in range(4):
                            unit_tm_u(ws, uT, mk(cb, add_to=(None if half == 0 else True)))
                            if half == 1:
                                finish_block(o, lambda su: ("o", su), gc, "gc", 1, cb)
                    if ti + 1 < NTILE:
                        outproj(ti + 1)
                    residual_from(o, lambda su: ("o", su), 1, 8, ti)
                    if layer == 0:
                        for su in range(4):
                            xa, xk = X(ti, su)
                            tr.dma(X1[tok0 + su * 128:tok0 + (su + 1) * 128, :], xa, r=[xk], sem="xo%d" % su)
                        norm_to_hs(gb, "gb", 12, ti)
                        hs_to_actT()
                        hooks = {}
                        if ti + 1 < NTILE:
                            def hook1(ti=ti):
                                load_gain(gb, "gb", 0, 2)
                                load_x(ti + 1, (2, 3))

                            def hook2(ti=ti):
                                pre_mlp(ti + 1)
                            hooks = {0: hook1, 3: hook2}
                        inproj(1, ti, actT, stage_views, hook=hooks, ws=pws)
                        if ti + 1 < NTILE:
                            load_gain(gb, "gb", 1, 0)
                    else:
                        for su in range(4):
                            xa, xk = X(ti, su)
                            tr.dma(y_out[tok0 + su * 128:tok0 + (su + 1) * 128, :], xa, r=[xk], sem="xo%d" % su)
                        if ti + 1 < NTILE:
                            load_x(ti + 1, (2, 3))
                            pre_mlp(ti + 1)
                tr.barrier(new_sems=True)

        def unit_tm_u(ws, uT, evac):
            bs = next_bank4()
            for hb in range(4):
                s = ws.get()

                def f(e, hb=hb, s=s):
                    ins = None
                    for k8 in range(8):
                        n = hb * 8 + k8
                        for su in range(4):
                            ins = e.matmul(banks[bs[su]][:], lhsT=uT[:, n, su * 128:(su + 1) * 128], rhs=wslots[s][:, k8, :],
                                           start=(n == 0), stop=(n == 31))
                    return ins
                tr.op("pe", f, r=[("w", s), ("uT", 0), ("uT", 1), ("uT", 2), ("uT", 3)], w=[("bank", b) for b in bs])
                ws.fill()
            for su in range(4):
                evac(su, bs[su])

        def pipeline(N, stages):
            S = len(stages)
            for step in range(N + S - 1):
                for si in range(S - 1, -1, -1):
                    n = step - si
                    if 0 <= n < N:
                        stages[si](n)

        RB = 4

        def phase_memattn(layer, ps, catst4):
            qm = sb("qm", [128, RB, 512], BF16, ps)
            pT = sb("pT", [128, RB, 2, 512], BF16, ps)
            rc = sb("rc", [128, RB, 4], F32, ps)
            om = sb("om", [128, RB, 4, 128], BF16, ps)
            st = {}

            def s0(i):
                ti, h = i // 4, i % 4
                seg = ti // 4
                tok0 = ti * 512
                r = i % RB
                tr.dma(qm[:, r, :], QT[layer][24 + h, :, tok0:tok0 + 512], w=[("qm", r)], sem="qm%d" % r)
                bs = [next_bank(), next_bank()]
                st[i] = bs
                for kc in range(2):
                    tr.op("pe", lambda e, kc=kc: e.matmul(banks[bs[kc]][:], lhsT=memh["K"][:, seg, h, kc * 128:(kc + 1) * 128],
                                                          rhs=qm[:, r, :], start=True, stop=True),
                          r=["memKT", ("qm", r)], w=[("bank", bs[kc])])

            def s1(i):
                r = i % RB
                bs = st[i]
                for kc in range(2):
                    tr.op("act", lambda e, kc=kc: e.activation(out=pT[:, r, kc, :], in_=banks[bs[kc]][:], func=AF.Exp, scale=SCALE),
                          r=[("bank", bs[kc])], w=[("pT", r)])

            def s2(i):
                ti, h = i // 4, i % 4
                seg = ti // 4
                r = i % RB
                bs = st[i]
                for pr in range(2):
                    def f(e, pr=pr):
                        ins = None
                        for s2_ in range(2):
                            su = pr * 2 + s2_
                            for kc in range(2):
                                ins = e.matmul(banks[bs[pr]][:, s2_ * 129:(s2_ + 1) * 129], lhsT=pT[:, r, kc, su * 128:(su + 1) * 128],
                                               rhs=memh["V"][:, seg, kc, h, :], start=(kc == 0), stop=(kc == 1))
                        return ins
                    tr.op("pe", f, r=[("pT", r), "memV"], w=[("bank", bs[pr])])

            def s3(i):
                r = i % RB
                bs = st[i]
                for pr in range(2):
                    v = banks[bs[pr]][:, 0:258].rearrange("p (s c) -> p s c", c=129)
                    tr.op("dve", lambda e, v=v, pr=pr: e.reciprocal(out=rc[:, r, pr * 2:pr * 2 + 2].unsqueeze(2), in_=v[:, :, 128:129]),
                          r=[("bank", bs[pr])], w=[("rc", r)])
                    tr.op("dve", lambda e, v=v, pr=pr: e.tensor_tensor(out=om[:, r, pr * 2:pr * 2 + 2, :], in0=v[:, :, 0:128],
                                                                      in1=rc[:, r, pr * 2:pr * 2 + 2].unsqueeze(2).to_broadcast([128, 2, 128]),
                                                                      op=ALU.mult),
                          r=[("bank", bs[pr]), ("rc", r)], w=[("om", r)])

            def s4(i):
                ti, h = i // 4, i % 4
                tok0 = ti * 512
                r = i % RB
                bs = st.pop(i)
                ci = i % 4
                for pr in range(2):
                    tb = bank_bf(bs[pr])

                    def f(e, pr=pr, tb=tb):
                        ins = None
                        for s2_ in range(2):
                            ins = e.transpose(out=tb[:, 768 + s2_ * 128:768 + (s2_ + 1) * 128], in_=om[:, r, pr * 2 + s2_, :], identity=ident[:])
                        return ins
                    tr.op("pe", f, r=[("om", r), "ident"], w=[("bank", bs[pr])])
                    en_ = tr.alt()
                    if en_ == "act":
                        tr.op("act", lambda e, pr=pr, tb=tb: e.activation(out=catst4[:, ci, pr * 256:(pr + 1) * 256], in_=tb[:, 768:1024], func=AF.Copy),
                              r=[("bank", bs[pr])], w=[("catst", ci)])
                    else:
                        tr.op("dve", lambda e, pr=pr, tb=tb: e.tensor_copy(out=catst4[:, ci, pr * 256:(pr + 1) * 256], in_=tb[:, 768:1024]),
                              r=[("bank", bs[pr])], w=[("catst", ci)])
                tr.dma(CAT[12 + h, :, tok0:tok0 + 512], catst4[:, ci, :], r=[("catst", ci)], sem="cs%d" % ci)

            pipeline(NTILE * 4, [s0, s1, s2, s3, s4])

        def phase_ret(catst):
            with ExitStack() as ps:
                qT2 = [sb("qT%d" % i, [128, NTOK], BF16, ps) for i in range(2)]
                kT2 = [sb("kT%d" % i, [128, NTOK], BF16, ps) for i in range(2)]
                Kt = sb("Kt", [128, NSUB, 128], BF16, ps)
                V2 = [sb("V%d" % i, [128, NSUB, 128], BF16, ps) for i in range(2)]
                Vs = sb("Vs", [128, NSUB, 128], BF16, ps)
                Sb = sb("Sb", [128, NSUB, 128], BF16, ps)
                rc_ = sb("rotc", [128, 2, 512], F32, ps)
                rs_ = sb("rots", [128, 2, 512], F32, ps)
                ta = sb("ta", [128, 2, 512], F32, ps)
                tb_ = sb("tbb", [128, 2, 512], F32, ps)
                dtab = sb("dtab", [128, 24], F32, ps)
                lg = sb("lg", [128, 24], F32, ps)
                hd = sb("hd", [128, 6, 128], F32, ps)
                hc = sb("hc", [128, 8], F32, ps)
                Sf = sb("Sf", [128, 2, 128], F32, ps)
                Sbk = sb("Sbk", [128, 2, 128], F32, ps)
                Sfb = sb("Sfb", [128, RB, 128], BF16, ps)
                qfb = sb("qfb", [128, 3, 2, 512], BF16, ps)
                gt = sb("gt", [128, 2, 1024], BF16, ps)
                sg = sb("sg", [128, 2, 1024], F32, ps)
                PTm = sb("PTm", [128, RB, 128], BF16, ps)
                tok = sb("tok", [128, RB, 128], BF16, ps)
                nst = sb("nst", [128, RB], F32, ps)
                junk = sb("rjunk", [128, 128], F32, ps)
                miscf = sb("miscf", [128, 6, 128], F32, ps)
                tr.dma(miscf[:], misc_in[:, 0:6, :], w=["miscf"], sem="c0")

                tr.dma(dtab[:], decay_in.broadcast_to([128, 24]), w=["dtab"], sem="dt")
                tr.op("act", lambda e: e.activation(out=dtab[:], in_=dtab[:], func=AF.Exp, scale=-float(np.log(2.0))),
                      r=["dtab"], w=["dtab"])
                tr.op("dve", lambda e: e.tensor_scalar(out=lg[:], in0=dtab[:], scalar1=1.0 / 9.0, scalar2=None, op0=ALU.mult),
                      r=["dtab"], w=["lg"])
                for kk in range(8, 0, -1):
                    tr.op("dve", lambda e, kk=kk: e.scalar_tensor_tensor(out=lg[:], in0=lg[:], scalar=1.0 / kk, in1=dtab[:],
                                                                        op0=ALU.add, op1=ALU.mult), r=["lg", "dtab"], w=["lg"])
                tr.op("dve", lambda e: e.tensor_scalar(out=lg[:], in0=lg[:], scalar1=-1.0, scalar2=None, op0=ALU.mult),
                      r=["lg"], w=["lg"])
                def head_loads(h):
                    hp = h % 2
                    tr.dma(qT2[hp][:], QT[0][h], w=[("qT", hp, b_) for b_ in range(16)], sem="lq%d" % hp)
                    tr.dma(kT2[hp][:], QT[0][12 + h], w=[("kT", hp, b_) for b_ in range(16)], sem="lk%d" % hp)
                    for q4 in range(4):
                        tr.dma(V2[hp][:, q4 * 16:(q4 + 1) * 16, :],
                               VG[q4 * 2048:(q4 + 1) * 2048, h * 128:(h + 1) * 128].rearrange("(n p) e -> p n e", p=128),
                               w=[("V", hp)], sem="lv%d" % hp)

                def rotary_block(hh, blk):
                    hpp = hh % 2
                    pi = blk % 2
                    tsl = slice(blk * 512, (blk + 1) * 512)
                    tr.dma(rc_[:, pi, :], rotc_in[:, tsl], w=[("rc", pi)], sem="rc%d" % pi)
                    tr.dma(rs_[:, pi, :], rots_in[:, tsl], w=[("rs", pi)], sem="rs%d" % pi)
                    for qk, (nm0, T_) in enumerate((("qT", qT2[hpp]), ("kT", kT2[hpp]))):
                        ti_ = qk
                        nm = (nm0, hpp)
                        b = next_bank()
                        tr.op("pe", lambda e, b=b, T_=T_: e.matmul(banks[b][:], lhsT=perm[:], rhs=T_[:, tsl], start=True, stop=True),
                              r=[nm + (blk,), "perm"], w=[("bank", b)])
                        tr.op("dve", lambda e, T_=T_: e.tensor_tensor(out=ta[:, ti_, :], in0=T_[:, tsl], in1=rc_[:, pi, :], op=ALU.mult),
                              r=[nm + (blk,), ("rc", pi)], w=[("ta", ti_)])
                        tr.op("dve", lambda e, b=b: e.tensor_tensor(out=tb_[:, ti_, :], in0=banks[b][:], in1=rs_[:, pi, :], op=ALU.mult),
                              r=[("bank", b), ("rs", pi)], w=[("tb", ti_)])
                        tr.op("pool", lambda e, T_=T_: e.tensor_tensor(out=T_[:, tsl], in0=ta[:, ti_, :], in1=tb_[:, ti_, :], op=ALU.add),
                              r=[("ta", ti_), ("tb", ti_)], w=[nm + (blk,)])

                head_loads(0)
                for h in range(12):
                    lf = lg[:, h:h + 1]
                    lb = lg[:, 12 + h:13 + h]
                    hp = h % 2
                    qT, kT, V = qT2[hp], kT2[hp], V2[hp]
                    Vk = ("V", hp)
                    tr.op("act", lambda e: e.activation(out=hd[:, 0, :], in_=miscf[:, 0, :], func=AF.Exp, scale=lf),
                          r=["lg", "miscf"], w=[("hd", 0)])
                    tr.op("act", lambda e: e.activation(out=hd[:, 1, :], in_=miscf[:, 1, :], func=AF.Exp, scale=lb),
                          r=["lg", "miscf"], w=[("hd", 1)])
                    tr.op("dve", lambda e: e.tensor_tensor(out=hd[:, 0, :], in0=hd[:, 0, :], in1=miscf[:, 2, :], op=ALU.mult),
                          r=[("hd", 0), "miscf"], w=[("hd", 0)])
                    tr.op("dve", lambda e: e.tensor_tensor(out=hd[:, 1, :], in0=hd[:, 1, :], in1=miscf[:, 3, :], op=ALU.mult),
                          r=[("hd", 1), "miscf"], w=[("hd", 1)])
                    tr.op("dve", lambda e: e.tensor_tensor(out=hd[:, 2, :], in0=hd[:, 0, :], in1=hd[:, 1, :], op=ALU.add),
                          r=[("hd", 0), ("hd", 1)], w=[("hd", 2)])
                    tr.op("act", lambda e: e.activation(out=hd[:, 3, :], in_=miscf[:, 4, :], func=AF.Exp, scale=lf),
                          r=["lg", "miscf"], w=[("hd", 3)])
                    tr.op("act", lambda e: e.activation(out=hd[:, 4, :], in_=miscf[:, 5, :], func=AF.Exp, scale=lb),
                          r=["lg", "miscf"], w=[("hd", 4)])
                    tr.op("act", lambda e: e.activation(out=hc[:, 0:1], in_=cols[:, 0:1], func=AF.Exp, scale=lf),
                          r=["lg", "cols"], w=["hc"])
                    tr.op("act", lambda e: e.activation(out=hc[:, 1:2], in_=cols[:, 1:2], func=AF.Exp, scale=lb),
                          r=["lg", "cols", "hc"], w=["hc"])
                    tr.op("act", lambda e: e.activation(out=hc[:, 2:3], in_=lf, func=AF.Exp, scale=128.0), r=["lg", "hc"], w=["hc"])
                    tr.op("act", lambda e: e.activation(out=hc[:, 3:4], in_=lb, func=AF.Exp, scale=128.0), r=["lg", "hc"], w=["hc"])
                    tr.op("dve", lambda e: e.tensor_scalar(out=hc[:, 0:2], in0=hc[:, 0:2], scalar1=float(SCALE), scalar2=None,
                                                           op0=ALU.mult), r=["hc"], w=["hc"])
                    kdf, kdb, cdf, cdb = hc[:, 0:1], hc[:, 1:2], hc[:, 2:3], hc[:, 3:4]
                    if h == 0:
                        for blk in range(16):
                            rotary_block(0, blk)
                    for g in range(16):
                        b = next_bank()
                        tbk = bank_bf(b)

                        def f(e, g=g, tbk=tbk):
                            ins = None
                            for q4 in range(4):
                                n = g * 4 + q4
                                ins = e.transpose(out=tbk[:, q4 * 128:(q4 + 1) * 128], in_=kT[:, n * 128:(n + 1) * 128], identity=ident[:])
                            return ins
                        tr.op("pe", f, r=[("kT", hp, g), "ident"], w=[("bank", b)])
                        en_ = tr.alt()
                        src = tbk[:, 0:512].rearrange("p (a d) -> p a d", a=4)
                        if en_ == "act":
                            tr.op("act", lambda e, g=g, src=src: e.activation(out=Kt[:, g * 4:(g + 1) * 4, :], in_=src, func=AF.Copy),
                                  r=[("bank", b)], w=[("Kt", g)])
                        else:
                            tr.op("dve", lambda e, g=g, src=src: e.tensor_copy(out=Kt[:, g * 4:(g + 1) * 4, :], in_=src),
                                  r=[("bank", b)], w=[("Kt", g)])
                    if h + 1 < 12:
                        head_loads(h + 1)
                    tr.op("dve", lambda e: e.tensor_scalar(out=Vs[:].rearrange("p n e -> p (n e)"), in0=V[:].rearrange("p n e -> p (n e)"),
                                                           scalar1=kdb, scalar2=None, op0=ALU.mult), r=[Vk, "hc"], w=["Vs"])
                    tr.op("pool", lambda e: e.memset(Sbk[:, (NSUB - 1) % 2, :], 0.0), w=[("Sbk", (NSUB - 1) % 2)])
                    tr.op("pool", lambda e: e.memset(Sb[:, NSUB - 1, :], 0.0), w=[("Sb", NSUB - 1)])
                    for n in range(NSUB - 2, -1, -1):
                        b = next_bank()
                        pw, pr_ = n % 2, (n + 1) % 2
                        tr.op("pe", lambda e, b=b, n=n: e.matmul(banks[b][:, 0:128], lhsT=Kt[:, n + 1, :], rhs=Vs[:, n + 1, :], start=True, stop=True),
                              r=[("Kt", (n + 1) // 4), "Vs"], w=[("bank", b)])
                        tr.op("dve", lambda e, b=b, pw=pw, pr_=pr_: e.scalar_tensor_tensor(out=Sbk[:, pw, :], in0=Sbk[:, pr_, :], scalar=cdb, in1=banks[b][:, 0:128],
                                                                          op0=ALU.mult, op1=ALU.add), r=[("Sbk", pr_), "hc", ("bank", b)], w=[("Sbk", pw)])
                        if (n + 1) % 16 == 0:
                            tr.op("dve", lambda e, pw=pw: e.tensor_scalar(out=Sbk[:, pw, :], in0=Sbk[:, pw, :], scalar1=carry_col, scalar2=None, op0=ALU.mult),
                                  r=[("Sbk", pw), "cols"], w=[("Sbk", pw)])
                        tr.op("act", lambda e, n=n, pw=pw: e.activation(out=Sb[:, n, :], in_=Sbk[:, pw, :], func=AF.Copy), r=[("Sbk", pw)], w=[("Sb", n)])
                    tr.op("dve", lambda e: e.tensor_scalar(out=Vs[:].rearrange("p n e -> p (n e)"), in0=V[:].rearrange("p n e -> p (n e)"),
                                                           scalar1=kdf, scalar2=None, op0=ALU.mult), r=[Vk, "hc"], w=["Vs"])
                    tr.op("pool", lambda e: e.memset(Sf[:, 1, :], 0.0), w=[("Sf", 1)])
                    tr.op("pool", lambda e: e.memset(Sfb[:, 0, :], 0.0), w=[("Sfb", 0)])
                    st = {}

                    def gate_group(G, h=h):
                        gp = G % 2
                        tr.dma(gt[:, gp, :].rearrange("p (n e) -> p n e", n=8),
                               VG[G * 1024:(G + 1) * 1024, 1536 + h * 128:1536 + (h + 1) * 128].rearrange("(n p) e -> p n e", p=128),
                               w=[("gt", gp)], sem="gt%d" % gp)
                        tr.op("act", lambda e: e.activation(out=sg[:, gp, :], in_=gt[:, gp, :], func=AF.Silu), r=[("gt", gp)], w=[("sg", gp)])

                    gate_group(0)

                    def f0(n, h=h):
                        g = n // 4
                        gi = g % 3
                        if n % 8 == 4 and n // 8 + 1 < 8:
                            gate_group(n // 8 + 1)
                        if n % 4 == 2 and h + 1 < 12:
                            rotary_block(h + 1, n // 4)
                        if n % 4 == 0:
                            gsl = slice(g * 512, (g + 1) * 512)
                            tr.op("pool", lambda e: e.tensor_tensor(out=qfb[:, gi, 0, :].rearrange("p (n i) -> p n i", n=4),
                                                                    in0=qT[:, gsl].rearrange("p (n i) -> p n i", n=4),
                                                                    in1=hd[:, 3:4, :].to_broadcast([128, 4, 128]), op=ALU.mult),
                                  r=[("qT", hp, g), ("hd", 3)], w=[("qfb", gi, 0)])
                            tr.op("pool", lambda e: e.tensor_tensor(out=qfb[:, gi, 1, :].rearrange("p (n i) -> p n i", n=4),
                                                                    in0=qT[:, gsl].rearrange("p (n i) -> p n i", n=4),
                                                                    in1=hd[:, 4:5, :].to_broadcast([128, 4, 128]), op=ALU.mult),
                                  r=[("qT", hp, g), ("hd", 4)], w=[("qfb", gi, 1)])
                        csl = slice(n * 128, (n + 1) * 128)
                        bA = next_bank()
                        bY = next_bank()
                        st[n] = (bA, bY)

                        def f(e):
                            e.matmul(banks[bA][:, 0:128], lhsT=kT[:, csl], rhs=qT[:, csl], start=True, stop=True)
                            return e.matmul(banks[bA][:, 128:256], lhsT=Kt[:, n, :], rhs=Vs[:, n, :], start=True, stop=True)
                        tr.op("pe", f, r=[("kT", hp, g), ("qT", hp, g), ("Kt", g), "Vs"], w=[("bank", bA)])

                    def f1(n):
                        bA, bY = st[n]
                        r_ = n % RB
                        tr.op("dve", lambda e: e.tensor_tensor(out=PTm[:, r_, :], in0=banks[bA][:, 0:128], in1=hd[:, 2, :], op=ALU.mult),
                              r=[("bank", bA), ("hd", 2)], w=[("PTm", r_)])
                        pw, pr_ = n % 2, (n + 1) % 2
                        tr.op("dve", lambda e: e.scalar_tensor_tensor(out=Sf[:, pw, :], in0=Sf[:, pr_, :], scalar=cdf, in1=banks[bA][:, 128:256],
                                                                     op0=ALU.mult, op1=ALU.add), r=[("Sf", pr_), "hc", ("bank", bA)], w=[("Sf", pw)])
                        if (n + 1) % 16 == 0 and n + 1 < NSUB:
                            tr.op("dve", lambda e: e.tensor_scalar(out=Sf[:, pw, :], in0=Sf[:, pw, :], scalar1=carry_col, scalar2=None, op0=ALU.mult),
                                  r=[("Sf", pw), "cols"], w=[("Sf", pw)])
                        if n + 1 < NSUB:
                            rn = (n + 1) % RB
                            tr.op("pool", lambda e: e.tensor_copy(out=Sfb[:, rn, :], in_=Sf[:, pw, :]), r=[("Sf", pw)], w=[("Sfb", rn)])

                    def f2(n):
                        bA, bY = st[n]
                        r_ = n % RB
                        gi = (n // 4) % 3
                        lsl = slice((n % 4) * 128, (n % 4 + 1) * 128)

                        def fy(e):
                            e.matmul(banks[bY][:, 0:128], lhsT=PTm[:, r_, :], rhs=V[:, n, :], start=True, stop=False)
                            e.matmul(banks[bY][:, 0:128], lhsT=qfb[:, gi, 0, lsl], rhs=Sfb[:, r_, :], start=False, stop=False)
                            return e.matmul(banks[bY][:, 0:128], lhsT=qfb[:, gi, 1, lsl], rhs=Sb[:, n, :], start=False, stop=True)
                        tr.op("pe", fy, r=[("PTm", r_), Vk, ("qfb", gi, 0), ("qfb", gi, 1), ("Sfb", r_), ("Sb", n)], w=[("bank", bY)])

                    def f3(n):
                        bA, bY = st[n]
                        r_ = n % RB
                        gi = (n // 8) % 2
                        lsl = slice((n % 8) * 128, (n % 8 + 1) * 128)
                        tr.op("act", lambda e: e.activation(out=junk[:], in_=banks[bY][:, 0:128], func=AF.Square, accum_out=nst[:, r_:r_ + 1]),
                              r=[("bank", bY)], w=["rjunk", ("nst", r_)])
                        rstd_from_ss(nst[:, r_:r_ + 1], nst[:, r_:r_ + 1], 128, [("nst", r_)], [("nst", r_)], 1.0 / 128.0)
                        tr.op("dve", lambda e: e.scalar_tensor_tensor(out=tok[:, r_, :], in0=banks[bY][:, 0:128], scalar=nst[:, r_:r_ + 1],
                                                                     in1=sg[:, gi, lsl], op0=ALU.mult, op1=ALU.mult),
                              r=[("bank", bY), ("nst", r_), ("sg", gi)], w=[("tok", r_)])

                    def f4(n, h=h):
                        bA, bY = st.pop(n)
                        r_ = n % RB
                        g = n // 4
                        ci = g % 4
                        lsl = slice((n % 4) * 128, (n % 4 + 1) * 128)
                        tbt = bank_bf(bY)
                        tr.op("pe", lambda e: e.transpose(out=tbt[:, 512:640], in_=tok[:, r_, :], identity=ident[:]),
                              r=[("tok", r_), "ident"], w=[("bank", bY)])
                        tr.op("dve", lambda e: e.tensor_copy(out=catst[:, ci, lsl], in_=tbt[:, 512:640]),
                              r=[("bank", bY)], w=[("catst", ci)])
                        if n % 4 == 3:
                            tr.dma(CAT[h, :, g * 512:(g + 1) * 512], catst[:, ci, :], r=[("catst", ci)], sem="cs%d" % ci)

                    pipeline(NSUB, [f0, f1, f2, f3, f4])
                tr.barrier(new_sems=True)

        def phase_na():
            with ExitStack() as ps:
                catst = sb("catst", [128, 4, 512], BF16, ps)
                with ExitStack() as ps2:
                    phase_memattn(1, ps2, catst)
                    tr.barrier()
                qT2 = [sb("qT%d" % i, [128, NTOK], BF16, ps) for i in range(2)]
                kT2 = [sb("kT%d" % i, [128, NTOK], BF16, ps) for i in range(2)]
                Va2 = [sb("Va%d" % i, [128, NSUB, 129], BF16, ps) for i in range(2)]
                Et2 = [sb("Et%d" % i, [128, 12, 128], F32, ps) for i in range(2)]
                maskc = sb("maskc", [128, NMASK], F32, ps)
                E1 = sb("E1", [128, RB, 8, 128], F32, ps)
                PT = sb("PT", [128, RB, 8, 128], BF16, ps)
                rc = sb("rcn", [128, RB], F32, ps)
                ot = sb("ot", [128, RB, 128], BF16, ps)
                tr.dma(maskc[:], maskc_in, w=["maskc"], sem="mc")
                for i_ in range(2):
                    tr.op("pool", lambda e, i_=i_: e.memset(Va2[i_][:], 1.0), w=[("Va", i_)])

                def head_loads(h):
                    hp = h % 2
                    tr.dma(qT2[hp][:], QT[1][h], w=[("qT", hp)], sem="lq%d" % hp)
                    tr.dma(kT2[hp][:], QT[1][12 + h], w=[("kT", hp)], sem="lk%d" % hp)
                    for q4 in range(4):
                        tr.dma(Va2[hp][:, q4 * 16:(q4 + 1) * 16, 0:128],
                               VG[q4 * 2048:(q4 + 1) * 2048, h * 128:(h + 1) * 128].rearrange("(n p) e -> p n e", p=128),
                               w=[("Va", hp)], sem="lv%d" % hp)
                    tr.dma(Et2[hp][:], btab_in[h], w=[("Et", hp)], sem="bt%d" % hp)

                head_loads(0)
                for h in range(12):
                    hp = h % 2
                    qT, kT, Va, Et = qT2[hp], kT2[hp], Va2[hp], Et2[hp]
                    qk_, kk_, vk_, ek_ = ("qT", hp), ("kT", hp), ("Va", hp), ("Et", hp)
                    tr.op("act", lambda e: e.activation(out=Et[:], in_=Et[:], func=AF.Exp), r=[ek_], w=[ek_])
                    if h + 1 < 12:
                        head_loads(h + 1)
                    st = {}

                    def jlist(T):
                        s_, t_ = T // 16, T % 16
                        interior = 2 <= t_ <= 13
                        return (list(range(-2, 3)) if interior else _BPLAN[(s_, t_)]), interior

                    def g0(T):
                        js, interior = jlist(T)
                        qsl = slice(T * 128, (T + 1) * 128)
                        b1 = next_bank()
                        b2 = next_bank()
                        st[T] = (b1, b2)
                        g1_, g2_ = js[0:4], js[4:8]

                        def f(e):
                            ins = None
                            for gi_, j in enumerate(g1_):
                                T2 = T + j
                                ins = e.matmul(banks[b1][:, gi_ * 128:(gi_ + 1) * 128], lhsT=kT[:, T2 * 128:(T2 + 1) * 128],
                                               rhs=qT[:, qsl], start=True, stop=True)
                            return ins
                        tr.op("pe", f, r=[kk_, qk_], w=[("bank", b1)])

                        def f2_(e):
                            ins = None
                            for gi_, j in enumerate(g2_):
                                T2 = T + j
                                ins = e.matmul(banks[b2][:, gi_ * 128:(gi_ + 1) * 128], lhsT=kT[:, T2 * 128:(T2 + 1) * 128],
                                               rhs=qT[:, qsl], start=True, stop=True)
                            return ins
                        if g2_:
                            tr.op("pe", f2_, r=[kk_, qk_], w=[("bank", b2)])

                    def g1(T):
                        js, interior = jlist(T)
                        b1, b2 = st[T]
                        r_ = T % RB
                        n1 = len(js[0:4])
                        n2 = len(js[4:8])
                        tr.op("act", lambda e: e.activation(out=E1[:, r_, 0:n1, :], in_=banks[b1][:, 0:n1 * 128].rearrange("p (a q) -> p a q", a=n1),
                                                            func=AF.Exp, scale=SCALE), r=[("bank", b1)], w=[("E1", r_)])
                        if n2:
                            tr.op("act", lambda e: e.activation(out=E1[:, r_, 4:4 + n2, :], in_=banks[b2][:, 0:n2 * 128].rearrange("p (a q) -> p a q", a=n2),
                                                                func=AF.Exp, scale=SCALE), r=[("bank", b2)], w=[("E1", r_)])

                    def g2(T):
                        js, interior = jlist(T)
                        s_, t_ = T // 16, T % 16
                        r_ = T % RB
                        if interior:
                            en_ = "pool" if T % 3 == 0 else "dve"
                            tr.op(en_, lambda e: e.tensor_tensor(out=PT[:, r_, 0:5, :], in0=E1[:, r_, 0:5, :], in1=Et[:, 0:5, :], op=ALU.mult),
                                  r=[("E1", r_), ek_], w=[("PT", r_)])
                        else:
                            for ji, j in enumerate(js):
                                for b_ in (0, 1):
                                    mi = _MASKIDX[(s_, t_, j, b_)]
                                    hs_ = slice(b_ * 64, (b_ + 1) * 64)
                                    tr.op("dve", lambda e, ji=ji, j=j, mi=mi, hs_=hs_: e.scalar_tensor_tensor(
                                        out=PT[:, r_, ji, hs_], in0=E1[:, r_, ji, hs_], scalar=maskc[:, mi:mi + 1],
                                        in1=Et[:, 5 + j + 3, hs_], op0=ALU.mult, op1=ALU.mult),
                                        r=[("E1", r_), ek_, "maskc"], w=[("PT", r_)])

                    def g3(T):
                        js, interior = jlist(T)
                        b1, b2 = st[T]
                        r_ = T % RB

                        def fo(e):
                            ins = None
                            for ji, j in enumerate(js):
                                ins = e.matmul(banks[b2][:, 256:385], lhsT=PT[:, r_, ji, :], rhs=Va[:, T + j, :],
                                               start=(ji == 0), stop=(ji == len(js) - 1))
                            return ins
                        tr.op("pe", fo, r=[("PT", r_), vk_], w=[("bank", b2)])

                    def g4(T):
                        b1, b2 = st[T]
                        r_ = T % RB
                        tr.op("dve", lambda e: e.reciprocal(out=rc[:, r_:r_ + 1], in_=banks[b2][:, 384:385]),
                              r=[("bank", b2)], w=[("rcn", r_)])
                        tr.op("act", lambda e: e.activation(out=ot[:, r_, :], in_=banks[b2][:, 256:384], func=AF.Copy, scale=rc[:, r_:r_ + 1]),
                              r=[("bank", b2), ("rcn", r_)], w=[("ot", r_)])

                    def g5(T, h=h):
                        b1, b2 = st.pop(T)
                        r_ = T % RB
                        tbt = bank_bf(b2)
                        tr.op("pe", lambda e: e.transpose(out=tbt[:, 800:928], in_=ot[:, r_, :], identity=ident[:]),
                              r=[("ot", r_), "ident"], w=[("bank", b2)])
                        g = T // 4
                        ci = g % 4
                        lsl = slice((T % 4) * 128, (T % 4 + 1) * 128)
                        tr.op("dve", lambda e: e.tensor_copy(out=catst[:, ci, lsl], in_=tbt[:, 800:928]),
                              r=[("bank", b2)], w=[("catst", ci)])
                        if T % 4 == 3:
                            tr.dma(CAT[h, :, g * 512:(g + 1) * 512], catst[:, ci, :], r=[("catst", ci)], sem="cs%d" % ci)

                    pipeline(NSUB, [g0, g1, g2, g3, g4, g5])
                tr.barrier(new_sems=True)

        phases = os.environ.get("MK_PHASES", "all")
        phase_c(0, a0=True)
        if phases != "a0":
            with ExitStack() as outer:
                catst0 = sb("catst", [128, 4, 512], BF16, outer)
                with ExitStack() as lb:
                    alloc_mem(lb)
                    phase_mem(0)
                    with ExitStack() as ps2:
                        phase_memattn(0, ps2, catst0)
                        tr.barrier()
                phase_ret(catst0)
            if phases != "b0":
                phase_c(0)
                if phases != "c0":
                    with ExitStack() as lb:
                        alloc_mem(lb)
                        phase_mem(1)
                        phase_na()
                    if phases != "b1":
                        phase_c(1)
        tr.barrier()
    return nc


_NC_CACHE = {}


def _prep_inputs(x_prompt, x_sample, mem_prompt, mem_sample, norm_gain, mem_norm_gain, w_mem_kv, w_out,
                 w_mlp_in, w_mlp_out, w_in_ret, ret_decay, w_in_na, na_rpb):
    f = lambda a: np.ascontiguousarray(np.asarray(a, dtype=np.float32))
    shared = {
        "norm_gain": f(norm_gain), "mem_norm_gain": f(mem_norm_gain), "w_mem_kv": f(w_mem_kv), "w_out": f(w_out),
        "w_mlp_in": f(w_mlp_in), "w_mlp_out": f(w_mlp_out), "w_in_ret": f(w_in_ret)[0], "w_in_na": f(w_in_na)[0],
        "ret_decay": f(ret_decay).reshape(1, 24), "btab": _na_bias_tables(f(na_rpb)[0]),
    }
    consts = {False: _const_tables(False), True: _const_tables(True)}
    xp = f(x_prompt)
    xs = f(x_sample)
    mp = f(mem_prompt)
    ms = f(mem_sample)
    in_maps = []
    for c in range(8):
        samp = c >= 4
        m = dict(shared)
        if not samp:
            m["x"] = xp[4 * c:4 * c + 4].reshape(NTOK, D)
            m["mem"] = mp[4 * c:4 * c + 4]
        else:
            m["x"] = xs[c - 4]
            m["mem"] = np.ascontiguousarray(np.broadcast_to(ms[c - 4][None], (4, 256, D)))
        m.update(consts[samp])
        in_maps.append(m)
    return in_maps


def kernel(x_prompt, x_sample, mem_prompt, mem_sample, norm_gain, mem_norm_gain, w_mem_kv, w_out,
           w_mlp_in, w_mlp_out, w_in_ret, ret_decay, w_in_na, na_rpb):
    in_maps = _prep_inputs(x_prompt, x_sample, mem_prompt, mem_sample, norm_gain, mem_norm_gain, w_mem_kv, w_out,
                           w_mlp_in, w_mlp_out, w_in_ret, ret_decay, w_in_na, na_rpb)
    if "nc" not in _NC_CACHE:
        _NC_CACHE["nc"] = build_program()
    res = run_bass_kernel_spmd(_NC_CACHE["nc"], in_maps, core_ids=list(range(8)))
    ys = [np.asarray(r["y"], dtype=np.float32) for r in res.results]
    y_prompt = np.concatenate([y.reshape(4, 2048, D) for y in ys[:4]], axis=0)
    y_sample = np.stack([y.reshape(8192, D) for y in ys[4:]], axis=0)
    return (y_prompt, y_sample)
```

```python
import os
import numpy as np
from contextlib import ExitStack
import concourse.bass as bass
import concourse.mybir as mybir
from concourse.bass_utils import run_bass_kernel_spmd

F32 = mybir.dt.float32
BF16 = mybir.dt.bfloat16
AF = mybir.ActivationFunctionType
ALU = mybir.AluOpType

D = 2048
NTOK = 8192
NSUB = 64
NTILE = 16
DFF = 8192
EPS = 1e-6
SCALE = 128.0 ** -0.5
W_RET = 6656
W_NA = 5120


class TR:
    def __init__(self, nc, es):
        self.nc = nc
        self.es = es
        self.eng = {"pe": nc.tensor, "act": nc.scalar, "dve": nc.vector, "pool": nc.gpsimd, "sp": nc.sync}
        self.sems = {}
        self.cnt = {}
        self.epoch = 0
        self.esem = {}
        self.waited = {e: {} for e in self.eng}
        self.lastw = {}
        self.reads = {}
        self.flip = 0
        self._new_engine_sems()

    def _sem(self, key):
        if key not in self.sems:
            self.sems[key] = self.es.enter_context(self.nc.semaphore("s_%s" % str(key).replace(" ", "")))
            self.cnt[key] = 0
        return self.sems[key]

    def _new_engine_sems(self):
        for e in ("pe", "act", "dve", "pool"):
            k = ("E", e, self.epoch)
            self._sem(k)
            self.esem[e] = k

    def _wait(self, e, tok):
        k, v = tok
        if k == self.esem.get(e) and e == "pe":
            return
        if self.waited[e].get(k, 0) >= v:
            return
        self.eng[e].wait_ge(self.sems[k], v)
        self.waited[e][k] = v

    def _deps(self, e, r, w):
        toks = []
        for x in r:
            t = self.lastw.get(x)
            if t is not None:
                toks.append(t)
        for x in w:
            t = self.lastw.get(x)
            if t is not None:
                toks.append(t)
            toks.extend(self.reads.get(x, ()))
        for t in toks:
            self._wait(e, t)

    def _commit(self, tok, r, w):
        for x in r:
            if isinstance(x, tuple) and x and x[0] == "const":
                continue
            self.reads.setdefault(x, []).append(tok)
        for x in w:
            self.lastw[x] = tok
            self.reads[x] = []

    def op(self, e, fn, r=(), w=()):
        self._deps(e, r, w)
        ins = fn(self.eng[e])
        k = self.esem[e]
        ins.then_inc(self.sems[k], 1)
        self.cnt[k] += 1
        tok = (k, self.cnt[k])
        self._commit(tok, r, w)
        return tok

    def dma(self, out, in_, r=(), w=(), sem="d", q="sp", **kw):
        self._deps(q, r, w)
        k = ("D", sem)
        s = self._sem(k)
        ins = self.eng[q].dma_start(out=out, in_=in_, **kw)
        ins.then_inc(s, 16)
        self.cnt[k] += 16
        tok = (k, self.cnt[k])
        self._commit(tok, r, w)
        return tok

    def barrier(self, new_sems=False):
        for e in self.eng:
            for k, s in self.sems.items():
                if self.cnt[k] > 0 and self.waited[e].get(k, 0) < self.cnt[k]:
                    if k == self.esem.get(e) and e == "pe":
                        continue
                    self.eng[e].wait_ge(s, self.cnt[k])
                    self.waited[e][k] = self.cnt[k]
        self.lastw = {}
        self.reads = {}
        if new_sems:
            self.epoch += 1
            self._new_engine_sems()

    def alt(self):
        self.flip ^= 1
        return "act" if self.flip else "dve"


def _row_window(core_is_sample, s, r_local):
    if core_is_sample:
        R = 32 * s + r_local
        return int(np.clip(R - 4, 0, 120))
    return 32 * s + int(np.clip(r_local - 4, 0, 24))


def _boundary_plan():
    plan = {}
    for s in range(4):
        for t in (0, 1, 14, 15):
            js = []
            for j in range(-3, 4):
                T2 = 16 * s + t + j
                if T2 < 0 or T2 > 63:
                    continue
                ok = False
                for samp in (False, True):
                    for b in (0, 1):
                        rs = _row_window(samp, s, 2 * t + b)
                        for a in (0, 1):
                            kr = 2 * T2 + a
                            if rs <= kr < rs + 8:
                                ok = True
                if ok:
                    js.append(j)
            plan[(s, t)] = js
    return plan


_BPLAN = _boundary_plan()
_MASKIDX = {}
for (_s, _t), _js in sorted(_BPLAN.items()):
    for _j in _js:
        for _b in (0, 1):
            _MASKIDX[(_s, _t, _j, _b)] = len(_MASKIDX)
NMASK = len(_MASKIDX)


def _mask_table(core_is_sample):
    m = np.zeros((128, NMASK), np.float32)
    a = np.arange(128) // 64
    for (s, t, j, b), idx in _MASKIDX.items():
        rs = _row_window(core_is_sample, s, 2 * t + b)
        kr = 2 * (16 * s + t + j) + a
        m[:, idx] = ((kr >= rs) & (kr < rs + 8)).astype(np.float32)
    return m


def _na_bias_tables(rpb):
    NEG = np.float32(-30000.0)
    kp = np.arange(128)
    a = (kp // 64)[:, None]
    kc = (kp % 64)[:, None]
    qp = np.arange(128)
    b = (qp // 64)[None, :]
    qc = (qp % 64)[None, :]
    ws = np.clip(qc - 8, 0, 48)
    colvalid = (kc >= ws) & (kc < ws + 16)
    dc = np.clip(kc - qc + 15, 0, 30)
    out = np.empty((12, 128, 12, 128), np.float32)
    tabs = [(j, True) for j in range(-2, 3)] + [(j, False) for j in range(-3, 4)]
    for ti, (j, interior) in enumerate(tabs):
        dr = 2 * j + a - b
        if interior:
            rowvalid = (dr >= -4) & (dr <= 3)
        else:
            rowvalid = (dr >= -7) & (dr <= 7)
        valid = colvalid & rowvalid
        dri = np.clip(dr + 7, 0, 14)
        for h in range(12):
            g = rpb[h][dri, dc]
            out[h, :, ti, :] = np.where(valid, g, NEG)
    return out


def _const_tables(core_is_sample):
    c = {}
    t = np.arange(NTOK)
    pos = (t if core_is_sample else (t % 2048)).astype(np.float32)
    half = 64
    inv_freq = (np.float32(10000.0) ** (-np.arange(half, dtype=np.float32) / np.float32(half))).astype(np.float32)
    ang = (pos[None, :] * inv_freq[:, None]).astype(np.float32)
    cos = np.cos(ang).astype(np.float32)
    sin = np.sin(ang).astype(np.float32)
    c["rotc"] = np.ascontiguousarray(np.concatenate([cos, cos], 0))
    c["rots"] = np.ascontiguousarray(np.concatenate([-sin, sin], 0))
    j = np.arange(128, dtype=np.float32)[:, None]
    i = np.arange(128, dtype=np.float32)[None, :]
    misc = np.zeros((128, 8, 128), np.float32)
    misc[:, 0, :] = np.maximum(i - j, 0.0)
    misc[:, 1, :] = np.maximum(j - i, 0.0)
    misc[:, 2, :] = (i >= j).astype(np.float32) * np.float32(SCALE)
    misc[:, 3, :] = (j > i).astype(np.float32) * np.float32(SCALE)
    misc[:, 4, :] = i + 1.0
    misc[:, 5, :] = 128.0 - i
    misc[:, 6, :] = np.eye(128, dtype=np.float32)
    perm = np.zeros((128, 128), np.float32)
    for m in range(128):
        perm[(m + 64) % 128, m] = 1.0
    misc[:, 7, :] = perm
    c["misc"] = misc
    cols = np.zeros((128, 8), np.float32)
    cols[:, 0] = 127.0 - np.arange(128)
    cols[:, 1] = np.arange(128)
    cols[:, 2] = 1.0 if core_is_sample else 0.0
    cols[:, 3] = EPS
    c["cols"] = cols
    c["maskc"] = _mask_table(core_is_sample)
    return c


def build_program(debug=False):
    nc = bass.Bass("TRN2", target_bir_lowering=False)

    def din(name, shape, dt=F32):
        return nc.dram_tensor(name, list(shape), dt, kind="ExternalInput").ap()

    dump = set(os.environ.get("MK_DUMP", "").split(","))

    def dscr(name, shape, dt):
        if debug and name in dump:
            return nc.dram_tensor(name, list(shape), dt, kind="ExternalOutput").ap()
        return nc.dram_tensor(name, list(shape), dt).ap()

    x_in = din("x", [NTOK, D])
    mem_in = din("mem", [4, 256, D])
    ng_in = din("norm_gain", [2, 4, D])
    mg_in = din("mem_norm_gain", [2, D])
    w_memkv = din("w_mem_kv", [2, D, 1024])
    w_out = din("w_out", [2, D, D])
    w_mi = din("w_mlp_in", [2, D, DFF])
    w_mo = din("w_mlp_out", [2, DFF, D])
    w_ret = din("w_in_ret", [D, W_RET])
    w_na = din("w_in_na", [D, W_NA])
    decay_in = din("ret_decay", [1, 24])
    btab_in = din("btab", [12, 128, 12, 128])
    rotc_in = din("rotc", [128, NTOK])
    rots_in = din("rots", [128, NTOK])
    misc_in = din("misc", [128, 8, 128])
    cols_in = din("cols", [128, 8])
    maskc_in = din("maskc", [128, NMASK])
    y_out = nc.dram_tensor("y", [NTOK, D], F32, kind="ExternalOutput").ap()

    wb_memkv = dscr("wb_memkv", [2, D, 1024], BF16)
    wb_out = dscr("wb_out", [2, D, D], BF16)
    wb_mi = dscr("wb_mi", [2, D, DFF], BF16)
    wb_mo = dscr("wb_mo", [2, DFF, D], BF16)
    wb_ret = dscr("wb_ret", [D, W_RET], BF16)
    wb_na = dscr("wb_na", [D, W_NA], BF16)
    QT = [dscr("qt%d" % l, [28, 128, NTOK], BF16) for l in range(2)]
    VG = dscr("vg", [NTOK, 3072], BF16)
    CAT = dscr("cat", [16, 128, NTOK], BF16)
    X1 = dscr("x1", [NTOK, D], F32)

    es = ExitStack()
    with es:
        tr = TR(nc, es)

        sbn = [0]

        def sb(name, shape, dt, stack=None):
            sbn[0] += 1
            return (stack or es).enter_context(nc.sbuf_tensor("sb%d_%s" % (sbn[0], name), list(shape), dt))

        banks = [es.enter_context(nc.psum_tensor("ps%d" % i, [128, 512], F32)) for i in range(8)]
        bank_rr = [0]

        def next_bank():
            b = bank_rr[0]
            bank_rr[0] = (b + 1) % 8
            return b

        def next_bank4():
            b = bank_rr[0]
            if b % 4 != 0:
                b = (b + 3) // 4 * 4 % 8
            bank_rr[0] = (b + 4) % 8
            return [b, b + 1, b + 2, b + 3]

        def bank_bf(b):
            return banks[b][:].bitcast(BF16)

        cols = sb("cols", [128, 8], F32)
        ident = sb("ident", [128, 128], BF16)
        perm = sb("perm", [128, 128], BF16)
        with ExitStack() as ps0:
            miscf0 = sb("miscf0", [128, 2, 128], F32, ps0)
            tr.dma(miscf0[:], misc_in[:, 6:8, :], w=["miscf0"], sem="c0")
            tr.dma(cols[:], cols_in, w=["cols"], sem="c1")
            tr.op("dve", lambda e: e.tensor_copy(out=ident[:], in_=miscf0[:, 0, :]), r=["miscf0"], w=["ident"])
            tr.op("dve", lambda e: e.tensor_copy(out=perm[:], in_=miscf0[:, 1, :]), r=["miscf0"], w=["perm"])
            tr.barrier()
        eps_col = cols[:, 3:4]
        carry_col = cols[:, 2:3]

        def cast(dst, src, key):
            n = 1
            for d_ in src.shape:
                n *= d_
            fl = "a b -> (a b)"
            s_ = src.rearrange(fl).rearrange("(p a n) -> p a n", p=128, n=2048)
            d_ = dst.rearrange(fl).rearrange("(p a n) -> p a n", p=128, n=2048)
            tr.dma(d_, s_, w=[("const", key)], sem="cast_" + key, q="pool")

        def emit_casts():
            cast(wb_memkv[0], w_memkv[0], "memkv0")
            cast(wb_out[0], w_out[0], "out0")
            cast(wb_mi[0], w_mi[0], "mi0")
            cast(wb_mo[0], w_mo[0], "mo0")
            cast(wb_na, w_na, "na")
            cast(wb_memkv[1], w_memkv[1], "memkv1")
            cast(wb_out[1], w_out[1], "out1")
            cast(wb_mi[1], w_mi[1], "mi1")
            cast(wb_mo[1], w_mo[1], "mo1")

        NSLOT = 4
        wslots = []
        wrr = [0]

        def load_wblock(wb, key, row0, col0):
            s = wrr[0]
            wrr[0] = (s + 1) % NSLOT
            src = wb[row0:row0 + 1024, col0:col0 + 512].rearrange("(kc p) n -> p kc n", p=128)
            ck = ("const", "ret", col0 // 512, row0 // 1024) if key == "ret" else ("const", key)
            tr.dma(wslots[s][:], src, r=[ck], w=[("w", s)], sem="w%d" % s)
            return s

        class WStream:
            def __init__(self, items, ahead=NSLOT - 1):
                self.items = items
                self.pos = 0
                self.slots = []
                self.ahead = ahead

            def extend(self, items):
                self.items = self.items + list(items)

            def fill(self):
                while self.pos < len(self.items) and len(self.slots) < self.ahead:
                    self.slots.append(load_wblock(*self.items[self.pos]))
                    self.pos += 1

            def get(self):
                self.fill()
                s = self.slots.pop(0)
                return s

        def unit_items(wb, key, row0, nkc, col0):
            return [(wb, key, row0 + hb * 1024, col0) for hb in range(nkc // 8)]

        def unit_fm(ws, nkc, actT, evac):
            bs = next_bank4()
            nhb = nkc // 8
            slots = []
            for hb in range(nhb):
                s = ws.get()
                slots.append(s)

                def f(e, hb=hb, s=s):
                    ins = None
                    for c in range(4):
                        for k8 in range(8):
                            kc = hb * 8 + k8
                            ins = e.matmul(banks[bs[c]][:], lhsT=wslots[s][:, k8, c * 128:(c + 1) * 128],
                                           rhs=actT[:, kc, :], start=(kc == 0), stop=(kc == nkc - 1))
                    return ins
                tr.op("pe", f, r=[("w", s), "actT"], w=[("bank", b) for b in bs])
                ws.fill()
            for c in range(4):
                evac(c, bs[c])

        def unit_tm(ws, nkc, actT, evac, kc_off=0, akey="actT"):
            bs = next_bank4()
            nhb = nkc // 8
            for hb in range(nhb):
                s = ws.get()

                def f(e, hb=hb, s=s):
                    ins = None
                    for k8 in range(8):
                        kc = hb * 8 + k8
                        for su in range(4):
                            ins = e.matmul(banks[bs[su]][:], lhsT=actT[:, kc_off + kc, su * 128:(su + 1) * 128],
                                           rhs=wslots[s][:, k8, :], start=(kc == 0), stop=(kc == nkc - 1))
                    return ins
                tr.op("pe", f, r=[("w", s)] + (list(akey) if isinstance(akey, (list, tuple)) and akey and isinstance(akey[0], tuple) else [akey]), w=[("bank", b) for b in bs])
                ws.fill()
            for su in range(4):
                evac(su, bs[su])

        def evac_copy(dst_ap, b, wkeys):
            e = tr.alt()
            if e == "act":
                tr.op("act", lambda en: en.activation(out=dst_ap, in_=banks[b][:], func=AF.Copy),
                      r=[("bank", b)], w=wkeys)
            else:
                tr.op("dve", lambda en: en.tensor_copy(out=dst_ap, in_=banks[b][:]), r=[("bank", b)], w=wkeys)

        def rstd_from_ss(ss_ap, rs_ap, n, keys_r, keys_w, inv_n):
            tr.op("act", lambda e: e.activation(out=rs_ap, in_=ss_ap, func=AF.Sqrt, scale=inv_n, bias=eps_col[0:n] if n < 128 else eps_col),
                  r=keys_r + ["cols"], w=keys_w)
            tr.op("dve", lambda e: e.reciprocal(out=rs_ap, in_=rs_ap), r=keys_w, w=keys_w)

        def transposes_to_actT(h_ap, hkey, actT, su):
            for g in range(4):
                b = next_bank()
                tb = bank_bf(b)

                def f(e, g=g, tb=tb):
                    ins = None
                    for q4 in range(4):
                        kc = g * 4 + q4
                        ins = e.transpose(out=tb[:, q4 * 128:(q4 + 1) * 128], in_=h_ap[:, kc * 128:(kc + 1) * 128],
                                          identity=ident[:])
                    return ins
                tr.op("pe", f, r=[hkey, "ident"], w=[("bank", b)])
                src = tb[:, 0:512].rearrange("p (a t) -> p a t", a=4)
                dst = actT[:, g * 4:(g + 1) * 4, su * 128:(su + 1) * 128]
                e_ = tr.alt()
                if e_ == "act":
                    tr.op("act", lambda en, src=src, dst=dst: en.activation(out=dst, in_=src, func=AF.Copy),
                          r=[("bank", b)], w=["actT"])
                else:
                    tr.op("dve", lambda en, src=src, dst=dst: en.tensor_copy(out=dst, in_=src),
                          r=[("bank", b)], w=["actT"])

        memh = {}

        def alloc_mem(stack):
            memh["K"] = sb("memKT", [128, 4, 4, 256], BF16, stack)
            memh["V"] = sb("memV", [128, 4, 2, 4, 129], BF16, stack)
            tr.op("pool", lambda e: e.memset(memh["V"][:], 1.0), w=["memV"])

        def phase_mem(layer):
            with ExitStack() as ps:
                wkv = sb("wkv", [128, 16, 1024], BF16, ps)
                mt = sb("mt", [128, 2, D], F32, ps)
                mh = sb("mh", [128, D], BF16, ps)
                mT = sb("mT", [128, 16, 256], BF16, ps)
                gm = sb("gm", [128, D], F32, ps)
                junk = sb("mjunk", [128, D], BF16, ps)
                ssm = sb("ssm", [128, 2], F32, ps)
                key = "memkv%d" % layer
                tr.dma(wkv[:], wb_memkv[layer].rearrange("(kc p) n -> p kc n", p=128), r=[("const", key)], w=["wkv"], sem="mk")
                tr.dma(gm[:], mg_in[layer:layer + 1, :].broadcast_to([128, D]), w=["gm"], sem="mk3")
                for seg in range(4):
                    tr.dma(mt[:], mem_in[seg].rearrange("(c p) d -> p c d", p=128), w=["mt"], sem="mk2")
                    for c in range(2):
                        tr.op("act", lambda e, c=c: e.activation(out=junk[:], in_=mt[:, c, :], func=AF.Square,
                                                                 accum_out=ssm[:, c:c + 1]), r=["mt"], w=["mjunk", ("ssm", c)])
                    rstd_from_ss(ssm[:], ssm[:], 128, [("ssm", 0), ("ssm", 1)], [("ssm", 0), ("ssm", 1)], 1.0 / D)
                    for c in range(2):
                        tr.op("dve", lambda e, c=c: e.scalar_tensor_tensor(out=mh[:], in0=mt[:, c, :], scalar=ssm[:, c:c + 1],
                                                                          in1=gm[:], op0=ALU.mult, op1=ALU.mult),
                              r=["mt", ("ssm", 0), ("ssm", 1), "gm"], w=["mh"])
                        for g in range(4):
                            b = next_bank()
                            tb = bank_bf(b)

                            def f(e, g=g, tb=tb):
                                ins = None
                                for q4 in range(4):
                                    kc = g * 4 + q4
                                    ins = e.transpose(out=tb[:, q4 * 128:(q4 + 1) * 128], in_=mh[:, kc * 128:(kc + 1) * 128],
                                                      identity=ident[:])
                                return ins
                            tr.op("pe", f, r=["mh", "ident"], w=[("bank", b)])
                            tr.op("act", lambda en, tb=tb, g=g, c=c: en.activation(
                                out=mT[:, g * 4:(g + 1) * 4, c * 128:(c + 1) * 128],
                                in_=tb[:, 0:512].rearrange("p (a t) -> p a t", a=4), func=AF.Copy),
                                r=[("bank", b)], w=["mT"])
                    for h in range(4):
                        b = next_bank()

                        def f(e, h=h, b=b):
                            ins = None
                            for kc in range(16):
                                ins = e.matmul(banks[b][:, 0:256], lhsT=wkv[:, kc, h * 128:(h + 1) * 128], rhs=mT[:, kc, :],
                                               start=(kc == 0), stop=(kc == 15))
                            return ins
                        tr.op("pe", f, r=["wkv", "mT"], w=[("bank", b)])
                        tr.op("act", lambda en, h=h, b=b, seg=seg: en.activation(out=memh["K"][:, seg, h, :], in_=banks[b][:, 0:256],
                                                                               func=AF.Copy), r=[("bank", b)], w=["memKT"])
                    for c in range(2):
                        b = next_bank()

                        def f(e, c=c, b=b):
                            ins = None
                            for kc in range(16):
                                ins = e.matmul(banks[b][:], lhsT=mT[:, kc, c * 128:(c + 1) * 128], rhs=wkv[:, kc, 512:1024],
                                               start=(kc == 0), stop=(kc == 15))
                            return ins
                        tr.op("pe", f, r=["wkv", "mT"], w=[("bank", b)])
                        tr.op("dve", lambda en, c=c, b=b, seg=seg: en.tensor_copy(
                            out=memh["V"][:, seg, c, :, 0:128], in_=banks[b][:].rearrange("p (h e) -> p h e", h=4)),
                            r=[("bank", b)], w=["memV"])
                tr.barrier()

        def inproj_items(layer):
            if layer == 0:
                wb, key, nblk = wb_ret, "ret", 13
            else:
                wb, key, nblk = wb_na, "na", 10
            items = []
            for b_ in range(nblk):
                items += unit_items(wb, key, 0, 16, b_ * 512)
            return items

        def inproj(layer, tile_i, actT, stage_views, hook=None, ws=None):
            tok0 = tile_i * 512
            if layer == 0:
                wb, key, nblk = wb_ret, "ret", 13
                kinds = ["fm"] * 6 + ["tm"] * 6 + ["fm"]
            else:
                wb, key, nblk = wb_na, "na", 10
                kinds = ["fm"] * 6 + ["tm"] * 3 + ["fm"]
            items = []
            for b_ in range(nblk):
                items += unit_items(wb, key, 0, 16, b_ * 512)
            if ws is None:
                ws = WStream(items)
            ws.fill()
            fm_i = 0
            tm_i = 0
            for b_ in range(nblk):
                if hook is not None and b_ in hook:
                    hook[b_]()
                if kinds[b_] == "fm":
                    sv, skey = stage_views[fm_i % 2]
                    fm_i += 1
                    if layer == 0:
                        chunk0 = 4 * b_ if b_ < 6 else 24
                    else:
                        chunk0 = 4 * b_ if b_ < 6 else 24

                    def ev(c, bank, sv=sv, skey=skey):
                        evac_copy(sv[:, c, :], bank, [skey])
                    unit_fm(ws, 16, actT, ev)
                    dst = QT[layer][chunk0:chunk0 + 4, :, tok0:tok0 + 512].rearrange("c p t -> p c t")
                    tr.dma(dst, sv, r=[skey], sem="st_" + str(skey))
                else:
                    sv, skey = stage_views[2 + tm_i % 2]
                    tm_i += 1
                    col0 = (b_ - 6) * 512

                    def ev(su, bank, sv=sv, skey=skey):
                        evac_copy(sv[:, su, :], bank, [skey])
                    unit_tm(ws, 16, actT, ev)
                    dst = VG[tok0:tok0 + 512, col0:col0 + 512].rearrange("(s p) c -> p s c", p=128)
                    tr.dma(dst, sv, r=[skey], sem="st_" + str(skey))

        def phase_c(layer, a0=False):
            with ExitStack() as ps:
                del wslots[:]
                wslots.extend(sb("wslot%d" % i, [128, 8, 512], BF16, ps) for i in range(NSLOT))
                xt = sb("xt", [128, 4, D], F32, ps)
                xp = sb("xp", [128, 2, D], F32, ps)
                actT = sb("actT", [128, 16, 512], BF16, ps)
                hs = sb("hs", [128, 4, D], BF16, ps)
                catT = hs[:].rearrange("p s d -> p (s d)").rearrange("p (c t) -> p c t", c=16)
                HSK = [("hs", su) for su in range(4)]

                def X(ti, su):
                    if su < 2 and ti % 2 == 1:
                        return xp[:, su, :], ("xp", su)
                    return xt[:, su, :], ("xt", su)
                ga = sb("ga", [128, D], F32, ps)
                gb = ga if a0 else sb("gb", [128, D], F32, ps)
                gc = ga if a0 else sb("gc", [128, D], F32, ps)
                tmp2 = sb("tmp2", [128, 2, 512], F32, ps)
                junk = tmp2[:].rearrange("p a n -> p (a n)").bitcast(BF16)
                ss = sb("ss", [128, 16], F32, ps)
                ssp = sb("ssp", [128, 2, 4, 4], F32, ps)
                o = sb("o", [128, 4, D], F32, ps)
                stage_views = []
                for k_ in range(4):
                    v = o[:, k_, :].bitcast(BF16)[:, 0:2048].rearrange("p (a n) -> p a n", a=4)
                    stage_views.append((v, ("o", k_)))
                uT = sb("uT", [128, 32, 512] if not a0 else [128, 1, 512], BF16, ps)
                mixed = None if a0 else uT[:].rearrange("p a n -> p (a n)").bitcast(F32).rearrange("p (s d) -> p s d", s=4)
                xsrc = x_in if layer == 0 else X1

                def load_gain(buf, bkey, l_, gi):
                    tr.dma(buf[:], ng_in[l_, gi:gi + 1, :].broadcast_to([128, D]), w=[bkey], sem="g_" + bkey)

                def load_x(ti, sus=(0, 1, 2, 3)):
                    tok0 = ti * 512
                    for su in sus:
                        xa, xk = X(ti, su)
                        tr.dma(xa, xsrc[tok0 + su * 128:tok0 + (su + 1) * 128, :], w=[xk], sem="x%d" % su)

                def load_cat(ti):
                    tok0 = ti * 512
                    tr.dma(catT, CAT[:, :, tok0:tok0 + 512].rearrange("c p t -> p c t"), w=HSK, sem="cat")

                def sumsq(src_ap, rkeys, col):
                    tr.op("act", lambda e: e.activation(out=junk, in_=src_ap, func=AF.Square, accum_out=ss[:, col:col + 1]),
                          r=rkeys, w=[("tmp", 0), ("tmp", 1), ("ss", col)])

                def evac_block(dst, skey_fn, gbuf, gkey, pi):
                    def mk(cb, add_to=None):
                        def ev(su, bank):
                            d = dst[:, su, cb * 512:(cb + 1) * 512]
                            if add_to is None:
                                tr.op("act", lambda en: en.activation(out=d, in_=banks[bank][:], func=AF.Copy),
                                      r=[("bank", bank)], w=[skey_fn(su)])
                            else:
                                tr.op("dve", lambda en: en.tensor_tensor(out=d, in0=banks[bank][:], in1=d, op=ALU.add),
                                      r=[("bank", bank), skey_fn(su)], w=[skey_fn(su)])
                        return ev
                    return mk

                def finish_block(dst, skey_fn, gbuf, gkey, pi, cb):
                    for su in range(4):
                        d = dst[:, su, cb * 512:(cb + 1) * 512]
                        tpi = (cb * 4 + su) % 2
                        tr.op("act", lambda en, d=d, su=su, tpi=tpi: en.activation(out=tmp2[:, tpi, :], in_=d, func=AF.Square,
                                                                                 accum_out=ssp[:, pi, su, cb:cb + 1]),
                              r=[skey_fn(su)], w=[("tmp", tpi), ("ssp", pi, su, cb)])
                        tr.op("pool", lambda en, d=d: en.tensor_tensor(out=d, in0=d, in1=gbuf[:, cb * 512:(cb + 1) * 512], op=ALU.mult),
                              r=[skey_fn(su), gkey, ("ssp", pi, su, cb)], w=[skey_fn(su)])

                def residual_from(dst, skey_fn, pi, col0, ti):
                    pk = [("ssp", pi, su, cb) for su in range(4) for cb in range(4)]
                    kk = [("ss", col0 + su) for su in range(4)]
                    tr.op("dve", lambda e: e.tensor_tensor(out=ss[:, col0:col0 + 4], in0=ssp[:, pi, :, 0], in1=ssp[:, pi, :, 1], op=ALU.add),
                          r=pk, w=kk)
                    tr.op("dve", lambda e: e.tensor_tensor(out=ss[:, col0:col0 + 4], in0=ss[:, col0:col0 + 4], in1=ssp[:, pi, :, 2], op=ALU.add),
                          r=pk + kk, w=kk)
                    tr.op("dve", lambda e: e.tensor_tensor(out=ss[:, col0:col0 + 4], in0=ss[:, col0:col0 + 4], in1=ssp[:, pi, :, 3], op=ALU.add),
                          r=pk + kk, w=kk)
                    rstd_from_ss(ss[:, col0:col0 + 4], ss[:, col0:col0 + 4], 128, kk, kk, 1.0 / D)
                    for su in range(4):
                        xa, xk = X(ti, su)
                        tr.op("dve", lambda e, su=su, xa=xa: e.scalar_tensor_tensor(
                            out=xa, in0=dst[:, su, :], scalar=ss[:, col0 + su:col0 + su + 1], in1=xa,
                            op0=ALU.mult, op1=ALU.add), r=[skey_fn(su), xk] + kk, w=[xk])

                def norm_to_hs(gbuf, gkey, col0, ti):
                    for su in range(4):
                        xa, xk = X(ti, su)
                        sumsq(xa, [xk], col0 + su)
                        k1 = [("ss", col0 + su)]
                        rstd_from_ss(ss[:, col0 + su:col0 + su + 1], ss[:, col0 + su:col0 + su + 1], 128, k1, k1, 1.0 / D)
                        tr.op("dve", lambda e, su=su, xa=xa: e.scalar_tensor_tensor(
                            out=hs[:, su, :], in0=xa, scalar=ss[:, col0 + su:col0 + su + 1], in1=gbuf[:],
                            op0=ALU.mult, op1=ALU.mult), r=[xk, gkey] + k1, w=[("hs", su)])

                def hs_to_actT():
                    for su in range(4):
                        transposes_to_actT(hs[:, su, :], ("hs", su), actT, su)

                def outproj_items():
                    items = []
                    for cb in range(4):
                        items += unit_items(wb_out[layer], "out%d" % layer, 0, 16, cb * 512)
                    return items

                def mlp_items(half):
                    items = []
                    for b_ in range(8):
                        items += unit_items(wb_mi[layer], "mi%d" % layer, 0, 16, half * 4096 + b_ * 512)
                    for cb in range(4):
                        items += unit_items(wb_mo[layer], "mo%d" % layer, half * 4096, 32, cb * 512)
                    return items

                pws = WStream([])
                if a0:
                    for ti in range(NTILE):
                        pws.extend(inproj_items(0))
                else:
                    pws.extend(outproj_items())
                    for ti in range(NTILE):
                        pws.extend(mlp_items(0))
                        pws.extend(mlp_items(1))
                        if ti + 1 < NTILE:
                            pws.extend(outproj_items())
                        if layer == 0:
                            pws.extend(inproj_items(1))

                def outproj(ti):
                    ws = pws
                    ws.fill()
                    mk = evac_block(mixed, lambda su: ("uT", su), ga, "ga", 0)
                    for cb in range(4):
                        unit_tm(ws, 16, catT, mk(cb), akey=HSK)
                        finish_block(mixed, lambda su: ("uT", su), ga, "ga", 0, cb)

                def pre_mlp(ti):
                    residual_from(mixed, lambda su: ("uT", su), 0, 0, ti)
                    norm_to_hs(gb, "gb", 4, ti)

                if a0:
                    load_gain(ga, "ga", 0, 0)
                    load_x(0)
                    with ExitStack() as pc:
                        wf = [sb("wf%d" % i, [128, 8, 512], F32, pc) for i in range(2)]
                        wc = [sb("wc%d" % i, [128, 8, 512], BF16, pc) for i in range(2)]
                        pn = 0
                        for b_ in range(13):
                            for hb in range(2):
                                pp = pn % 2
                                pn += 1
                                rows = slice(hb * 1024, (hb + 1) * 1024)
                                csl_ = slice(b_ * 512, (b_ + 1) * 512)
                                tr.dma(wf[pp][:], w_ret[rows, csl_].rearrange("(kc p) n -> p kc n", p=128), w=[("wf", pp)], sem="cl%d" % pp)
                                tr.op("pool", lambda e, pp=pp: e.tensor_copy(out=wc[pp][:, 0:3, :], in_=wf[pp][:, 0:3, :]),
                                      r=[("wf", pp)], w=[("wc", pp, 0)])
                                tr.op("act", lambda e, pp=pp: e.activation(out=wc[pp][:, 3:5, :], in_=wf[pp][:, 3:5, :], func=AF.Copy),
                                      r=[("wf", pp)], w=[("wc", pp, 1)])
                                tr.op("dve", lambda e, pp=pp: e.tensor_copy(out=wc[pp][:, 5:8, :], in_=wf[pp][:, 5:8, :]),
                                      r=[("wf", pp)], w=[("wc", pp, 2)])
                                tr.dma(wb_ret[rows, csl_].rearrange("(kc p) n -> p kc n", p=128), wc[pp][:],
                                       r=[("wc", pp, 0), ("wc", pp, 1), ("wc", pp, 2)], w=[("const", "ret", b_, hb)], sem="cw%d" % pp)
                    emit_casts()
                    norm_to_hs(ga, "ga", 12, 0)
                    for ti in range(NTILE):
                        hs_to_actT()
                        hooks = {}
                        if ti + 1 < NTILE:
                            hooks = {0: (lambda ti=ti: load_x(ti + 1)), 3: (lambda ti=ti: norm_to_hs(ga, "ga", 12, ti + 1))}
                        inproj(0, ti, actT, stage_views, hook=hooks, ws=pws)
                    tr.barrier()
                    return
                load_gain(ga, "ga", layer, 1)
                load_gain(gb, "gb", layer, 2)
                load_gain(gc, "gc", layer, 3)
                load_x(0)
                load_cat(0)
                outproj(0)
                pre_mlp(0)
                if layer == 0:
                    load_gain(gb, "gb", 1, 0)
                for ti in range(NTILE):
                    tok0 = ti * 512
                    hs_to_actT()
                    for half in range(2):
                        ws = pws
                        ws.fill()
                        for b_ in range(8):
                            def ev(c, bank, b_=b_):
                                tpi = (b_ * 4 + c) % 2
                                tr.op("act", lambda en: en.activation(out=tmp2[:, tpi, :], in_=banks[bank][:], func=AF.Relu),
                                      r=[("bank", bank)], w=[("tmp", tpi)])
                                tr.op("pool", lambda en: en.tensor_tensor(out=uT[:, b_ * 4 + c, :], in0=tmp2[:, tpi, :],
                                                                          in1=tmp2[:, tpi, :], op=ALU.mult),
                                      r=[("tmp", tpi)], w=[("uT", 0), ("uT", 1), ("uT", 2), ("uT", 3)])
                            unit_fm(ws, 16, actT, ev)
                        if half == 1 and ti + 1 < NTILE:
                            load_cat(ti + 1)
                            load_x(ti + 1, (0, 1))
                        mk = evac_block(o, lambda su: ("o", su), gc, "gc", 1)
                        for cb in range(4):
                            unit_tm_u(ws, uT, mk(cb, add_to=(None if half == 0 else True)))
                            if half == 1:
                                finish_block(o, lambda su: ("o", su), gc, "gc", 1, cb)
                    if ti + 1 < NTILE:
                        outproj(ti + 1)
                    residual_from(o, lambda su: ("o", su), 1, 8, ti)
                    if layer == 0:
                        for su in range(4):
                            xa, xk = X(ti, su)
                            tr.dma(X1[tok0 + su * 128:tok0 + (su + 1) * 128, :], xa, r=[xk], sem="xo%d" % su)
                        norm_to_hs(gb, "gb", 12, ti)
                        hs_to_actT()
                        hooks = {}
                        if ti + 1 < NTILE:
                            def hook1(ti=ti):
                                load_gain(gb, "gb", 0, 2)
                                load_x(ti + 1, (2, 3))

                            def hook2(ti=ti):
                                pre_mlp(ti + 1)
                            hooks = {0: hook1, 3: hook2}
                        inproj(1, ti, actT, stage_views, hook=hooks, ws=pws)
                        if ti + 1 < NTILE:
                            load_gain(gb, "gb", 1, 0)
                    else:
                        for su in range(4):
                            xa, xk = X(ti, su)
                            tr.dma(y_out[tok0 + su * 128:tok0 + (su + 1) * 128, :], xa, r=[xk], sem="xo%d" % su)
                        if ti + 1 < NTILE:
                            load_x(ti + 1, (2, 3))
                            pre_mlp(ti + 1)
                tr.barrier(new_sems=True)

        def unit_tm_u(ws, uT, evac):
            bs = next_bank4()
            for hb in range(4):
                s = ws.get()

                def f(e, hb=hb, s=s):
                    ins = None
                    for k8 in range(8):
                        n = hb * 8 + k8
                        for su in range(4):
                            ins = e.matmul(banks[bs[su]][:], lhsT=uT[:, n, su * 128:(su + 1) * 128], rhs=wslots[s][:, k8, :],
                                           start=(n == 0), stop=(n == 31))
                    return ins
                tr.op("pe", f, r=[("w", s), ("uT", 0), ("uT", 1), ("uT", 2), ("uT", 3)], w=[("bank", b) for b in bs])
                ws.fill()
            for su in range(4):
                evac(su, bs[su])

        def pipeline(N, stages):
            S = len(stages)
            for step in range(N + S - 1):
                for si in range(S - 1, -1, -1):
                    n = step - si
                    if 0 <= n < N:
                        stages[si](n)

        RB = 4

        def phase_memattn(layer, ps, catst4):
            qm = sb("qm", [128, RB, 512], BF16, ps)
            pT = sb("pT", [128, RB, 2, 512], BF16, ps)
            rc = sb("rc", [128, RB, 4], F32, ps)
            om = sb("om", [128, RB, 4, 128], BF16, ps)
            st = {}

            def s0(i):
                ti, h = i // 4, i % 4
                seg = ti // 4
                tok0 = ti * 512
                r = i % RB
                tr.dma(qm[:, r, :], QT[layer][24 + h, :, tok0:tok0 + 512], w=[("qm", r)], sem="qm%d" % r)
                bs = [next_bank(), next_bank()]
                st[i] = bs
                for kc in range(2):
                    tr.op("pe", lambda e, kc=kc: e.matmul(banks[bs[kc]][:], lhsT=memh["K"][:, seg, h, kc * 128:(kc + 1) * 128],
                                                          rhs=qm[:, r, :], start=True, stop=True),
                          r=["memKT", ("qm", r)], w=[("bank", bs[kc])])

            def s1(i):
                r = i % RB
                bs = st[i]
                for kc in range(2):
                    tr.op("act", lambda e, kc=kc: e.activation(out=pT[:, r, kc, :], in_=banks[bs[kc]][:], func=AF.Exp, scale=SCALE),
                          r=[("bank", bs[kc])], w=[("pT", r)])

            def s2(i):
                ti, h = i // 4, i % 4
                seg = ti // 4
                r = i % RB
                bs = st[i]
                for pr in range(2):
                    def f(e, pr=pr):
                        ins = None
                        for s2_ in range(2):
                            su = pr * 2 + s2_
                            for kc in range(2):
                                ins = e.matmul(banks[bs[pr]][:, s2_ * 129:(s2_ + 1) * 129], lhsT=pT[:, r, kc, su * 128:(su + 1) * 128],
                                               rhs=memh["V"][:, seg, kc, h, :], start=(kc == 0), stop=(kc == 1))
                        return ins
                    tr.op("pe", f, r=[("pT", r), "memV"], w=[("bank", bs[pr])])

            def s3(i):
                r = i % RB
                bs = st[i]
                for pr in range(2):
                    v = banks[bs[pr]][:, 0:258].rearrange("p (s c) -> p s c", c=129)
                    tr.op("dve", lambda e, v=v, pr=pr: e.reciprocal(out=rc[:, r, pr * 2:pr * 2 + 2].unsqueeze(2), in_=v[:, :, 128:129]),
                          r=[("bank", bs[pr])], w=[("rc", r)])
                    tr.op("dve", lambda e, v=v, pr=pr: e.tensor_tensor(out=om[:, r, pr * 2:pr * 2 + 2, :], in0=v[:, :, 0:128],
                                                                      in1=rc[:, r, pr * 2:pr * 2 + 2].unsqueeze(2).to_broadcast([128, 2, 128]),
                                                                      op=ALU.mult),
                          r=[("bank", bs[pr]), ("rc", r)], w=[("om", r)])

            def s4(i):
                ti, h = i // 4, i % 4
                tok0 = ti * 512
                r = i % RB
                bs = st.pop(i)
                ci = i % 4
                for pr in range(2):
                    tb = bank_bf(bs[pr])

                    def f(e, pr=pr, tb=tb):
                        ins = None
                        for s2_ in range(2):
                            ins = e.transpose(out=tb[:, 768 + s2_ * 128:768 + (s2_ + 1) * 128], in_=om[:, r, pr * 2 + s2_, :], identity=ident[:])
                        return ins
                    tr.op("pe", f, r=[("om", r), "ident"], w=[("bank", bs[pr])])
                    en_ = tr.alt()
                    if en_ == "act":
                        tr.op("act", lambda e, pr=pr, tb=tb: e.activation(out=catst4[:, ci, pr * 256:(pr + 1) * 256], in_=tb[:, 768:1024], func=AF.Copy),
                              r=[("bank", bs[pr])], w=[("catst", ci)])
                    else:
                        tr.op("dve", lambda e, pr=pr, tb=tb: e.tensor_copy(out=catst4[:, ci, pr * 256:(pr + 1) * 256], in_=tb[:, 768:1024]),
                              r=[("bank", bs[pr])], w=[("catst", ci)])
                tr.dma(CAT[12 + h, :, tok0:tok0 + 512], catst4[:, ci, :], r=[("catst", ci)], sem="cs%d" % ci)

            pipeline(NTILE * 4, [s0, s1, s2, s3, s4])

        def phase_ret(catst):
            with ExitStack() as ps:
                qT2 = [sb("qT%d" % i, [128, NTOK], BF16, ps) for i in range(2)]
                kT2 = [sb("kT%d" % i, [128, NTOK], BF16, ps) for i in range(2)]
                Kt = sb("Kt", [128, NSUB, 128], BF16, ps)
                V2 = [sb("V%d" % i, [128, NSUB, 128], BF16, ps) for i in range(2)]
                Vs = sb("Vs", [128, NSUB, 128], BF16, ps)
                Sb = sb("Sb", [128, NSUB, 128], BF16, ps)
                rc_ = sb("rotc", [128, 2, 512], F32, ps)
                rs_ = sb("rots", [128, 2, 512], F32, ps)
                ta = sb("ta", [128, 2, 512], F32, ps)
                tb_ = sb("tbb", [128, 2, 512], F32, ps)
                dtab = sb("dtab", [128, 24], F32, ps)
                lg = sb("lg", [128, 24], F32, ps)
                hd = sb("hd", [128, 6, 128], F32, ps)
                hc = sb("hc", [128, 8], F32, ps)
                Sf = sb("Sf", [128, 2, 128], F32, ps)
                Sbk = sb("Sbk", [128, 2, 128], F32, ps)
                Sfb = sb("Sfb", [128, RB, 128], BF16, ps)
                qfb = sb("qfb", [128, 3, 2, 512], BF16, ps)
                gt = sb("gt", [128, 2, 1024], BF16, ps)
                sg = sb("sg", [128, 2, 1024], F32, ps)
                PTm = sb("PTm", [128, RB, 128], BF16, ps)
                tok = sb("tok", [128, RB, 128], BF16, ps)
                nst = sb("nst", [128, RB], F32, ps)
                junk = sb("rjunk", [128, 128], F32, ps)
                miscf = sb("miscf", [128, 6, 128], F32, ps)
                tr.dma(miscf[:], misc_in[:, 0:6, :], w=["miscf"], sem="c0")

                tr.dma(dtab[:], decay_in.broadcast_to([128, 24]), w=["dtab"], sem="dt")
                tr.op("act", lambda e: e.activation(out=dtab[:], in_=dtab[:], func=AF.Exp, scale=-float(np.log(2.0))),
                      r=["dtab"], w=["dtab"])
                tr.op("dve", lambda e: e.tensor_scalar(out=lg[:], in0=dtab[:], scalar1=1.0 / 9.0, scalar2=None, op0=ALU.mult),
                      r=["dtab"], w=["lg"])
                for kk in range(8, 0, -1):
                    tr.op("dve", lambda e, kk=kk: e.scalar_tensor_tensor(out=lg[:], in0=lg[:], scalar=1.0 / kk, in1=dtab[:],
                                                                        op0=ALU.add, op1=ALU.mult), r=["lg", "dtab"], w=["lg"])
                tr.op("dve", lambda e: e.tensor_scalar(out=lg[:], in0=lg[:], scalar1=-1.0, scalar2=None, op0=ALU.mult),
                      r=["lg"], w=["lg"])
                def head_loads(h):
                    hp = h % 2
                    tr.dma(qT2[hp][:], QT[0][h], w=[("qT", hp, b_) for b_ in range(16)], sem="lq%d" % hp)
                    tr.dma(kT2[hp][:], QT[0][12 + h], w=[("kT", hp, b_) for b_ in range(16)], sem="lk%d" % hp)
                    for q4 in range(4):
                        tr.dma(V2[hp][:, q4 * 16:(q4 + 1) * 16, :],
                               VG[q4 * 2048:(q4 + 1) * 2048, h * 128:(h + 1) * 128].rearrange("(n p) e -> p n e", p=128),
                               w=[("V", hp)], sem="lv%d" % hp)

                def rotary_block(hh, blk):
                    hpp = hh % 2
                    pi = blk % 2
                    tsl = slice(blk * 512, (blk + 1) * 512)
                    tr.dma(rc_[:, pi, :], rotc_in[:, tsl], w=[("rc", pi)], sem="rc%d" % pi)
                    tr.dma(rs_[:, pi, :], rots_in[:, tsl], w=[("rs", pi)], sem="rs%d" % pi)
                    for qk, (nm0, T_) in enumerate((("qT", qT2[hpp]), ("kT", kT2[hpp]))):
                        ti_ = qk
                        nm = (nm0, hpp)
                        b = next_bank()
                        tr.op("pe", lambda e, b=b, T_=T_: e.matmul(banks[b][:], lhsT=perm[:], rhs=T_[:, tsl], start=True, stop=True),
                              r=[nm + (blk,), "perm"], w=[("bank", b)])
                        tr.op("dve", lambda e, T_=T_: e.tensor_tensor(out=ta[:, ti_, :], in0=T_[:, tsl], in1=rc_[:, pi, :], op=ALU.mult),
                              r=[nm + (blk,), ("rc", pi)], w=[("ta", ti_)])
                        tr.op("dve", lambda e, b=b: e.tensor_tensor(out=tb_[:, ti_, :], in0=banks[b][:], in1=rs_[:, pi, :], op=ALU.mult),
                              r=[("bank", b), ("rs", pi)], w=[("tb", ti_)])
                        tr.op("pool", lambda e, T_=T_: e.tensor_tensor(out=T_[:, tsl], in0=ta[:, ti_, :], in1=tb_[:, ti_, :], op=ALU.add),
                              r=[("ta", ti_), ("tb", ti_)], w=[nm + (blk,)])

                head_loads(0)
                for h in range(12):
                    lf = lg[:, h:h + 1]
                    lb = lg[:, 12 + h:13 + h]
                    hp = h % 2
                    qT, kT, V = qT2[hp], kT2[hp], V2[hp]
                    Vk = ("V", hp)
                    tr.op("act", lambda e: e.activation(out=hd[:, 0, :], in_=miscf[:, 0, :], func=AF.Exp, scale=lf),
                          r=["lg", "miscf"], w=[("hd", 0)])
                    tr.op("act", lambda e: e.activation(out=hd[:, 1, :], in_=miscf[:, 1, :], func=AF.Exp, scale=lb),
                          r=["lg", "miscf"], w=[("hd", 1)])
                    tr.op("dve", lambda e: e.tensor_tensor(out=hd[:, 0, :], in0=hd[:, 0, :], in1=miscf[:, 2, :], op=ALU.mult),
                          r=[("hd", 0), "miscf"], w=[("hd", 0)])
                    tr.op("dve", lambda e: e.tensor_tensor(out=hd[:, 1, :], in0=hd[:, 1, :], in1=miscf[:, 3, :], op=ALU.mult),
                          r=[("hd", 1), "miscf"], w=[("hd", 1)])
                    tr.op("dve", lambda e: e.tensor_tensor(out=hd[:, 2, :], in0=hd[:, 0, :], in1=hd[:, 1, :], op=ALU.add),
                          r=[("hd", 0), ("hd", 1)], w=[("hd", 2)])
                    tr.op("act", lambda e: e.activation(out=hd[:, 3, :], in_=miscf[:, 4, :], func=AF.Exp, scale=lf),
                          r=["lg", "miscf"], w=[("hd", 3)])
                    tr.op("act", lambda e: e.activation(out=hd[:, 4, :], in_=miscf[:, 5, :], func=AF.Exp, scale=lb),
                          r=["lg", "miscf"], w=[("hd", 4)])
                    tr.op("act", lambda e: e.activation(out=hc[:, 0:1], in_=cols[:, 0:1], func=AF.Exp, scale=lf),
                          r=["lg", "cols"], w=["hc"])
                    tr.op("act", lambda e: e.activation(out=hc[:, 1:2], in_=cols[:, 1:2], func=AF.Exp, scale=lb),
                          r=["lg", "cols", "hc"], w=["hc"])
                    tr.op("act", lambda e: e.activation(out=hc[:, 2:3], in_=lf, func=AF.Exp, scale=128.0), r=["lg", "hc"], w=["hc"])
                    tr.op("act", lambda e: e.activation(out=hc[:, 3:4], in_=lb, func=AF.Exp, scale=128.0), r=["lg", "hc"], w=["hc"])
                    tr.op("dve", lambda e: e.tensor_scalar(out=hc[:, 0:2], in0=hc[:, 0:2], scalar1=float(SCALE), scalar2=None,
                                                           op0=ALU.mult), r=["hc"], w=["hc"])
                    kdf, kdb, cdf, cdb = hc[:, 0:1], hc[:, 1:2], hc[:, 2:3], hc[:, 3:4]
                    if h == 0:
                        for blk in range(16):
                            rotary_block(0, blk)
                    for g in range(16):
                        b = next_bank()
                        tbk = bank_bf(b)

                        def f(e, g=g, tbk=tbk):
                            ins = None
                            for q4 in range(4):
                                n = g * 4 + q4
                                ins = e.transpose(out=tbk[:, q4 * 128:(q4 + 1) * 128], in_=kT[:, n * 128:(n + 1) * 128], identity=ident[:])
                            return ins
                        tr.op("pe", f, r=[("kT", hp, g), "ident"], w=[("bank", b)])
                        en_ = tr.alt()
                        src = tbk[:, 0:512].rearrange("p (a d) -> p a d", a=4)
                        if en_ == "act":
                            tr.op("act", lambda e, g=g, src=src: e.activation(out=Kt[:, g * 4:(g + 1) * 4, :], in_=src, func=AF.Copy),
                                  r=[("bank", b)], w=[("Kt", g)])
                        else:
                            tr.op("dve", lambda e, g=g, src=src: e.tensor_copy(out=Kt[:, g * 4:(g + 1) * 4, :], in_=src),
                                  r=[("bank", b)], w=[("Kt", g)])
                    if h + 1 < 12:
                        head_loads(h + 1)
                    tr.op("dve", lambda e: e.tensor_scalar(out=Vs[:].rearrange("p n e -> p (n e)"), in0=V[:].rearrange("p n e -> p (n e)"),
                                                           scalar1=kdb, scalar2=None, op0=ALU.mult), r=[Vk, "hc"], w=["Vs"])
                    tr.op("pool", lambda e: e.memset(Sbk[:, (NSUB - 1) % 2, :], 0.0), w=[("Sbk", (NSUB - 1) % 2)])
                    tr.op("pool", lambda e: e.memset(Sb[:, NSUB - 1, :], 0.0), w=[("Sb", NSUB - 1)])
                    for n in range(NSUB - 2, -1, -1):
                        b = next_bank()
                        pw, pr_ = n % 2, (n + 1) % 2
                        tr.op("pe", lambda e, b=b, n=n: e.matmul(banks[b][:, 0:128], lhsT=Kt[:, n + 1, :], rhs=Vs[:, n + 1, :], start=True, stop=True),
                              r=[("Kt", (n + 1) // 4), "Vs"], w=[("bank", b)])
                        tr.op("dve", lambda e, b=b, pw=pw, pr_=pr_: e.scalar_tensor_tensor(out=Sbk[:, pw, :], in0=Sbk[:, pr_, :], scalar=cdb, in1=banks[b][:, 0:128],
                                                                          op0=ALU.mult, op1=ALU.add), r=[("Sbk", pr_), "hc", ("bank", b)], w=[("Sbk", pw)])
                        if (n + 1) % 16 == 0:
                            tr.op("dve", lambda e, pw=pw: e.tensor_scalar(out=Sbk[:, pw, :], in0=Sbk[:, pw, :], scalar1=carry_col, scalar2=None, op0=ALU.mult),
                                  r=[("Sbk", pw), "cols"], w=[("Sbk", pw)])
                        tr.op("act", lambda e, n=n, pw=pw: e.activation(out=Sb[:, n, :], in_=Sbk[:, pw, :], func=AF.Copy), r=[("Sbk", pw)], w=[("Sb", n)])
                    tr.op("dve", lambda e: e.tensor_scalar(out=Vs[:].rearrange("p n e -> p (n e)"), in0=V[:].rearrange("p n e -> p (n e)"),
                                                           scalar1=kdf, scalar2=None, op0=ALU.mult), r=[Vk, "hc"], w=["Vs"])
                    tr.op("pool", lambda e: e.memset(Sf[:, 1, :], 0.0), w=[("Sf", 1)])
                    tr.op("pool", lambda e: e.memset(Sfb[:, 0, :], 0.0), w=[("Sfb", 0)])
                    st = {}

                    def gate_group(G, h=h):
                        gp = G % 2
                        tr.dma(gt[:, gp, :].rearrange("p (n e) -> p n e", n=8),
                               VG[G * 1024:(G + 1) * 1024, 1536 + h * 128:1536 + (h + 1) * 128].rearrange("(n p) e -> p n e", p=128),
                               w=[("gt", gp)], sem="gt%d" % gp)
                        tr.op("act", lambda e: e.activation(out=sg[:, gp, :], in_=gt[:, gp, :], func=AF.Silu), r=[("gt", gp)], w=[("sg", gp)])

                    gate_group(0)

                    def f0(n, h=h):
                        g = n // 4
                        gi = g % 3
                        if n % 8 == 4 and n // 8 + 1 < 8:
                            gate_group(n // 8 + 1)
                        if n % 4 == 2 and h + 1 < 12:
                            rotary_block(h + 1, n // 4)
                        if n % 4 == 0:
                            gsl = slice(g * 512, (g + 1) * 512)
                            tr.op("pool", lambda e: e.tensor_tensor(out=qfb[:, gi, 0, :].rearrange("p (n i) -> p n i", n=4),
                                                                    in0=qT[:, gsl].rearrange("p (n i) -> p n i", n=4),
                                                                    in1=hd[:, 3:4, :].to_broadcast([128, 4, 128]), op=ALU.mult),
                                  r=[("qT", hp, g), ("hd", 3)], w=[("qfb", gi, 0)])
                            tr.op("pool", lambda e: e.tensor_tensor(out=qfb[:, gi, 1, :].rearrange("p (n i) -> p n i", n=4),
                                                                    in0=qT[:, gsl].rearrange("p (n i) -> p n i", n=4),
                                                                    in1=hd[:, 4:5, :].to_broadcast([128, 4, 128]), op=ALU.mult),
                                  r=[("qT", hp, g), ("hd", 4)], w=[("qfb", gi, 1)])
                        csl = slice(n * 128, (n + 1) * 128)
                        bA = next_bank()
                        bY = next_bank()
                        st[n] = (bA, bY)

                        def f(e):
                            e.matmul(banks[bA][:, 0:128], lhsT=kT[:, csl], rhs=qT[:, csl], start=True, stop=True)
                            return e.matmul(banks[bA][:, 128:256], lhsT=Kt[:, n, :], rhs=Vs[:, n, :], start=True, stop=True)
                        tr.op("pe", f, r=[("kT", hp, g), ("qT", hp, g), ("Kt", g), "Vs"], w=[("bank", bA)])

                    def f1(n):
                        bA, bY = st[n]
                        r_ = n % RB
                        tr.op("dve", lambda e: e.tensor_tensor(out=PTm[:, r_, :], in0=banks[bA][:, 0:128], in1=hd[:, 2, :], op=ALU.mult),
                              r=[("bank", bA), ("hd", 2)], w=[("PTm", r_)])
                        pw, pr_ = n % 2, (n + 1) % 2
                        tr.op("dve", lambda e: e.scalar_tensor_tensor(out=Sf[:, pw, :], in0=Sf[:, pr_, :], scalar=cdf, in1=banks[bA][:, 128:256],
                                                                     op0=ALU.mult, op1=ALU.add), r=[("Sf", pr_), "hc", ("bank", bA)], w=[("Sf", pw)])
                        if (n + 1) % 16 == 0 and n + 1 < NSUB:
                            tr.op("dve", lambda e: e.tensor_scalar(out=Sf[:, pw, :], in0=Sf[:, pw, :], scalar1=carry_col, scalar2=None, op0=ALU.mult),
                                  r=[("Sf", pw), "cols"], w=[("Sf", pw)])
                        if n + 1 < NSUB:
                            rn = (n + 1) % RB
                            tr.op("pool", lambda e: e.tensor_copy(out=Sfb[:, rn, :], in_=Sf[:, pw, :]), r=[("Sf", pw)], w=[("Sfb", rn)])

                    def f2(n):
                        bA, bY = st[n]
                        r_ = n % RB
                        gi = (n // 4) % 3
                        lsl = slice((n % 4) * 128, (n % 4 + 1) * 128)

                        def fy(e):
                            e.matmul(banks[bY][:, 0:128], lhsT=PTm[:, r_, :], rhs=V[:, n, :], start=True, stop=False)
                            e.matmul(banks[bY][:, 0:128], lhsT=qfb[:, gi, 0, lsl], rhs=Sfb[:, r_, :], start=False, stop=False)
                            return e.matmul(banks[bY][:, 0:128], lhsT=qfb[:, gi, 1, lsl], rhs=Sb[:, n, :], start=False, stop=True)
                        tr.op("pe", fy, r=[("PTm", r_), Vk, ("qfb", gi, 0), ("qfb", gi, 1), ("Sfb", r_), ("Sb", n)], w=[("bank", bY)])

                    def f3(n):
                        bA, bY = st[n]
                        r_ = n % RB
                        gi = (n // 8) % 2
                        lsl = slice((n % 8) * 128, (n % 8 + 1) * 128)
                        tr.op("act", lambda e: e.activation(out=junk[:], in_=banks[bY][:, 0:128], func=AF.Square, accum_out=nst[:, r_:r_ + 1]),
                              r=[("bank", bY)], w=["rjunk", ("nst", r_)])
                        rstd_from_ss(nst[:, r_:r_ + 1], nst[:, r_:r_ + 1], 128, [("nst", r_)], [("nst", r_)], 1.0 / 128.0)
                        tr.op("dve", lambda e: e.scalar_tensor_tensor(out=tok[:, r_, :], in0=banks[bY][:, 0:128], scalar=nst[:, r_:r_ + 1],
                                                                     in1=sg[:, gi, lsl], op0=ALU.mult, op1=ALU.mult),
                              r=[("bank", bY), ("nst", r_), ("sg", gi)], w=[("tok", r_)])

                    def f4(n, h=h):
                        bA, bY = st.pop(n)
                        r_ = n % RB
                        g = n // 4
                        ci = g % 4
                        lsl = slice((n % 4) * 128, (n % 4 + 1) * 128)
                        tbt = bank_bf(bY)
                        tr.op("pe", lambda e: e.transpose(out=tbt[:, 512:640], in_=tok[:, r_, :], identity=ident[:]),
                              r=[("tok", r_), "ident"], w=[("bank", bY)])
                        tr.op("dve", lambda e: e.tensor_copy(out=catst[:, ci, lsl], in_=tbt[:, 512:640]),
                              r=[("bank", bY)], w=[("catst", ci)])
                        if n % 4 == 3:
                            tr.dma(CAT[h, :, g * 512:(g + 1) * 512], catst[:, ci, :], r=[("catst", ci)], sem="cs%d" % ci)

                    pipeline(NSUB, [f0, f1, f2, f3, f4])
                tr.barrier(new_sems=True)

        def phase_na():
            with ExitStack() as ps:
                catst = sb("catst", [128, 4, 512], BF16, ps)
                with ExitStack() as ps2:
                    phase_memattn(1, ps2, catst)
                    tr.barrier()
                qT2 = [sb("qT%d" % i, [128, NTOK], BF16, ps) for i in range(2)]
                kT2 = [sb("kT%d" % i, [128, NTOK], BF16, ps) for i in range(2)]
                Va2 = [sb("Va%d" % i, [128, NSUB, 129], BF16, ps) for i in range(2)]
                Et2 = [sb("Et%d" % i, [128, 12, 128], F32, ps) for i in range(2)]
                maskc = sb("maskc", [128, NMASK], F32, ps)
                E1 = sb("E1", [128, RB, 8, 128], F32, ps)
                PT = sb("PT", [128, RB, 8, 128], BF16, ps)
                rc = sb("rcn", [128, RB], F32, ps)
                ot = sb("ot", [128, RB, 128], BF16, ps)
                tr.dma(maskc[:], maskc_in, w=["maskc"], sem="mc")
                for i_ in range(2):
                    tr.op("pool", lambda e, i_=i_: e.memset(Va2[i_][:], 1.0), w=[("Va", i_)])

                def head_loads(h):
                    hp = h % 2
                    tr.dma(qT2[hp][:], QT[1][h], w=[("qT", hp)], sem="lq%d" % hp)
                    tr.dma(kT2[hp][:], QT[1][12 + h], w=[("kT", hp)], sem="lk%d" % hp)
                    for q4 in range(4):
                        tr.dma(Va2[hp][:, q4 * 16:(q4 + 1) * 16, 0:128],
                               VG[q4 * 2048:(q4 + 1) * 2048, h * 128:(h + 1) * 128].rearrange("(n p) e -> p n e", p=128),
                               w=[("Va", hp)], sem="lv%d" % hp)
                    tr.dma(Et2[hp][:], btab_in[h], w=[("Et", hp)], sem="bt%d" % hp)

                head_loads(0)
                for h in range(12):
                    hp = h % 2
                    qT, kT, Va, Et = qT2[hp], kT2[hp], Va2[hp], Et2[hp]
                    qk_, kk_, vk_, ek_ = ("qT", hp), ("kT", hp), ("Va", hp), ("Et", hp)
                    tr.op("act", lambda e: e.activation(out=Et[:], in_=Et[:], func=AF.Exp), r=[ek_], w=[ek_])
                    if h + 1 < 12:
                        head_loads(h + 1)
                    st = {}

                    def jlist(T):
                        s_, t_ = T // 16, T % 16
                        interior = 2 <= t_ <= 13
                        return (list(range(-2, 3)) if interior else _BPLAN[(s_, t_)]), interior

                    def g0(T):
                        js, interior = jlist(T)
                        qsl = slice(T * 128, (T + 1) * 128)
                        b1 = next_bank()
                        b2 = next_bank()
                        st[T] = (b1, b2)
                        g1_, g2_ = js[0:4], js[4:8]

                        def f(e):
                            ins = None
                            for gi_, j in enumerate(g1_):
                                T2 = T + j
                                ins = e.matmul(banks[b1][:, gi_ * 128:(gi_ + 1) * 128], lhsT=kT[:, T2 * 128:(T2 + 1) * 128],
                                               rhs=qT[:, qsl], start=True, stop=True)
                            return ins
                        tr.op("pe", f, r=[kk_, qk_], w=[("bank", b1)])

                        def f2_(e):
                            ins = None
                            for gi_, j in enumerate(g2_):
                                T2 = T + j
                                ins = e.matmul(banks[b2][:, gi_ * 128:(gi_ + 1) * 128], lhsT=kT[:, T2 * 128:(T2 + 1) * 128],
                                               rhs=qT[:, qsl], start=True, stop=True)
                            return ins
                        if g2_:
                            tr.op("pe", f2_, r=[kk_, qk_], w=[("bank", b2)])

                    def g1(T):
                        js, interior = jlist(T)
                        b1, b2 = st[T]
                        r_ = T % RB
                        n1 = len(js[0:4])
                        n2 = len(js[4:8])
                        tr.op("act", lambda e: e.activation(out=E1[:, r_, 0:n1, :], in_=banks[b1][:, 0:n1 * 128].rearrange("p (a q) -> p a q", a=n1),
                                                            func=AF.Exp, scale=SCALE), r=[("bank", b1)], w=[("E1", r_)])
                        if n2:
                            tr.op("act", lambda e: e.activation(out=E1[:, r_, 4:4 + n2, :], in_=banks[b2][:, 0:n2 * 128].rearrange("p (a q) -> p a q", a=n2),
                                                                func=AF.Exp, scale=SCALE), r=[("bank", b2)], w=[("E1", r_)])

                    def g2(T):
                        js, interior = jlist(T)
                        s_, t_ = T // 16, T % 16
                        r_ = T % RB
                        if interior:
                            en_ = "dve"
                            tr.op(en_, lambda e: e.tensor_tensor(out=PT[:, r_, 0:5, :], in0=E1[:, r_, 0:5, :], in1=Et[:, 0:5, :], op=ALU.mult),
                                  r=[("E1", r_), ek_], w=[("PT", r_)])
                        else:
                            for ji, j in enumerate(js):
                                for b_ in (0, 1):
                                    mi = _MASKIDX[(s_, t_, j, b_)]
                                    hs_ = slice(b_ * 64, (b_ + 1) * 64)
                                    tr.op("dve", lambda e, ji=ji, j=j, mi=mi, hs_=hs_: e.scalar_tensor_tensor(
                                        out=PT[:, r_, ji, hs_], in0=E1[:, r_, ji, hs_], scalar=maskc[:, mi:mi + 1],
                                        in1=Et[:, 5 + j + 3, hs_], op0=ALU.mult, op1=ALU.mult),
                                        r=[("E1", r_), ek_, "maskc"], w=[("PT", r_)])

                    def g3(T):
                        js, interior = jlist(T)
                        b1, b2 = st[T]
                        r_ = T % RB

                        def fo(e):
                            ins = None
                            for ji, j in enumerate(js):
                                ins = e.matmul(banks[b2][:, 256:385], lhsT=PT[:, r_, ji, :], rhs=Va[:, T + j, :],
                                               start=(ji == 0), stop=(ji == len(js) - 1))
                            return ins
                        tr.op("pe", fo, r=[("PT", r_), vk_], w=[("bank", b2)])

                    def g4(T):
                        b1, b2 = st[T]
                        r_ = T % RB
                        tr.op("dve", lambda e: e.reciprocal(out=rc[:, r_:r_ + 1], in_=banks[b2][:, 384:385]),
                              r=[("bank", b2)], w=[("rcn", r_)])
                        tr.op("act", lambda e: e.activation(out=ot[:, r_, :], in_=banks[b2][:, 256:384], func=AF.Copy, scale=rc[:, r_:r_ + 1]),
                              r=[("bank", b2), ("rcn", r_)], w=[("ot", r_)])

                    def g5(T, h=h):
                        b1, b2 = st.pop(T)
                        r_ = T % RB
                        tbt = bank_bf(b2)
                        tr.op("pe", lambda e: e.transpose(out=tbt[:, 800:928], in_=ot[:, r_, :], identity=ident[:]),
                              r=[("ot", r_), "ident"], w=[("bank", b2)])
                        g = T // 4
                        ci = g % 4
                        lsl = slice((T % 4) * 128, (T % 4 + 1) * 128)
                        tr.op("dve", lambda e: e.tensor_copy(out=catst[:, ci, lsl], in_=tbt[:, 800:928]),
                              r=[("bank", b2)], w=[("catst", ci)])
                        if T % 4 == 3:
                            tr.dma(CAT[h, :, g * 512:(g + 1) * 512], catst[:, ci, :], r=[("catst", ci)], sem="cs%d" % ci)

                    pipeline(NSUB, [g0, g1, g2, g3, g4, g5])
                tr.barrier(new_sems=True)

        phases = os.environ.get("MK_PHASES", "all")
        phase_c(0, a0=True)
        if phases != "a0":
            with ExitStack() as outer:
                catst0 = sb("catst", [128, 4, 512], BF16, outer)
                with ExitStack() as lb:
                    alloc_mem(lb)
                    phase_mem(0)
                    with ExitStack() as ps2:
                        phase_memattn(0, ps2, catst0)
                        tr.barrier()
                phase_ret(catst0)
            if phases != "b0":
                phase_c(0)
                if phases != "c0":
                    with ExitStack() as lb:
                        alloc_mem(lb)
                        phase_mem(1)
                        phase_na()
                    if phases != "b1":
                        phase_c(1)
        tr.barrier()
    return nc


_NC_CACHE = {}


def _prep_inputs(x_prompt, x_sample, mem_prompt, mem_sample, norm_gain, mem_norm_gain, w_mem_kv, w_out,
                 w_mlp_in, w_mlp_out, w_in_ret, ret_decay, w_in_na, na_rpb):
    f = lambda a: np.ascontiguousarray(np.asarray(a, dtype=np.float32))
    shared = {
        "norm_gain": f(norm_gain), "mem_norm_gain": f(mem_norm_gain), "w_mem_kv": f(w_mem_kv), "w_out": f(w_out),
        "w_mlp_in": f(w_mlp_in), "w_mlp_out": f(w_mlp_out), "w_in_ret": f(w_in_ret)[0], "w_in_na": f(w_in_na)[0],
        "ret_decay": f(ret_decay).reshape(1, 24), "btab": _na_bias_tables(f(na_rpb)[0]),
    }
    consts = {False: _const_tables(False), True: _const_tables(True)}
    xp = f(x_prompt)
    xs = f(x_sample)
    mp = f(mem_prompt)
    ms = f(mem_sample)
    in_maps = []
    for c in range(8):
        samp = c >= 4
        m = dict(shared)
        if not samp:
            m["x"] = xp[4 * c:4 * c + 4].reshape(NTOK, D)
            m["mem"] = mp[4 * c:4 * c + 4]
        else:
            m["x"] = xs[c - 4]
            m["mem"] = np.ascontiguousarray(np.broadcast_to(ms[c - 4][None], (4, 256, D)))
        m.update(consts[samp])
        in_maps.append(m)
    return in_maps


def kernel(x_prompt, x_sample, mem_prompt, mem_sample, norm_gain, mem_norm_gain, w_mem_kv, w_out,
           w_mlp_in, w_mlp_out, w_in_ret, ret_decay, w_in_na, na_rpb):
    in_maps = _prep_inputs(x_prompt, x_sample, mem_prompt, mem_sample, norm_gain, mem_norm_gain, w_mem_kv, w_out,
                           w_mlp_in, w_mlp_out, w_in_ret, ret_decay, w_in_na, na_rpb)
    if "nc" not in _NC_CACHE:
        _NC_CACHE["nc"] = build_program()
    res = run_bass_kernel_spmd(_NC_CACHE["nc"], in_maps, core_ids=list(range(8)))
    ys = [np.asarray(r["y"], dtype=np.float32) for r in res.results]
    y_prompt = np.concatenate([y.reshape(4, 2048, D) for y in ys[:4]], axis=0)
    y_sample = np.stack([y.reshape(8192, D) for y in ys[4:]], axis=0)
    return (y_prompt, y_sample)
```
